# Optimizing a Trainium2 kernel written in Bass

```python
import jax, jax.numpy as jnp
from jax import lax
import numpy as np

D_MODEL = 2048
BATCH = 4
SEQ = 2048
DEPTH = 1
DEC_BATCH = 128
DEC_SEQ = 1
PAST_LEN = 16384
PAGE_SIZE = 128

ML_HEADS = 4
ML_WIDTH = D_MODEL
ML_DV = ML_WIDTH // ML_HEADS
ML_DK = ML_DV // 2
ML_QK = ML_HEADS * ML_DK
SSM_WIDTH = D_MODEL
SSM_HEADDIM = 64
SSM_HEADS = SSM_WIDTH // SSM_HEADDIM
SSM_GROUPS = 2
SSM_STATE = 128
CONV_W = 4
XBC_DIM = SSM_WIDTH + 2 * SSM_GROUPS * SSM_STATE
D_FF = 5632
IN_DIM = 2 * ML_QK + ML_WIDTH + 2 * ML_HEADS + ML_WIDTH + SSM_WIDTH + XBC_DIM + SSM_HEADS + 2 * D_MODEL
CHUNK = 128
EPS = 1e-6
NEG_INIT = -1e30

kernel_name = "hybrid_mlstm_ssd_macaron_step"


def _rms(x):
    xf = x.astype(jnp.float32)
    return xf * lax.rsqrt(jnp.mean(xf * xf, axis=-1, keepdims=True) + EPS)


def rmsnorm(x, w):
    return (_rms(x) * w.astype(jnp.float32)).astype(x.dtype)


def swiglu(x, w_gate, w_up, w_down):
    return (jax.nn.silu(x @ w_gate) * (x @ w_up)) @ w_down


def _chunking(L):
    c = CHUNK if L % CHUNK == 0 else L
    return L // c, c


def causal_conv(xbc, conv_state, w, b):
    L = xbc.shape[1]
    xp = jnp.concatenate([conv_state.astype(xbc.dtype), xbc], axis=1)
    y = sum(xp[:, j:j + L] * w[j] for j in range(CONV_W)) + b
    return jax.nn.silu(y), xp[:, -(CONV_W - 1):]


def mlstm_chunked(q, k, v, li, lf, C0, n0, m0):
    Bsz, L, H, _ = q.shape
    DV = v.shape[-1]
    nc, c = _chunking(L)

    def to_chunks(t):
        t = t.reshape((Bsz, nc, c) + t.shape[2:])
        return jnp.moveaxis(jnp.moveaxis(t, 1, 0), 3, 2)

    causal = jnp.tril(jnp.ones((c, c), dtype=bool))

    def step(carry, inp):
        C, n, m = carry
        qc, kc, vc, lic, lfc = inp
        b = jnp.cumsum(lfc, axis=-1)
        dmat = jnp.where(causal, b[..., :, None] - b[..., None, :] + lic[..., None, :], -jnp.inf)
        inter = b + m[..., None]
        m_t = jnp.maximum(inter, dmat.max(-1))
        w_intra = jnp.exp(dmat - m_t[..., None])
        w_inter = jnp.exp(inter - m_t)
        s = jnp.einsum('bhtd,bhsd->bhts', qc, kc) * w_intra
        num = jnp.einsum('bhts,bhsv->bhtv', s, vc) + w_inter[..., None] * jnp.einsum('bhtd,bhdv->bhtv', qc, C)
        den = s.sum(-1) + w_inter * jnp.einsum('bhtd,bhd->bht', qc, n)
        h = num / jnp.maximum(jnp.abs(den), jnp.exp(-m_t))[..., None]
        m_new = m_t[..., -1]
        w_end = jnp.exp(b[..., -1:] - b + lic - m_new[..., None])
        decay = jnp.exp(b[..., -1] + m - m_new)
        kw = kc * w_end[..., None]
        C_new = decay[..., None, None] * C + jnp.einsum('bhsd,bhsv->bhdv', kw, vc)
        n_new = decay[..., None] * n + kw.sum(-2)
        return (C_new, n_new, m_new), h

    (C1, n1, m1), hs = lax.scan(step, (C0, n0, m0), (to_chunks(q), to_chunks(k), to_chunks(v), to_chunks(li), to_chunks(lf)))
    hs = jnp.swapaxes(jnp.moveaxis(hs, 0, 1), 2, 3).reshape(Bsz, L, H, DV)
    return hs, C1, n1, m1


def ssd_chunked(x, dt, A, Bm, Cm, S0):
    Bsz, L, H, P = x.shape
    G, N = Bm.shape[2], Bm.shape[3]
    R = H // G
    nc, c = _chunking(L)
    xg = x.reshape(Bsz, nc, c, G, R, P).transpose(1, 0, 3, 4, 2, 5)
    dtg = dt.reshape(Bsz, nc, c, G, R).transpose(1, 0, 3, 4, 2)
    ag = dtg * A.reshape(G, R)[None, None, :, :, None]
    Bg = Bm.reshape(Bsz, nc, c, G, N).transpose(1, 0, 3, 2, 4)
    Cg = Cm.reshape(Bsz, nc, c, G, N).transpose(1, 0, 3, 2, 4)
    causal = jnp.tril(jnp.ones((c, c), dtype=bool))

    def step(S, inp):
        xc, dtc, ac, Bc, Cc = inp
        b = jnp.cumsum(ac, axis=-1)
        decay = jnp.exp(jnp.where(causal, b[..., :, None] - b[..., None, :], -jnp.inf))
        cb = jnp.einsum('bgtn,bgsn->bgts', Cc, Bc)
        M = cb[:, :, None] * decay * dtc[..., None, :]
        y = jnp.einsum('bgrts,bgrsp->bgrtp', M, xc) + jnp.exp(b)[..., None] * jnp.einsum('bgtn,bgrpn->bgrtp', Cc, S)
        w_end = jnp.exp(b[..., -1:] - b) * dtc
        S_new = jnp.exp(b[..., -1])[..., None, None] * S + jnp.einsum('bgrs,bgrsp,bgsn->bgrpn', w_end, xc, Bc)
        return S_new, y

    S1, ys = lax.scan(step, S0.reshape(Bsz, G, R, P, N), (xg, dtg, ag, Bg, Cg))
    ys = ys.transpose(1, 0, 4, 2, 3, 5).reshape(Bsz, L, H, P)
    return ys, S1.reshape(Bsz, H, P, N)


def token_mixer(u, conv_st, C0, n0, m0, S0, w_in, ml_i_bias, ml_f_bias, ml_head_norm,
                conv_w, conv_b, dt_bias, A_log, D_skip, ssm_norm, w_out):
    Bsz, L, _ = u.shape
    f32 = jnp.float32
    proj = u @ w_in
    sizes = [ML_QK, ML_QK, ML_WIDTH, ML_HEADS, ML_HEADS, ML_WIDTH, SSM_WIDTH, XBC_DIM, SSM_HEADS, 2 * D_MODEL]
    idx = [int(s) for s in np.cumsum(sizes)[:-1]]
    q, k, v, ig, fg, og, z, xbc, dt_raw, gates = jnp.split(proj, idx, axis=-1)
    q = q.astype(f32).reshape(Bsz, L, ML_HEADS, ML_DK) * (ML_DK ** -0.5)
    k = k.astype(f32).reshape(Bsz, L, ML_HEADS, ML_DK)
    v = v.astype(f32).reshape(Bsz, L, ML_HEADS, ML_DV)
    li = (ig + ml_i_bias).astype(f32)
    lf = jax.nn.log_sigmoid((fg + ml_f_bias).astype(f32))
    h_ml, C1, n1, m1 = mlstm_chunked(q, k, v, li, lf, C0.astype(f32), n0.astype(f32), m0.astype(f32))
    h_ml = (_rms(h_ml) * ml_head_norm.astype(f32).reshape(ML_HEADS, ML_DV)).reshape(Bsz, L, ML_WIDTH)
    y_a = jax.nn.sigmoid(og.astype(f32)) * h_ml
    xbc_act, conv_new = causal_conv(xbc, conv_st, conv_w, conv_b)
    xs, Bm, Cm = jnp.split(xbc_act.astype(f32), [SSM_WIDTH, SSM_WIDTH + SSM_GROUPS * SSM_STATE], axis=-1)
    xs = xs.reshape(Bsz, L, SSM_HEADS, SSM_HEADDIM)
    Bm = Bm.reshape(Bsz, L, SSM_GROUPS, SSM_STATE)
    Cm = Cm.reshape(Bsz, L, SSM_GROUPS, SSM_STATE)
    dt = jax.nn.softplus((dt_raw + dt_bias).astype(f32))
    A = -jnp.exp(A_log.astype(f32))
    y_s, S1 = ssd_chunked(xs, dt, A, Bm, Cm, S0.astype(f32))
    y_s = (y_s + D_skip.astype(f32)[:, None] * xs).reshape(Bsz, L, SSM_WIDTH)
    y_s = y_s * jax.nn.silu(z.astype(f32))
    y_b = _rms(y_s.reshape(Bsz, L, SSM_GROUPS, SSM_WIDTH // SSM_GROUPS)).reshape(Bsz, L, SSM_WIDTH) * ssm_norm.astype(f32)
    g = jax.nn.sigmoid(gates.astype(f32))
    merged = g[..., :D_MODEL] * y_a + g[..., D_MODEL:] * y_b
    out = merged.astype(u.dtype) @ w_out
    return out, conv_new, C1, n1, m1, S1


def trunk_layer(x, conv_st, C0, n0, m0, S0,
                ffn1_norm, ffn1_w_gate, ffn1_w_up, ffn1_w_down, mix_norm, w_in, ml_i_bias, ml_f_bias,
                ml_head_norm, conv_w, conv_b, dt_bias, A_log, D_skip, ssm_norm, w_out,
                ffn2_norm, ffn2_w_gate, ffn2_w_up, ffn2_w_down):
    x = x + 0.5 * swiglu(rmsnorm(x, ffn1_norm), ffn1_w_gate, ffn1_w_up, ffn1_w_down)
    mix, conv_new, C1, n1, m1, S1 = token_mixer(rmsnorm(x, mix_norm), conv_st, C0, n0, m0, S0, w_in,
                                                ml_i_bias, ml_f_bias, ml_head_norm, conv_w, conv_b,
                                                dt_bias, A_log, D_skip, ssm_norm, w_out)
    x = x + mix
    x = x + 0.5 * swiglu(rmsnorm(x, ffn2_norm), ffn2_w_gate, ffn2_w_up, ffn2_w_down)
    return x, conv_new, C1, n1, m1, S1


def setup_inputs(seed: int = 0) -> dict:
    key = jax.random.key(seed)
    ks = jax.random.split(key, 32)
    nrm = jax.random.normal
    f32 = jnp.float32
    dt0 = jnp.exp(jax.random.uniform(ks[20], (DEPTH, SSM_HEADS)) * (np.log(0.1) - np.log(0.001)) + np.log(0.001))
    return {
        "x_prompt": nrm(ks[0], (BATCH, SEQ, D_MODEL), f32),
        "x_sample": nrm(ks[1], (DEC_BATCH, DEC_SEQ, D_MODEL), f32),
        "state_conv": nrm(ks[2], (DEPTH, DEC_BATCH, CONV_W - 1, XBC_DIM), f32),
        "state_mlstm_C": 0.5 * nrm(ks[3], (DEPTH, DEC_BATCH, ML_HEADS, ML_DK, ML_DV), f32),
        "state_mlstm_n": 0.5 * nrm(ks[4], (DEPTH, DEC_BATCH, ML_HEADS, ML_DK), f32),
        "state_mlstm_m": nrm(ks[5], (DEPTH, DEC_BATCH, ML_HEADS), f32),
        "state_ssm": 0.5 * nrm(ks[6], (DEPTH, DEC_BATCH, SSM_HEADS, SSM_HEADDIM, SSM_STATE), f32),
        "ffn1_norm": 1.0 + 0.02 * nrm(ks[7], (DEPTH, D_MODEL), f32),
        "ffn1_w_gate": nrm(ks[8], (DEPTH, D_MODEL, D_FF), f32) * D_MODEL ** -0.5,
        "ffn1_w_up": nrm(ks[9], (DEPTH, D_MODEL, D_FF), f32) * D_MODEL ** -0.5,
        "ffn1_w_down": nrm(ks[10], (DEPTH, D_FF, D_MODEL), f32) * D_FF ** -0.5,
        "mix_norm": 1.0 + 0.02 * nrm(ks[11], (DEPTH, D_MODEL), f32),
        "w_in": nrm(ks[12], (DEPTH, D_MODEL, IN_DIM), f32) * D_MODEL ** -0.5,
        "ml_i_bias": 0.1 * nrm(ks[13], (DEPTH, ML_HEADS), f32),
        "ml_f_bias": jax.random.uniform(ks[14], (DEPTH, ML_HEADS), f32, 3.0, 6.0),
        "ml_head_norm": 1.0 + 0.02 * nrm(ks[15], (DEPTH, ML_WIDTH), f32),
        "ssm_conv_w": 0.5 * nrm(ks[16], (DEPTH, CONV_W, XBC_DIM), f32),
        "ssm_conv_b": 0.02 * nrm(ks[17], (DEPTH, XBC_DIM), f32),
        "ssm_dt_bias": (dt0 + jnp.log(-jnp.expm1(-dt0))).astype(f32),
        "ssm_A_log": jnp.log(jax.random.uniform(ks[18], (DEPTH, SSM_HEADS), f32, 1.0, 16.0)),
        "ssm_D": 1.0 + 0.1 * nrm(ks[19], (DEPTH, SSM_HEADS), f32),
        "ssm_norm": 1.0 + 0.02 * nrm(ks[21], (DEPTH, SSM_WIDTH), f32),
        "w_out": nrm(ks[22], (DEPTH, D_MODEL, D_MODEL), f32) * D_MODEL ** -0.5,
        "ffn2_norm": 1.0 + 0.02 * nrm(ks[23], (DEPTH, D_MODEL), f32),
        "ffn2_w_gate": nrm(ks[24], (DEPTH, D_MODEL, D_FF), f32) * D_MODEL ** -0.5,
        "ffn2_w_up": nrm(ks[25], (DEPTH, D_MODEL, D_FF), f32) * D_MODEL ** -0.5,
        "ffn2_w_down": nrm(ks[26], (DEPTH, D_FF, D_MODEL), f32) * D_FF ** -0.5,
        "final_norm": 1.0 + 0.02 * nrm(ks[27], (D_MODEL,), f32),
    }


def reference(x_prompt, x_sample, state_conv, state_mlstm_C, state_mlstm_n, state_mlstm_m, state_ssm,
              ffn1_norm, ffn1_w_gate, ffn1_w_up, ffn1_w_down, mix_norm, w_in, ml_i_bias, ml_f_bias,
              ml_head_norm, ssm_conv_w, ssm_conv_b, ssm_dt_bias, ssm_A_log, ssm_D, ssm_norm, w_out,
              ffn2_norm, ffn2_w_gate, ffn2_w_up, ffn2_w_down, final_norm):
    f32 = jnp.float32
    Bp = x_prompt.shape[0]
    p_st = (jnp.zeros((Bp, CONV_W - 1, XBC_DIM), x_prompt.dtype),
            jnp.zeros((Bp, ML_HEADS, ML_DK, ML_DV), f32),
            jnp.zeros((Bp, ML_HEADS, ML_DK), f32),
            jnp.full((Bp, ML_HEADS), NEG_INIT, f32),
            jnp.zeros((Bp, SSM_HEADS, SSM_HEADDIM, SSM_STATE), f32))
    xp, xs = x_prompt, x_sample
    pc, pC, pn, pm, pS = [], [], [], [], []
    sc, sC, sn, sm, sS = [], [], [], [], []
    for l in range(DEPTH):
        w = (ffn1_norm[l], ffn1_w_gate[l], ffn1_w_up[l], ffn1_w_down[l], mix_norm[l], w_in[l],
             ml_i_bias[l], ml_f_bias[l], ml_head_norm[l], ssm_conv_w[l], ssm_conv_b[l], ssm_dt_bias[l],
             ssm_A_log[l], ssm_D[l], ssm_norm[l], w_out[l], ffn2_norm[l], ffn2_w_gate[l], ffn2_w_up[l],
             ffn2_w_down[l])
        xp, c1, C1, n1, m1, S1 = trunk_layer(xp, *p_st, *w)
        pc.append(c1); pC.append(C1); pn.append(n1); pm.append(m1); pS.append(S1)
        xs, c2, C2, n2, m2, S2 = trunk_layer(xs, state_conv[l], state_mlstm_C[l], state_mlstm_n[l],
                                             state_mlstm_m[l], state_ssm[l], *w)
        sc.append(c2); sC.append(C2); sn.append(n2); sm.append(m2); sS.append(S2)
    y_prompt = rmsnorm(xp, final_norm)
    y_sample = rmsnorm(xs, final_norm)
    return (y_prompt, y_sample,
            jnp.stack(pc), jnp.stack(pC), jnp.stack(pn), jnp.stack(pm), jnp.stack(pS),
            jnp.stack(sc), jnp.stack(sC), jnp.stack(sn), jnp.stack(sm), jnp.stack(sS))
```

```python
import numpy as np
from contextlib import ExitStack
import concourse.bass as bass
import concourse.mybir as mybir
from concourse.alu_op_type import AluOpType as ALU
from concourse.bass_utils import run_bass_kernel_spmd

F32 = mybir.dt.float32
BF16 = mybir.dt.bfloat16
AF = mybir.ActivationFunctionType
AX = mybir.AxisListType

D = 2048
DFF = 5632
NKC = D // 128
NFC = DFF // 128
T = 512
NS = 16
TW = T + NS
EPS = 1e-6
IN_DIM = 14888
STRICT_WAR = False


class Buf:
    __slots__ = ("name", "w", "r")

    def __init__(self, name):
        self.name = name
        self.w = None
        self.r = {}


class Kern:
    ENGS = ("pe", "act", "dve", "pool", "sp")

    def __init__(self, nc, stack):
        self.nc = nc
        self.stack = stack
        self.ops = {e: [] for e in self.ENGS}
        self.seq = {e: 0 for e in self.ENGS}
        self.seen = {e: {} for e in self.ENGS}
        self.esem = {e: stack.enter_context(nc.semaphore("s_" + e)) for e in self.ENGS}
        self.dsems = {}
        self.dcount = {}
        self.bufs = {}
        self.waited = set()
        self.lastreal = {e: 0 for e in self.ENGS}

    def buf(self, name):
        b = self.bufs.get(name)
        if b is None:
            b = self.bufs[name] = Buf(name)
        return b

    def dma_sem(self, name):
        if name not in self.dsems:
            self.dsems[name] = self.stack.enter_context(self.nc.semaphore("d_" + name))
            self.dcount[name] = 0
        return name

    def _deps(self, eng, reads, writes, own=None):
        deps = {}

        def add(k, s):
            if deps.get(k, 0) < s:
                deps[k] = s
        for b in reads:
            b = self.buf(b) if isinstance(b, str) else b
            if b.w is not None:
                add(*b.w)
        for b in writes:
            b = self.buf(b) if isinstance(b, str) else b
            if b.w is not None:
                add(*b.w)
            for k, s in b.r.items():
                if k != eng or STRICT_WAR:
                    add(k, s)
        waits = []
        seen = self.seen[eng]
        for k, s in deps.items():
            if (k == eng and eng in ("pe", "sp")) or k == own:
                continue
            if seen.get(k, 0) < s:
                seen[k] = s
                waits.append((k, s))
                if k in self.ENGS:
                    self.waited.add((k, s))
        return waits

    def _commit(self, key, seq, eng, reads, writes):
        for b in reads:
            b = self.buf(b) if isinstance(b, str) else b
            if b.r.get(key, 0) < seq:
                b.r[key] = seq
        for b in writes:
            b = self.buf(b) if isinstance(b, str) else b
            b.w = (key, seq)
            b.r = {}

    def op(self, eng, fn, reads=(), writes=()):
        waits = self._deps(eng, reads, writes)
        self.seq[eng] += 1
        seq = self.seq[eng]
        self.ops[eng].append((waits, fn, None, seq))
        self.lastreal[eng] = seq
        self._commit(eng, seq, eng, reads, writes)

    def dma(self, eng, fn, sem, reads=(), writes=()):
        self.dma_sem(sem)
        waits = self._deps(eng, reads, writes, own="D:" + sem)
        self.dcount[sem] += 16
        val = self.dcount[sem]
        self.seq[eng] += 1
        self.ops[eng].append((waits, fn, sem, self.seq[eng]))
        key = "D:" + sem
        self._commit(key, val, eng, reads, writes)

    def barrier(self, engs):
        last = {e: self.lastreal[e] for e in engs if self.lastreal[e] > 0 and e != "sp"}
        dl = {"D:" + s: v for s, v in self.dcount.items() if v > 0 and not s.startswith("w")}
        for e in engs:
            waits = []
            seen = self.seen[e]
            for k, s in list(last.items()) + list(dl.items()):
                if k == e:
                    continue
                if seen.get(k, 0) < s:
                    seen[k] = s
                    waits.append((k, s))
                    if k in self.ENGS:
                        self.waited.add((k, s))
            if waits:
                self.seq[e] += 1
                self.ops[e].append((waits, None, None, self.seq[e]))

    def final_wait_all_dma(self, eng="sp"):
        waits = [("D:" + s, v) for s, v in self.dcount.items() if v > 0]
        self.seq[eng] += 1
        self.ops[eng].append((waits, None, None, self.seq[eng]))

    def emit(self):
        nc = self.nc
        handles = {"pe": "tensor", "act": "scalar", "dve": "vector", "pool": "gpsimd", "sp": "sync"}
        val = {}
        for e in self.ENGS:
            c = 0
            for (_, _, dsem, seq) in self.ops[e]:
                if (e, seq) in self.waited:
                    c += 1
                    val[(e, seq)] = c
        ops, esem, dsems, waited = self.ops, self.esem, self.dsems, self.waited

        def run(e):
            def body(h):
                for (waits, fn, dsem, seq) in ops[e]:
                    for (k, s) in waits:
                        if k.startswith("D:"):
                            h.wait_ge(dsems[k[2:]], s)
                        else:
                            h.wait_ge(esem[k], val[(k, s)])
                    if fn is None:
                        continue
                    ins = fn(h)
                    if dsem is not None:
                        ins.then_inc(dsems[dsem], 16)
                    elif (e, seq) in waited:
                        ins.then_inc(esem[e], 1)
            return body

        with nc.Block() as block:
            block.sync(run("sp"))
            block.gpsimd(run("pool"))
            block.tensor(run("pe"))
            block.scalar(run("act"))
            block.vector(run("dve"))


class Prog:
    def __init__(self, cfg):
        self.cfg = cfg
        self.nc = bass.Bass("TRN2", target_bir_lowering=False)
        self.stack = ExitStack()
        self.K = Kern(self.nc, self.stack)
        self.dram = {}

    def din(self, name, shape, dtype=F32):
        t = self.nc.dram_tensor(name, list(shape), dtype, kind="ExternalInput")
        self.dram[name] = t
        return t.ap()

    def dout(self, name, shape, dtype=F32):
        t = self.nc.dram_tensor(name, list(shape), dtype, kind="ExternalOutput")
        self.dram[name] = t
        return t.ap()

    def sb(self, name, shape, dtype):
        return self.stack.enter_context(self.nc.sbuf_tensor(name, list(shape), dtype))

    def ps(self, name, shape, dtype=F32):
        return self.stack.enter_context(self.nc.psum_tensor(name, list(shape), dtype))


DK = 256
DV = 512
NH = 4
SH = 32
HP = 64
SN = 128
XBC = 2560
NEG = -1e30
ARENA = 20480


def win_plan():
    oq, ok_, ov, oig, ofg, oog, oz, ox = 0, 1024, 2048, 4096, 4100, 4104, 6152, 8200
    oB, oC, odt, ogA, ogB = 10248, 10504, 10760, 10792, 12840
    pl = []
    ar = np.arange
    for g in range(2):
        pl.append((f"BC{g}", np.concatenate([oB + g * 128 + ar(128), oC + g * 128 + ar(128)])))
    sm = np.full(256, -1, np.int64)
    sm[0:4] = oig + ar(4); sm[4:8] = ofg + ar(4); sm[8:40] = odt + ar(32)
    pl.append(("SM", sm))
    for h in range(NH):
        pl.append((f"Q{h}", oq + h * 256 + ar(256)))
        pl.append((f"K{h}", ok_ + h * 256 + ar(256)))
        for a in range(2):
            pl.append((f"V{h}{a}", ov + h * 512 + a * 256 + ar(256)))
        for a in range(2):
            pl.append((f"OG{h}{a}", oog + h * 512 + a * 256 + ar(256)))
        for a in range(2):
            pl.append((f"GA{h}{a}", ogA + h * 512 + a * 256 + ar(256)))
    for u in range(4):
        for a in range(2):
            pl.append((f"X{u}{a}", ox + u * 512 + a * 256 + ar(256)))
        for a in range(2):
            pl.append((f"Z{u}{a}", oz + u * 512 + a * 256 + ar(256)))
        for a in range(2):
            pl.append((f"GB{u}{a}", ogB + u * 512 + a * 256 + ar(256)))
    return pl


WIN_PLAN = win_plan()
WIN_IDX = {t: i for i, (t, _) in enumerate(WIN_PLAN)}


class Builder(Prog):
    def __init__(self, cfg):
        super().__init__(cfg)
        self.NT = cfg.get("NT", 4)
        self.full = cfg.get("full", [True] * self.NT)
        self.bank_rr = 0
        self.wq = []
        self.wq_issued = 0
        self.wq_next = 0
        self.NSLOT = cfg.get("nslot", 4)
        self.uid = 0

    def _op(self, eng, meth, reads, writes, **kw):
        self.K.op(eng, lambda e: getattr(e, meth)(**kw), reads, writes)

    def dve(self, meth, reads, writes, **kw):
        self._op("dve", meth, reads, writes, **kw)

    def act(self, reads, writes, **kw):
        self._op("act", "activation", reads, writes, **kw)

    def mm(self, reads, writes, **kw):
        self._op("pe", "matmul", reads, writes, **kw)

    def tr(self, reads, writes, **kw):
        self._op("pe", "transpose", reads, writes, **kw)

    def dma(self, sem, reads, writes, out, in_, eng="sp", nc_ok=False):
        nc = self.nc
        if nc_ok:
            def f(e):
                with nc.allow_non_contiguous_dma(reason="small strided transfer"):
                    return e.dma_start(out=out, in_=in_)
        else:
            def f(e):
                return e.dma_start(out=out, in_=in_)
        self.K.dma(eng, f, sem, reads=reads, writes=writes)

    def av(self, off, shape, dtype, rows=None):
        n = int(np.prod(shape[1:]))
        e = n * (2 if dtype == F32 else 1)
        a = off // 2
        assert off % 4 == 0 and a + e <= ARENA, (off, shape, a + e)
        ap = self.arena[:shape[0], a:a + e]
        if dtype == F32:
            ap = ap.bitcast(F32)
        if len(shape) == 3:
            ap = ap.rearrange("p (a b) -> p a b", b=shape[2])
        elif len(shape) == 4:
            ap = ap.rearrange("p (a b c) -> p a b c", b=shape[2], c=shape[3])
        return ap

    def bank(self):
        b = self.bank_rr
        self.bank_rr = (b + 1) % 7
        return b

    def declare(self):
        NT = self.NT
        self.xp = self.din("xp", [NT * T, D])
        self.xs = self.din("xs", [NS, D])
        self.vecs = self.din("vecs", [8, D])
        self.small = self.din("small", [8, 32])
        self.convw = self.din("convw", [5, XBC])
        self.wg = [self.din(f"wg{i}", [22, 128, 16, 256]) for i in (1, 2)]
        self.wu = [self.din(f"wu{i}", [22, 128, 16, 256]) for i in (1, 2)]
        self.wd = [self.din(f"wd{i}", [24, 128, 8, 512]) for i in (1, 2)]
        self.wo = self.din("wo", [8, 128, 16, 256])
        self.win = self.din("win", [len(WIN_PLAN), 128, 16, 256])
        self.st_conv = self.din("st_conv", [NS * 3, XBC])
        self.st_C = self.din("st_C", [NS, NH, DK, DV])
        self.st_n = self.din("st_n", [NS, NH * DK])
        self.st_m = self.din("st_m", [NS, NH])
        self.st_S = self.din("st_S", [NS, SH * HP, SN])
        self.yp = self.dout("yp", [sum(1 for f in self.full if f) * T, D])
        self.ys = self.dout("ys", [NS, D])
        self.o_pconv = self.dout("o_pconv", [3, XBC])
        self.o_pC = self.dout("o_pC", [NH, DK, DV])
        self.o_pn = self.dout("o_pn", [NH * DK])
        self.o_pm = self.dout("o_pm", [NH])
        self.o_pS = self.dout("o_pS", [SH * HP, SN])
        self.o_sconv = self.dout("o_sconv", [NS, 3, XBC])
        self.o_sC = self.dout("o_sC", [NS, NH, DK, DV])
        self.o_sn = self.dout("o_sn", [NS, NH * DK])
        self.o_sm = self.dout("o_sm", [NS, NH])
        self.o_sS = self.dout("o_sS", [NS, SH * HP, SN])

    def alloc(self):
        sb = self.sb
        self.xres = sb("xres", [128, 5, D], F32)
        self.xnT = sb("xnT", [128, NKC, TW], BF16)
        self.arena = sb("arena", [128, ARENA], BF16)
        self.hT = self.arena[:, 0:24 * TW].rearrange("p (f t) -> p f t", t=TW)
        self.xsb = self.arena[:, 0:4 * D].rearrange("p (c d) -> p c d", d=D)
        self.xsbs = self.arena[:, 4 * D:5 * D]
        self.junk = self.arena[:, 5 * D:6 * D]
        self.fnrow = self.arena[:, 6 * D:8 * D].bitcast(F32)
        self.ybuf = self.arena[:, 0:2 * D].bitcast(F32)
        self.ws = [sb(f"ws{i}", [128, 4096], BF16) for i in range(self.NSLOT)]
        self.ident = sb("ident", [128, 128], BF16)
        self.identf = sb("identf", [128, 128], F32)
        self.wcols = sb("wcols", [128, 8, NKC], F32)
        self.stat = sb("stat", [128, 64], F32)
        self.sg = [self.arena[:, 12672 + i * 2 * TW:12672 + (i + 1) * 2 * TW].bitcast(F32) for i in range(2)]
        self.pb = [self.ps(f"pb{i}", [128, 512], F32) for i in range(8)]
        self.alloc_mix()

    def wplan(self, ap, kc, cols, tag):
        self.wq.append((ap, kc, cols, tag))

    def _issue_w(self, i):
        ap, kc, cols, tag = self.wq[i]
        s = i % self.NSLOT
        dst = self.ws[s][:, 0:kc * cols].rearrange("p (k c) -> p k c", c=cols)
        step = 2048 // cols
        for k0 in range(0, kc, step):
            k1 = min(kc, k0 + step)
            self.dma(f"w{s}", [], [f"ws{s}"], out=dst[:, k0:k1, :], in_=ap[:, k0:k1, :], eng="pool")

    def wget(self, tag, hold=1):
        i = self.wq_next
        assert self.wq[i][3] == tag, (self.wq[i][3], tag)
        self.wq_next += 1
        while self.wq_issued < min(len(self.wq), i - (hold - 1) + self.NSLOT):
            self._issue_w(self.wq_issued)
            self.wq_issued += 1
        s = i % self.NSLOT
        _, kc, cols, _ = self.wq[i]
        return self.ws[s][:, 0:kc * cols].rearrange("p (k c) -> p k c", c=cols), f"ws{s}"

    def consts(self):
        K = self.K
        self._op("pool", "memset", [], ["identf"], ap=self.identf[:], constant=0.0)
        self._op("pool", "affine_select", ["identf"], ["identf"], out=self.identf[:], in_=self.identf[:], pattern=[[-1, 128]],
                 compare_op=ALU.not_equal, fill=1.0, base=0, channel_multiplier=1)
        self.dve("tensor_copy", ["identf"], ["ident"], out=self.ident[:], in_=self.identf[:])
        self.dma("ld_c0", [], ["wcols"], out=self.wcols[:], in_=self.vecs.rearrange("v (c p) -> p v c", p=128), nc_ok=True)
        self.consts_mix()

    def rstd_of(self, tc, rows, col0=0):
        ss = self.stat[:rows, col0 + tc:col0 + tc + 1]
        rs = self.stat[:rows, col0 + 8 + tc:col0 + 9 + tc]
        xin = self.xres[:rows, tc, :]
        self.dve("memset", [], [f"ss{tc}"], ap=ss, constant=0.0)
        self.act([f"xres{tc}", f"ss{tc}"], ["junk", f"ss{tc}"], out=self.junk[:rows, :], in_=xin, func=AF.Square, accum_out=ss)
        self.dve("tensor_scalar", [f"ss{tc}"], [f"rs{tc}"], out=rs, in0=ss, scalar1=1.0 / D, scalar2=EPS, op0=ALU.mult, op1=ALU.add)
        self.act([f"rs{tc}"], [f"rs{tc}"], out=rs, in_=rs, func=AF.Sqrt)
        self.dve("reciprocal", [f"rs{tc}"], [f"rs{tc}"], out=rs, in_=rs)
        return rs

    def rms_to_T(self, vec_idx, with_samples):
        chunks = [0, 1, 2, 3] + ([4] if with_samples else [])
        for tc in chunks:
            rows = 128 if tc < 4 else NS
            rs = self.rstd_of(tc, rows)
            dst = self.xsb[:, tc, :] if tc < 4 else self.xsbs[:rows, :]
            self.act([f"xres{tc}", f"rs{tc}"], [f"xsb{tc}"], out=dst, in_=self.xres[:rows, tc, :], func=AF.Copy, scale=rs)
        w = TW if with_samples else T
        for dc in range(NKC):
            b = self.bank()
            pt = self.pb[b][:].bitcast(BF16)
            for tc in chunks:
                rows = 128 if tc < 4 else NS
                src = self.xsb[:, tc, dc * 128:(dc + 1) * 128] if tc < 4 else self.xsbs[:rows, dc * 128:(dc + 1) * 128]
                self.tr([f"xsb{tc}", "ident"], [f"pb{b}"], out=pt[:, tc * 128:tc * 128 + rows], in_=src, identity=self.ident[:rows, :rows])
            self.dve("tensor_scalar", [f"pb{b}", "wcols"], ["xnT"], out=self.xnT[:, dc, 0:w], in0=pt[:, 0:w],
                     scalar1=self.wcols[:, vec_idx, dc:dc + 1], scalar2=None, op0=ALU.mult)

    HALVES = ((0, 12), (12, 22))

    def plan_ffn(self, li):
        for hf, (b0, b1) in enumerate(self.HALVES):
            for fb in range(b0, b1):
                self.wplan(self.wg[li][fb], 16, 256, f"g{li}_{fb}")
                self.wplan(self.wu[li][fb], 16, 256, f"u{li}_{fb}")
            nfc = (b1 - b0) * 2
            for db in range(4):
                for kb in range(3):
                    nk = min(8, nfc - kb * 8)
                    self.wplan(self.wd[li][hf * 12 + db * 3 + kb], nk, 512, f"d{li}_{hf}_{db}_{kb}")

    def ffn(self, li, with_samples):
        P7 = ["pb7"]
        chunks = [0, 1, 2, 3] + ([4] if with_samples else [])
        for hf, (b0, b1) in enumerate(self.HALVES):
            nfc = (b1 - b0) * 2
            for fb in range(b0, b1):
                wg, wgn = self.wget(f"g{li}_{fb}")
                wu, wun = self.wget(f"u{li}_{fb}", hold=2)
                for j in range(2):
                    fc = (fb - b0) * 2 + j
                    bg, bu = self.bank(), self.bank()
                    par = fc % 2
                    for (wt, wn, b, col, sn) in ((wg, wgn, bg, par * 32, f"pb7g{par}"), (wu, wun, bu, par * 32 + 16, f"pb7u{par}")):
                        for dc in range(NKC):
                            self.mm([wn, "xnT"], [f"pb{b}"], out=self.pb[b][:, 0:T], lhsT=wt[:, dc, j * 128:(j + 1) * 128],
                                    rhs=self.xnT[:, dc, 0:T], start=(dc == 0), stop=(dc == NKC - 1))
                        if with_samples:
                            for dc in range(NKC):
                                self.mm([wn, "xnT"], [sn], out=self.pb[7][:, col:col + NS], lhsT=wt[:, dc, j * 128:(j + 1) * 128],
                                        rhs=self.xnT[:, dc, T:TW], start=(dc == 0), stop=(dc == NKC - 1))
                    sgi = fc % 2
                    sg = self.sg[sgi]
                    self.act([f"pb{bg}"], [f"sg{sgi}"], out=sg[:, 0:T], in_=self.pb[bg][:, 0:T], func=AF.Silu)
                    self.dve("tensor_tensor", [f"sg{sgi}", f"pb{bu}"], [f"hT{fc}"], out=self.hT[:, fc, 0:T], in0=sg[:, 0:T],
                             in1=self.pb[bu][:, 0:T], op=ALU.mult)
                    if with_samples:
                        self.act([f"pb7g{par}"], [f"sg{sgi}s"], out=sg[:, T:TW], in_=self.pb[7][:, par * 32:par * 32 + NS], func=AF.Silu)
                        self.dve("tensor_tensor", [f"sg{sgi}s", f"pb7u{par}"], [f"hT{fc}s"], out=self.hT[:, fc, T:TW], in0=sg[:, T:TW],
                                 in1=self.pb[7][:, par * 32 + 16:par * 32 + 16 + NS], op=ALU.mult)
            P7n = ["pb7g0", "pb7g1", "pb7u0", "pb7u1"]
            for db in range(4):
                accb = {tc: (self.bank() if tc < 4 else 7) for tc in chunks}
                for kb in range(3):
                    nk = min(8, nfc - kb * 8)
                    wd, wdn = self.wget(f"d{li}_{hf}_{db}_{kb}")
                    for tc in chunks:
                        rows = 128 if tc < 4 else NS
                        b = accb[tc]
                        for kc in range(nk):
                            fc = kb * 8 + kc
                            first = (fc == 0)
                            last = (fc == nfc - 1)
                            self.mm([wdn, f"hT{fc}" if tc < 4 else f"hT{fc}s"], [f"pb{b}"] if tc < 4 else P7n,
                                    out=self.pb[b][:rows, :], lhsT=self.hT[:, fc, tc * 128:tc * 128 + rows], rhs=wd[:, kc, :],
                                    start=first, stop=last)
                for tc in chunks:
                    rows = 128 if tc < 4 else NS
                    b = accb[tc]
                    xs = self.xres[:rows, tc, db * 512:(db + 1) * 512]
                    self.dve("scalar_tensor_tensor", ([f"pb{b}"] if tc < 4 else P7n) + [f"xres{tc}"], [f"xres{tc}"],
                             out=xs, in0=self.pb[b][:rows, :], scalar=0.5, in1=xs, op0=ALU.mult, op1=ALU.add)

    def final_store(self, ti, with_samples):
        chunks = [0, 1, 2, 3] + ([4] if with_samples else [])
        self.dma("ld_fn", [], ["fnrow"], out=self.fnrow, in_=self.vecs[5:6, :].to_broadcast([128, D]))
        for tc in chunks:
            rows = 128 if tc < 4 else NS
            rs = self.rstd_of(tc, rows)
            yb = self.ybuf
            self.dve("scalar_tensor_tensor", [f"xres{tc}", f"rs{tc}", "fnrow"], ["ybuf"], out=yb[:rows, :], in0=self.xres[:rows, tc, :],
                     scalar=rs, in1=self.fnrow[:rows, :], op0=ALU.mult, op1=ALU.mult)
            dst = self.yp[ti * T + tc * 128: ti * T + (tc + 1) * 128, :] if tc < 4 else self.ys[:, :]
            self.dma("st_y", ["ybuf"], [], out=dst, in_=yb[:rows, :])

    def load_tile(self, ti, with_samples):
        for tc in range(4):
            src = self.xp[ti * T + tc * 128: ti * T + (tc + 1) * 128, :]
            self.dma(f"ld_x{tc}", [], [f"xres{tc}"], out=self.xres[:, tc, :], in_=src)
        if with_samples:
            self.dma("ld_x4", [], ["xres4"], out=self.xres[:NS, 4, :], in_=self.xs[:, :])

    def barrier(self):
        self.K.barrier(("pe", "act", "dve", "sp"))

    def build(self):
        self.declare()
        self.alloc()
        stages = self.cfg.get("stages", ("ffn1", "mix", "ffn2"))
        for ti in range(self.NT):
            full = self.full[ti]
            if "ffn1" in stages:
                self.plan_ffn(0)
            if "mix" in stages:
                self.plan_mix(ti, full)
            if "ffn2" in stages and full:
                self.plan_ffn(1)
        self.consts()
        npre = sum(1 for f in self.full if not f)
        for ti in range(self.NT):
            full = self.full[ti]
            ws = (ti == self.NT - 1)
            self.load_tile(ti, ws)
            if "ffn1" in stages:
                self.rms_to_T(0, ws)
                self.ffn(0, ws)
                self.barrier()
            if "mix" in stages:
                self.mixer(ti, ws, full)
                self.barrier()
            if full:
                if "ffn2" in stages:
                    self.rms_to_T(2, ws)
                    self.ffn(1, ws)
                    self.barrier()
                self.final_store(ti - npre, ws)
                self.barrier()
            if (not full) and ti == npre - 1:
                self.reset_state()
                self.barrier()
        self.K.final_wait_all_dma()
        self.K.emit()
        return self


class MixMixin:
    STATE_TAGS = ("BC", "SM", "K", "V", "X")

    def alloc_mix(self):
        sb = self.sb
        self.Cst = sb("Cst", [128, NH, 2, DV], F32)
        self.nst = sb("nst", [128, NH * 2], F32)
        self.mcol = sb("mcol", [128, NH], F32)
        self.Sst = sb("Sst", [128, 4, 512], F32)
        self.convhist = sb("convhist", [128, 20, 3], F32)
        self.mergedT = sb("mergedT", [128, NKC, TW], BF16)
        self.yzgA = sb("yzgA", [128, 4, 512], BF16)
        self.Umat = sb("Umat", [128, 128], F32)
        self.Lsm = sb("Lsm", [128, 128], F32)
        self.E127 = sb("E127", [128, 128], F32)
        self.CB = sb("CB", [128, 128], F32)
        self.onesf = sb("onesf", [128, 128], F32)
        self.smallb = sb("smallb", [128, 8, 32], F32)
        self.Arow = sb("Arow", [128, 32], F32)
        self.convwc = sb("convwc", [128, 20, 5], F32)
        self.tsm = sb("tsm", [128, 4, 256], F32)
        self.ssq = sb("ssq", [128, 16], F32)
        self.BCT = sb("BCT", [128, 2, 2, T], BF16)
        self.Btok = sb("Btok", [128, 2, 4, 128], BF16)
        self.sarea = sb("sarea", [128, 2048], F32)
        self.alloc_smp()

    def consts_mix(self):
        ps_ = self._op
        for (t, nm, pat, cmp_, fill, base, cm, init) in (
                (self.Umat, "Umat", [[1, 128]], ALU.is_ge, 0.0, 0, -1, 1.0),
                (self.Lsm, "Lsm", [[-1, 128]], ALU.is_gt, 0.0, 0, 1, 1.0),
                (self.E127, "E127", [[0, 128]], ALU.is_equal, 0.0, -127, 1, 1.0),
                (self.CB, "CB", [[-1, 128]], ALU.is_ge, NEG, 0, 1, 0.0),
        ):
            ps_("pool", "memset", [], [nm], ap=t[:], constant=init)
            ps_("pool", "affine_select", [nm], [nm], out=t[:], in_=t[:], pattern=pat, compare_op=cmp_, fill=fill, base=base,
                channel_multiplier=cm)
        ps_("pool", "memset", [], ["onesf"], ap=self.onesf[:], constant=1.0)
        self.dma("ld_c1", [], ["smallb"], out=self.smallb[:].rearrange("p a b -> p (a b)"),
                 in_=self.small.rearrange("a b -> (a b)").unsqueeze(0).to_broadcast([128, 256]))
        for j in range(5):
            self.dma(f"ld_c{2 + j}", [], ["convwc"], out=self.convwc[:, :, j], in_=self.convw[j].rearrange("(c p) -> p c", p=128), nc_ok=True)
        self.act(["smallb"], ["Arow"], out=self.Arow[:], in_=self.smallb[:, 3, :], func=AF.Exp)
        self.dve("tensor_scalar", ["Arow"], ["Arow"], out=self.Arow[:], in0=self.Arow[:], scalar1=-1.0, scalar2=None, op0=ALU.mult)
        self.dve("memset", [], ["Cst"], ap=self.Cst[:].rearrange("p a b c -> p (a b c)"), constant=0.0)
        self.dve("memset", [], ["nst"], ap=self.nst[:], constant=0.0)
        self.dve("memset", [], ["mcol"], ap=self.mcol[:], constant=NEG)
        self.dve("memset", [], ["Sst"], ap=self.Sst[:].rearrange("p a b -> p (a b)"), constant=0.0)
        self.dve("memset", [], ["convhist"], ap=self.convhist[:].rearrange("p a b -> p (a b)"), constant=0.0)

    def reset_state(self):
        fl = self.smallb[:, 5, 0:1]
        for (t, nm, ap) in ((self.Cst, "Cst", self.Cst[:].rearrange("p a b c -> p (a b c)")), (self.nst, "nst", self.nst[:]),
                            (self.Sst, "Sst", self.Sst[:].rearrange("p a b -> p (a b)")),
                            (self.convhist, "convhist", self.convhist[:].rearrange("p a b -> p (a b)"))):
            self.dve("tensor_scalar", [nm, "smallb"], [nm], out=ap, in0=ap, scalar1=fl, scalar2=None, op0=ALU.mult)
        self.dve("tensor_scalar", ["mcol", "smallb"], ["mcol"], out=self.mcol[:], in0=self.mcol[:], scalar1=fl, scalar2=self.smallb[:, 5, 1:2],
                 op0=ALU.mult, op1=ALU.add)

    def plan_mix(self, ti, full):
        for i, (tag, _) in enumerate(WIN_PLAN):
            if full or tag.startswith(self.STATE_TAGS):
                self.wplan(self.win[i], 16, 256, f"in_{tag}")
        if full:
            for b in range(8):
                self.wplan(self.wo[b], 16, 256, f"wo_{b}")

    def proj_feat(self, wt, wn, oc, b, cols=T, c0=0, bankcols=None):
        bc = bankcols if bankcols is not None else (0, cols)
        for dc in range(NKC):
            self.mm([wn, "xnT"], [f"pb{b}"], out=self.pb[b][:, bc[0]:bc[0] + cols], lhsT=wt[:, dc, oc * 128:(oc + 1) * 128],
                    rhs=self.xnT[:, dc, c0:c0 + cols], start=(dc == 0), stop=(dc == NKC - 1))

    def proj_tok(self, wt, wn, tc, b, ncols=256, wc0=0, rows=128, bc0=0):
        t0 = tc * 128
        for dc in range(NKC):
            self.mm([wn, "xnT"], [f"pb{b}"], out=self.pb[b][:rows, bc0:bc0 + ncols], lhsT=self.xnT[:, dc, t0:t0 + rows],
                    rhs=wt[:, dc, wc0:wc0 + ncols], start=(dc == 0), stop=(dc == NKC - 1))

    def conv_silu(self, xpre, nm_pre, chunk, out_ap, nm_out, width=T):
        acc = self.av(self.OFF_CONVACC, [128, T], F32)
        w = self.convwc
        self.dve("tensor_scalar", [nm_pre, "convwc"], ["convacc"], out=acc[:, 0:width], in0=xpre[:, 0:width], scalar1=w[:, chunk, 0:1],
                 scalar2=None, op0=ALU.mult)
        for j in (1, 2, 3):
            self.dve("scalar_tensor_tensor", [nm_pre, "convwc", "convacc"], ["convacc"], out=acc[:, 0:width], in0=xpre[:, j:j + width],
                     scalar=w[:, chunk, j:j + 1], in1=acc[:, 0:width], op0=ALU.mult, op1=ALU.add)
        self.act(["convacc", "convwc"], [nm_out], out=out_ap, in_=acc[:, 0:width], func=AF.Silu, bias=w[:, chunk, 4:5], scale=1.0)

    OFF_CONVACC = 0
    OFF_XPRE = 2048
    OFF_BUFY = 10304
    OFF_XDT = 18496
    OFF_LR = 22592
    OFF_M = 26688
    OFF_YT = 28736
    OFF_YI = 30784
    OFF_MISC = 32832

    def mixer(self, ti, ws, full):
        if ws and full:
            self.smp_init()
        self.rms_to_T(1, ws)
        self.mix_bc(ti, ws, full)
        self.mix_small(ti, ws, full)
        self.barrier()
        self.mix_mlstm_all(ti, ws, full)
        self.barrier()
        for u in range(4):
            self.mix_ssd(ti, u, ws, full)
        if full:
            self.mix_out(ti, ws)
        if ti == self.NT - 1:
            self.store_pstates()

    def mix_bc(self, ti, ws, full):
        for g in range(2):
            wt, wn = self.wget("in_BC%d" % g)
            for oc in range(2):
                chunk = 16 + g + 2 * oc
                b = self.bank()
                self.proj_feat(wt, wn, oc, b)
                xpre = self.av(self.OFF_XPRE, [128, T + 3], F32)
                self.dve("tensor_copy", ["convhist"], ["xpre"], out=xpre[:, 0:3], in_=self.convhist[:, chunk, :])
                self.act([f"pb{b}"], ["xpre"], out=xpre[:, 3:T + 3], in_=self.pb[b][:, 0:T], func=AF.Copy)
                self.dve("tensor_copy", ["xpre"], ["convhist"], out=self.convhist[:, chunk, :], in_=xpre[:, T:T + 3])
                if oc == 0 or full:
                    self.conv_silu(xpre, "xpre", chunk, self.BCT[:, g, oc, :], f"BCT{g}{oc}")
            if ws and full:
                self.smp_bc(g, wt, wn)
            b = self.bank()
            pt = self.pb[b][:].bitcast(BF16)
            for tc in range(4):
                self.tr([f"BCT{g}0", "ident"], [f"pb{b}"], out=pt[:, tc * 128:(tc + 1) * 128], in_=self.BCT[:, g, 0, tc * 128:(tc + 1) * 128],
                        identity=self.ident[:])
            self.dve("tensor_copy", [f"pb{b}"], [f"Btok{g}"], out=self.Btok[:, g, :, :].rearrange("p a b -> p (a b)"), in_=pt[:, 0:T])

    def mix_small(self, ti, ws, full):
        wt, wn = self.wget("in_SM")
        sm = self.tsm
        for tc in range(4):
            b = self.bank()
            self.proj_tok(wt, wn, tc, b, ncols=40)
            self.dve("tensor_copy", [f"pb{b}"], ["tsm"], out=sm[:, tc, 0:40], in_=self.pb[b][:, 0:40])
        if ws and full:
            self.smp_small(wt, wn)
        R, Wt = ["tsm", "smallb", "Arow"], ["tsm"]
        bi = self.smallb[:, 0, 0:4].unsqueeze(1).to_broadcast([128, 4, 4])
        bf = self.smallb[:, 1, 0:4].unsqueeze(1).to_broadcast([128, 4, 4])
        bdt = self.smallb[:, 2, :].unsqueeze(1).to_broadcast([128, 4, 32])
        Ab = self.Arow[:].unsqueeze(1).to_broadcast([128, 4, 32])
        self.dve("tensor_tensor", R, Wt, out=sm[:, :, 40:44], in0=sm[:, :, 0:4], in1=bi, op=ALU.add)
        z = sm[:, :, 184:188]
        self.dve("tensor_tensor", R, Wt, out=z, in0=sm[:, :, 4:8], in1=bf, op=ALU.add)
        self.softplus_neg(z, sm[:, :, 44:48], sm[:, :, 188:192], neg=True)
        z2 = sm[:, :, 188:220]
        self.dve("tensor_tensor", R, Wt, out=z2, in0=sm[:, :, 8:40], in1=bdt, op=ALU.add)
        self.softplus_neg(z2, sm[:, :, 56:88], sm[:, :, 220:252], neg=False)
        self.dve("tensor_tensor", R, Wt, out=sm[:, :, 88:120], in0=sm[:, :, 56:88], in1=Ab, op=ALU.mult)
        for tc in range(4):
            b = self.bank()
            self.mm(["tsm", "Umat"], [f"pb{b}"], out=self.pb[b][:, 0:4], lhsT=self.Umat[:], rhs=sm[:, tc, 44:48], start=True, stop=True)
            self.mm(["tsm", "Umat"], [f"pb{b}"], out=self.pb[b][:, 32:64], lhsT=self.Umat[:], rhs=sm[:, tc, 88:120], start=True, stop=True)
            self.dve("tensor_copy", [f"pb{b}"], ["tsm"], out=sm[:, tc, 48:52], in_=self.pb[b][:, 0:4])
            self.dve("tensor_copy", [f"pb{b}"], ["tsm"], out=sm[:, tc, 120:152], in_=self.pb[b][:, 32:64])
            b2 = self.bank()
            self.mm(["tsm", "E127"], [f"pb{b2}"], out=self.pb[b2][:, 0:32], lhsT=self.E127[:], rhs=sm[:, tc, 120:152], start=True, stop=True)
            self.dve("tensor_copy", [f"pb{b2}"], ["tsm"], out=sm[:, tc, 152:184], in_=self.pb[b2][:, 0:32])
        self.dve("tensor_tensor", ["tsm"], ["tsm"], out=sm[:, :, 52:56], in0=sm[:, :, 40:44], in1=sm[:, :, 48:52], op=ALU.subtract)
        self.act(["tsm"], ["tsm"], out=sm[:, :, 220:252], in_=sm[:, :, 120:152], func=AF.Exp)

    def softplus_neg(self, z, out, tmp, neg):
        R, Wt = ["tsm"], ["tsm"]
        self.dve("scalar_tensor_tensor", R, Wt, out=tmp, in0=z, scalar=-1.0, in1=z, op0=ALU.mult, op1=ALU.max)
        self.act(R, Wt, out=tmp, in_=tmp, func=AF.Exp, scale=-1.0)
        self.act(R, Wt, out=tmp, in_=tmp, func=AF.Ln, bias=self.onesf[:, 0:1], scale=1.0)
        if neg:
            self.dve("scalar_tensor_tensor", R, Wt, out=out, in0=z, scalar=0.0, in1=tmp, op0=ALU.min, op1=ALU.subtract)
        else:
            self.dve("scalar_tensor_tensor", R, Wt, out=out, in0=z, scalar=0.0, in1=tmp, op0=ALU.max, op1=ALU.add)

    ML_SET = 15360
    ML_CH = 30720

    def ml_views(self, p):
        av, o = self.av, p * self.ML_SET
        v = {"p": p}
        v["qT"] = av(o, [128, 2, T], BF16)
        v["kT"] = av(o + 2048, [128, 2, T], BF16)
        v["ktok"] = av(o + 4096, [128, 4, DK], BF16)
        v["vtok"] = av(o + 6144, [128, 4, DV], BF16)
        v["gateA"] = av(o + 10240, [128, 4, DV], BF16)
        v["sgt"] = av(o + 14336, [128, 256], F32)
        c0 = self.ML_CH
        v["Cbf"] = av(c0, [128, 2, DV], BF16)
        v["nbf"] = av(c0 + 2048, [128, 2], BF16)
        v["diagA"] = av(c0 + 2112, [128, 128], F32)
        v["Rm"] = av(c0 + 2624, [128, 128], F32)
        v["wI"] = av(c0 + 3136, [128, 128], F32)
        v["sw"] = av(c0 + 3648, [128, 128], BF16)
        v["swT"] = av(c0 + 3904, [128, 128], BF16)
        v["vw"] = av(c0 + 4160, [128, DV], BF16)
        v["hA"] = av(c0 + 5184, [128, DV], F32)
        v["yab"] = av(c0 + 7232, [128, DV], BF16)
        v["c"] = av(c0 + 8256, [128, 32], F32)
        v["wendbf"] = av(c0 + 8384, [128, 2], BF16)
        return v

    def ml_proj_gen(self, h, ws, full, p):
        v = self.ml_views(p)
        qT, kT, ktok, vtok, gateA, sgt = v["qT"], v["kT"], v["ktok"], v["vtok"], v["gateA"], v["sgt"]
        N = lambda s_: f"ml_{s_}{p}"
        if full:
            wt, wn = self.wget(f"in_Q{h}")
            for oc in range(2):
                b = self.bank()
                self.proj_feat(wt, wn, oc, b)
                self.act([f"pb{b}"], [N("qT")], out=qT[:, oc, :], in_=self.pb[b][:, 0:T], func=AF.Copy, scale=DK ** -0.5)
                yield
            if ws:
                self.smp_proj_tok(wt, wn, "qS", scale=DK ** -0.5)
        wt, wn = self.wget(f"in_K{h}")
        if full:
            for oc in range(2):
                b = self.bank()
                self.proj_feat(wt, wn, oc, b)
                self.act([f"pb{b}"], [N("kT")], out=kT[:, oc, :], in_=self.pb[b][:, 0:T], func=AF.Copy)
                yield
        for tc in range(4):
            b = self.bank()
            self.proj_tok(wt, wn, tc, b)
            self.dve("tensor_copy", [f"pb{b}"], [N("ktok")], out=ktok[:, tc, :], in_=self.pb[b][:, 0:256])
            yield
        if ws and full:
            self.smp_proj_tok(wt, wn, "kS")
        for a in range(2):
            wt, wn = self.wget(f"in_V{h}{a}")
            for tc in range(4):
                b = self.bank()
                self.proj_tok(wt, wn, tc, b)
                self.act([f"pb{b}"], [N("vtok")], out=vtok[:, tc, a * 256:(a + 1) * 256], in_=self.pb[b][:, 0:256], func=AF.Copy)
                yield
            if ws and full:
                self.smp_proj_tok(wt, wn, "vS", c0=a * 256)
        if full:
            for a in range(2):
                wt, wn = self.wget(f"in_OG{h}{a}")
                for tc in range(4):
                    b = self.bank()
                    self.proj_tok(wt, wn, tc, b)
                    self.act([f"pb{b}"], [N("gate")], out=gateA[:, tc, a * 256:(a + 1) * 256], in_=self.pb[b][:, 0:256], func=AF.Sigmoid)
                    yield
                if ws:
                    self.smp_proj_tok(wt, wn, "gS", c0=a * 256, func=AF.Sigmoid)
            for a in range(2):
                wt, wn = self.wget(f"in_GA{h}{a}")
                for tc in range(4):
                    b = self.bank()
                    self.proj_tok(wt, wn, tc, b)
                    self.act([f"pb{b}"], [N("sgt")], out=sgt[:], in_=self.pb[b][:, 0:256], func=AF.Sigmoid)
                    self.dve("tensor_tensor", [N("sgt"), N("gate")], [N("gate")], out=gateA[:, tc, a * 256:(a + 1) * 256],
                             in0=gateA[:, tc, a * 256:(a + 1) * 256], in1=sgt[:], op=ALU.mult)
                    yield
                if ws:
                    self.smp_proj_tok(wt, wn, "gS", c0=a * 256, func=AF.Sigmoid, mul=True)

    def ml_chunk_gen(self, h, ws, full, p):
        v = self.ml_views(p)
        qT, kT, ktok, vtok, gateA = v["qT"], v["kT"], v["ktok"], v["vtok"], v["gateA"]
        Cbf, nbf, diagA, Rm, wI, sw, swT, vw, hA, yab, c, wendbf = (v[k] for k in
                                                                    ("Cbf", "nbf", "diagA", "Rm", "wI", "sw", "swT", "vw", "hA", "yab", "c", "wendbf"))
        N = lambda s_: f"ml_{s_}{p}"
        sm = self.tsm
        if full:
            self.act(["Cst"], ["ml_Cbf"], out=Cbf[:].rearrange("p a b -> p (a b)"), in_=self.Cst[:, h, :, :].rearrange("p a b -> p (a b)"), func=AF.Copy)
            self.dve("tensor_copy", ["nst"], ["ml_nbf"], out=nbf[:], in_=self.nst[:, 2 * h:2 * h + 2])
        mprev = self.mcol[:, h:h + 1]
        C = ["ml_c"]
        for tc in range(4):
            ts_ = slice(tc * 128, (tc + 1) * 128)
            a_ = sm[:, tc, 52 + h:53 + h]
            bcol = sm[:, tc, 48 + h:49 + h]
            self.dve("tensor_scalar", ["identf", "tsm"], ["ml_diag"], out=diagA[:], in0=self.identf[:], scalar1=a_, scalar2=None, op0=ALU.mult)
            b = self.bank()
            self.mm(["onesf", "ml_diag"], [f"pb{b}"], out=self.pb[b][:, 0:128], lhsT=self.onesf[:], rhs=diagA[:], start=True, stop=True)
            self.dve("tensor_tensor", [f"pb{b}", "CB"], ["ml_Rm"], out=Rm[:], in0=self.pb[b][:, 0:128], in1=self.CB[:], op=ALU.add)
            self.dve("tensor_reduce", ["ml_Rm"], C, out=c[:, 0:1], in_=Rm[:], axis=AX.X, op=ALU.max)
            self.dve("tensor_tensor", C + ["mcol"], C, out=c[:, 1:2], in0=c[:, 0:1], in1=mprev, op=ALU.max)
            self.dve("tensor_scalar", C, C, out=c[:, 2:3], in0=c[:, 1:2], scalar1=-1.0, scalar2=None, op0=ALU.mult)
            yield
            if full:
                self.act(["ml_Rm"] + C, ["ml_wI"], out=wI[:], in_=Rm[:], func=AF.Exp, bias=c[:, 2:3], scale=1.0)
                b = self.bank()
                for kc in range(2):
                    self.mm([N("qT"), N("kT")], [f"pb{b}"], out=self.pb[b][:, 0:128], lhsT=qT[:, kc, ts_], rhs=kT[:, kc, ts_],
                            start=(kc == 0), stop=(kc == 1))
                self.dve("memset", [], C, ap=c[:, 3:4], constant=0.0)
                self.dve("scalar_tensor_tensor", [f"pb{b}", "ml_wI"] + C, ["ml_sw"] + C, out=sw[:], in0=self.pb[b][:, 0:128], scalar=1.0,
                         in1=wI[:], op0=ALU.mult, op1=ALU.mult, accum_out=c[:, 3:4])
                yield
                b = self.bank()
                pt = self.pb[b][:].bitcast(BF16)
                self.tr(["ml_sw", "ident"], [f"pb{b}"], out=pt[:, 0:128], in_=sw[:], identity=self.ident[:])
                self.dve("tensor_copy", [f"pb{b}"], ["ml_swT"], out=swT[:], in_=pt[:, 0:128])
                bA, bB, bC = self.bank(), self.bank(), self.bank()
                for kc in range(2):
                    self.mm([N("qT"), "ml_Cbf"], [f"pb{bB}"], out=self.pb[bB][:, :], lhsT=qT[:, kc, ts_], rhs=Cbf[:, kc, :],
                            start=(kc == 0), stop=(kc == 1))
                for kc in range(2):
                    self.mm([N("qT"), "ml_nbf"], [f"pb{bC}"], out=self.pb[bC][:, 0:1], lhsT=qT[:, kc, ts_], rhs=nbf[:, kc:kc + 1],
                            start=(kc == 0), stop=(kc == 1))
                yield
                self.mm(["ml_swT", N("vtok")], [f"pb{bA}"], out=self.pb[bA][:, :], lhsT=swT[:], rhs=vtok[:, tc, :], start=True, stop=True)
                self.act(C + ["mcol"], C, out=c[:, 4:5], in_=mprev, func=AF.Exp, bias=c[:, 2:3], scale=1.0)
                self.dve("scalar_tensor_tensor", [f"pb{bC}"] + C, C, out=c[:, 5:6], in0=self.pb[bC][:, 0:1], scalar=c[:, 4:5], in1=c[:, 3:4],
                         op0=ALU.mult, op1=ALU.add)
                self.dve("tensor_tensor", C + ["tsm"], C, out=c[:, 6:7], in0=bcol, in1=c[:, 1:2], op=ALU.add)
                self.act(C, C, out=c[:, 7:8], in_=c[:, 6:7], func=AF.Exp, scale=-1.0)
                self.dve("scalar_tensor_tensor", C, C, out=c[:, 8:9], in0=c[:, 5:6], scalar=-1.0, in1=c[:, 5:6], op0=ALU.mult, op1=ALU.max)
                self.dve("tensor_tensor", C, C, out=c[:, 8:9], in0=c[:, 8:9], in1=c[:, 7:8], op=ALU.max)
                self.dve("reciprocal", C, C, out=c[:, 9:10], in_=c[:, 8:9])
                self.act([f"pb{bA}"], ["ml_hA"], out=hA[:], in_=self.pb[bA][:, :], func=AF.Copy)
                self.dve("scalar_tensor_tensor", [f"pb{bB}", "ml_hA"] + C, ["ml_hA"], out=hA[:], in0=self.pb[bB][:, :], scalar=c[:, 4:5],
                         in1=hA[:], op0=ALU.mult, op1=ALU.add)
                yield
                self.dve("memset", [], C, ap=c[:, 10:11], constant=0.0)
                self.act(["ml_hA"] + C, ["ml_yab"] + C, out=yab[:], in_=hA[:], func=AF.Square, scale=c[:, 9:10], accum_out=c[:, 10:11])
                self.dve("tensor_scalar", C, C, out=c[:, 11:12], in0=c[:, 10:11], scalar1=1.0 / DV, scalar2=EPS, op0=ALU.mult, op1=ALU.add)
                self.act(C, C, out=c[:, 11:12], in_=c[:, 11:12], func=AF.Sqrt)
                self.dve("reciprocal", C, C, out=c[:, 11:12], in_=c[:, 11:12])
                self.dve("tensor_tensor", C, C, out=c[:, 12:13], in0=c[:, 11:12], in1=c[:, 9:10], op=ALU.mult)
                self.dve("scalar_tensor_tensor", ["ml_hA", N("gate")] + C, ["ml_yab"], out=yab[:], in0=hA[:], scalar=c[:, 12:13],
                         in1=gateA[:, tc, :], op0=ALU.mult, op1=ALU.mult)
                yield
                b = self.bank()
                pt = self.pb[b][:].bitcast(BF16)
                for j in range(4):
                    self.tr(["ml_yab", "ident"], [f"pb{b}"], out=pt[:, j * 128:(j + 1) * 128], in_=yab[:, j * 128:(j + 1) * 128],
                            identity=self.ident[:])
                for j in range(4):
                    self.dve("tensor_scalar", [f"pb{b}", "wcols"], ["mergedT"], out=self.mergedT[:, h * 4 + j, ts_], in0=pt[:, j * 128:(j + 1) * 128],
                             scalar1=self.wcols[:, 3, h * 4 + j:h * 4 + j + 1], scalar2=None, op0=ALU.mult)
            b = self.bank()
            self.mm(["E127"] + C, [f"pb{b}"], out=self.pb[b][:, 0:2], lhsT=self.E127[:], rhs=c[:, 1:3], start=True, stop=True)
            self.mm(["E127", "tsm"], [f"pb{b}"], out=self.pb[b][:, 2:4], lhsT=self.E127[:], rhs=sm[:, tc, 48 + h:50 + h], start=True, stop=True)
            self.dve("tensor_copy", [f"pb{b}"], C, out=c[:, 14:18], in_=self.pb[b][:, 0:4])
            self.act(C + ["tsm"], C, out=c[:, 17:18], in_=a_, func=AF.Exp, bias=c[:, 15:16], scale=1.0)
            self.act(C + ["mcol"], C, out=c[:, 18:19], in_=mprev, func=AF.Exp, bias=c[:, 15:16], scale=1.0)
            self.dve("tensor_scalar", [N("vtok")] + C, ["ml_vw"], out=vw[:], in0=vtok[:, tc, :], scalar1=c[:, 17:18], scalar2=None, op0=ALU.mult)
            self.dve("tensor_copy", C, ["ml_wendbf"], out=wendbf[:, 0:1], in_=c[:, 17:18])
            yield
            for kc in range(2):
                b = self.bank()
                self.mm([N("ktok"), "ml_vw"], [f"pb{b}"], out=self.pb[b][:, :], lhsT=ktok[:, tc, kc * 128:(kc + 1) * 128], rhs=vw[:],
                        start=True, stop=True)
                self.dve("scalar_tensor_tensor", [f"pb{b}", "Cst"] + C, ["Cst"], out=self.Cst[:, h, kc, :], in0=self.Cst[:, h, kc, :],
                         scalar=c[:, 18:19], in1=self.pb[b][:, :], op0=ALU.mult, op1=ALU.add)
            b = self.bank()
            for kc in range(2):
                self.mm([N("ktok"), "ml_wendbf"], [f"pb{b}"], out=self.pb[b][:, 8 * kc:8 * kc + 1], lhsT=ktok[:, tc, kc * 128:(kc + 1) * 128],
                        rhs=wendbf[:, 0:1], start=True, stop=True)
            for kc in range(2):
                self.dve("scalar_tensor_tensor", [f"pb{b}", "nst"] + C, ["nst"], out=self.nst[:, 2 * h + kc:2 * h + kc + 1],
                         in0=self.nst[:, 2 * h + kc:2 * h + kc + 1], scalar=c[:, 18:19], in1=self.pb[b][:, 8 * kc:8 * kc + 1],
                         op0=ALU.mult, op1=ALU.add)
            self.dve("tensor_tensor", C, ["mcol"], out=self.mcol[:, h:h + 1], in0=c[:, 14:15], in1=c[:, 16:17], op=ALU.add)
            if full and tc < 3:
                self.act(["Cst"], ["ml_Cbf"], out=Cbf[:].rearrange("p a b -> p (a b)"), in_=self.Cst[:, h, :, :].rearrange("p a b -> p (a b)"),
                         func=AF.Copy)
                self.dve("tensor_copy", ["nst"], ["ml_nbf"], out=nbf[:], in_=self.nst[:, 2 * h:2 * h + 2])
            yield

    @staticmethod
    def interleave(main, filler, ratio=2):
        fdone = filler is None
        for _ in main:
            for _k in range(ratio):
                if not fdone:
                    try:
                        next(filler)
                    except StopIteration:
                        fdone = True
        if not fdone:
            for _ in filler:
                pass

    def mix_mlstm_all(self, ti, ws, full):
        if ws and full:
            for h in range(NH):
                for _ in self.ml_proj_gen(h, ws, full, 0):
                    pass
                self.interleave(self.ml_chunk_gen(h, ws, full, 0), self.smp_mlstm(h), ratio=1)
            return
        for _ in self.ml_proj_gen(0, ws, full, 0):
            pass
        for h in range(NH):
            nxt = self.ml_proj_gen(h + 1, ws, full, (h + 1) % 2) if h + 1 < NH else None
            self.interleave(self.ml_chunk_gen(h, ws, full, h % 2), nxt, ratio=self.cfg.get("ml_ratio", 1))

    def mix_ssd(self, ti, u, ws, full):
        av = self.av
        g = u // 2
        sm = self.tsm
        xpre = av(self.OFF_XPRE, [128, 4, T + 3], F32)
        xtok = av(self.OFF_XPRE, [128, 4, T], F32)
        bufY = av(self.OFF_BUFY, [128, 4, T], F32)
        xdt = av(self.OFF_XDT, [128, 4, T], BF16)
        Lr = av(self.OFF_LR, [128, 8, 128], F32)
        M = av(self.OFF_M, [128, 8, 128], BF16)
        yt = av(self.OFF_YT, [128, T], F32)
        yi = av(self.OFF_YI, [128, T], F32)
        STbf = av(self.OFF_MISC, [128, T], BF16)
        wend = av(self.OFF_MISC + 1024, [128, 8], F32)
        decb = av(self.OFF_MISC + 1056, [128, 8], F32)
        tcol = av(self.OFF_MISC + 1088, [128, 8], F32)
        xw = av(self.OFF_MISC + 1216, [128, T], BF16)
        zt = av(self.OFF_MISC + 2240, [128, 256], F32)
        BX, BY = "ssd_bufX", "ssd_bufY"
        for a in range(2):
            wt, wn = self.wget(f"in_X{u}{a}")
            for oc in range(2):
                j = 2 * a + oc
                chunk = 4 * u + j
                b = self.bank()
                self.proj_feat(wt, wn, oc, b)
                self.dve("tensor_copy", ["convhist"], [BX], out=xpre[:, j, 0:3], in_=self.convhist[:, chunk, :])
                self.act([f"pb{b}"], [BX], out=xpre[:, j, 3:T + 3], in_=self.pb[b][:, 0:T], func=AF.Copy)
                self.dve("tensor_copy", [BX], ["convhist"], out=self.convhist[:, chunk, :], in_=xpre[:, j, T:T + 3])
                self.conv_silu(xpre[:, j, :], BX, chunk, bufY[:, j, :], BY)
            if ws and full:
                self.smp_x(u, a, wt, wn)
        for tc in range(4):
            b = self.bank()
            for j in range(4):
                self.tr([BY, "identf"], [f"pb{b}"], out=self.pb[b][:, j * 128:(j + 1) * 128], in_=bufY[:, j, tc * 128:(tc + 1) * 128],
                        identity=self.identf[:])
            self.act([f"pb{b}"], [BX], out=xtok[:, tc, :], in_=self.pb[b][:, :], func=AF.Copy)
        dtb = sm[:, :, 56 + 8 * u:64 + 8 * u].unsqueeze(3).to_broadcast([128, 4, 8, HP])
        self.dve("tensor_tensor", [BX, "tsm"], ["ssd_xdt"], out=xdt.rearrange("p a (h q) -> p a h q", q=HP),
                 in0=xtok.rearrange("p a (h q) -> p a h q", q=HP), in1=dtb, op=ALU.mult)
        self.cbTm = av(self.OFF_CONVACC, [128, 4, 128], F32)
        if full:
            for tc in range(4):
                ts_ = slice(tc * 128, (tc + 1) * 128)
                b = self.bank()
                self.mm([f"BCT{g}0", f"BCT{g}1"], [f"pb{b}"], out=self.pb[b][:, 0:128], lhsT=self.BCT[:, g, 0, ts_], rhs=self.BCT[:, g, 1, ts_],
                        start=True, stop=True)
                self.dve("tensor_tensor", [f"pb{b}", "Umat"], ["convacc"], out=self.cbTm[:, tc, :], in0=self.pb[b][:, 0:128], in1=self.Umat[:], op=ALU.mult)
        if full and u % 2 == 0:
            self.dve("memset", [], ["ssq"], ap=self.ssq[:, 0:8], constant=0.0)
        if full:
            self.act(["Sst"], ["ssd_STbf"], out=STbf[:], in_=self.Sst[:, u, :], func=AF.Copy)
        self.interleave(self.ssd_chunk_gen(u, full, locals()), self.smp_ssd(u) if (ws and full) else None, ratio=self.cfg.get("ssd_ratio", 2))
        if not full:
            return
        self.ssd_gates(u, ws, locals())

    def ssd_chunk_gen(self, u, full, L):
        g, sm, BX, BY = L["g"], L["sm"], L["BX"], L["BY"]
        Lr, M, yt, yi, STbf, wend, decb, xw, xdt, xtok, bufY = (L[k] for k in ("Lr", "M", "yt", "yi", "STbf", "wend", "decb", "xw", "xdt", "xtok", "bufY"))
        for tc in range(4):
            ts_ = slice(tc * 128, (tc + 1) * 128)
            if full:
                self.dve("tensor_tensor", ["Umat", "tsm"], ["ssd_Lr"], out=Lr[:], in0=self.Umat[:].unsqueeze(1).to_broadcast([128, 8, 128]),
                         in1=sm[:, tc, 88 + 8 * u:96 + 8 * u].unsqueeze(2).to_broadcast([128, 8, 128]), op=ALU.mult)
                for q in range(2):
                    b = self.bank()
                    self.mm(["Lsm", "ssd_Lr"], [f"pb{b}"], out=self.pb[b][:, :], lhsT=self.Lsm[:],
                            rhs=Lr[:, 4 * q:4 * q + 4, :].rearrange("p a b -> p (a b)"), start=True, stop=True)
                    self.act([f"pb{b}"], ["ssd_M"], out=M[:, 4 * q:4 * q + 4, :].rearrange("p a b -> p (a b)"), in_=self.pb[b][:, :], func=AF.Exp)
                self.dve("tensor_tensor", ["ssd_M", "convacc"], ["ssd_M"], out=M[:], in0=M[:],
                         in1=self.cbTm[:, tc, :].unsqueeze(1).to_broadcast([128, 8, 128]), op=ALU.mult)
                yield
                bI, bE = self.bank(), self.bank()
                for hh in range(8):
                    self.mm(["ssd_M", "ssd_xdt"], [f"pb{bI}"], out=self.pb[bI][:, hh * HP:(hh + 1) * HP], lhsT=M[:, hh, :],
                            rhs=xdt[:, tc, hh * HP:(hh + 1) * HP], start=True, stop=True)
                self.mm([f"BCT{g}1", "ssd_STbf"], [f"pb{bE}"], out=self.pb[bE][:, :], lhsT=self.BCT[:, g, 1, ts_], rhs=STbf[:], start=True, stop=True)
                ebb = sm[:, tc, 220 + 8 * u:228 + 8 * u].unsqueeze(2).to_broadcast([128, 8, HP])
                self.dve("tensor_tensor", [f"pb{bE}", "tsm"], ["ssd_yi"], out=yi.rearrange("p (h q) -> p h q", q=HP),
                         in0=self.pb[bE][:, :].rearrange("p (h q) -> p h q", q=HP), in1=ebb, op=ALU.mult)
                self.dve("tensor_tensor", [f"pb{bI}", "ssd_yi"], ["ssd_yt"], out=yt[:], in0=self.pb[bI][:, :], in1=yi[:], op=ALU.add)
                Db = self.smallb[:, 4, 8 * u:8 * u + 8].unsqueeze(2).to_broadcast([128, 8, HP])
                self.dve("tensor_tensor", [BX, "smallb"], ["ssd_yi"], out=yi.rearrange("p (h q) -> p h q", q=HP),
                         in0=xtok[:, tc, :].rearrange("p (h q) -> p h q", q=HP), in1=Db, op=ALU.mult)
                self.dve("tensor_tensor", ["ssd_yt", "ssd_yi"], [BY], out=bufY[:, tc, :], in0=yt[:], in1=yi[:], op=ALU.add)
                yield
            self.dve("tensor_tensor", ["tsm"], ["ssd_wend"], out=wend[:], in0=sm[:, tc, 152 + 8 * u:160 + 8 * u],
                     in1=sm[:, tc, 120 + 8 * u:128 + 8 * u], op=ALU.subtract)
            self.act(["ssd_wend"], ["ssd_wend"], out=wend[:], in_=wend[:], func=AF.Exp)
            self.dve("tensor_tensor", ["ssd_xdt", "ssd_wend"], ["ssd_xw"], out=xw.rearrange("p (h q) -> p h q", q=HP),
                     in0=xdt[:, tc, :].rearrange("p (h q) -> p h q", q=HP), in1=wend[:].unsqueeze(2).to_broadcast([128, 8, HP]), op=ALU.mult)
            bS = self.bank()
            self.mm([f"Btok{g}", "ssd_xw"], [f"pb{bS}"], out=self.pb[bS][:, :], lhsT=self.Btok[:, g, tc, :], rhs=xw[:], start=True, stop=True)
            self.act(["tsm"], ["ssd_decb"], out=decb[:], in_=sm[:, tc, 152 + 8 * u:160 + 8 * u], func=AF.Exp)
            Sv = self.Sst[:, u, :].rearrange("p (h q) -> p h q", q=HP)
            self.dve("tensor_tensor", ["Sst", "ssd_decb"], ["Sst"], out=Sv, in0=Sv, in1=decb[:].unsqueeze(2).to_broadcast([128, 8, HP]), op=ALU.mult)
            self.dve("tensor_tensor", ["Sst", f"pb{bS}"], ["Sst"], out=self.Sst[:, u, :], in0=self.Sst[:, u, :], in1=self.pb[bS][:, :], op=ALU.add)
            if full and tc < 3:
                self.act(["Sst"], ["ssd_STbf"], out=STbf[:], in_=self.Sst[:, u, :], func=AF.Copy)
            yield

    def ssd_gates(self, u, ws, L):
        av = self.av
        g, BX, BY = L["g"], L["BX"], L["BY"]
        bufY, xdt, zt, tcol = L["bufY"], L["xdt"], L["zt"], L["tcol"]
        for a in range(2):
            wt, wn = self.wget(f"in_Z{u}{a}")
            for tc in range(4):
                b = self.bank()
                self.proj_tok(wt, wn, tc, b)
                self.act([f"pb{b}"], ["ssd_zt"], out=zt[:], in_=self.pb[b][:, 0:256], func=AF.Silu)
                self.dve("memset", [], ["ssd_tcol"], ap=tcol[:, 0:1], constant=0.0)
                ysl = bufY[:, tc, a * 256:(a + 1) * 256]
                self.dve("scalar_tensor_tensor", [BY, "ssd_zt", "ssd_tcol"], [BY, "ssd_tcol"], out=ysl, in0=ysl, scalar=1.0, in1=zt[:],
                         op0=ALU.mult, op1=ALU.mult)
                self.act([BY, "ssd_tcol"], ["ssd_zt", "ssd_tcol"], out=zt[:], in_=ysl, func=AF.Square, accum_out=tcol[:, 0:1])
                self.dve("tensor_tensor", ["ssq", "ssd_tcol"], ["ssq"], out=self.ssq[:, tc:tc + 1], in0=self.ssq[:, tc:tc + 1], in1=tcol[:, 0:1], op=ALU.add)
            if ws:
                self.smp_feat(u, a, wt, wn, "zS", AF.Silu)
        yz2 = self.yzgA if u % 2 == 0 else xdt
        yzn = "yzgA" if u % 2 == 0 else "ssd_xdt"
        for a in range(2):
            wt, wn = self.wget(f"in_GB{u}{a}")
            for tc in range(4):
                b = self.bank()
                self.proj_tok(wt, wn, tc, b)
                self.act([f"pb{b}"], ["ssd_zt"], out=zt[:], in_=self.pb[b][:, 0:256], func=AF.Sigmoid)
                self.dve("tensor_tensor", [BY, "ssd_zt"], [yzn], out=yz2[:, tc, a * 256:(a + 1) * 256], in0=bufY[:, tc, a * 256:(a + 1) * 256],
                         in1=zt[:], op=ALU.mult)
            if ws:
                self.smp_feat(u, a, wt, wn, "gbS", AF.Sigmoid)
        if u % 2 == 1:
            sc = av(self.OFF_M, [128, T], BF16)
            for tc in range(4):
                ts_ = slice(tc * 128, (tc + 1) * 128)
                r = self.ssq[:, 8 + tc:9 + tc]
                self.dve("tensor_scalar", ["ssq"], ["ssq"], out=r, in0=self.ssq[:, tc:tc + 1], scalar1=1.0 / 1024, scalar2=EPS, op0=ALU.mult, op1=ALU.add)
                self.act(["ssq"], ["ssq"], out=r, in_=r, func=AF.Sqrt)
                self.dve("reciprocal", ["ssq"], ["ssq"], out=r, in_=r)
                for (src, srcn, uh) in ((self.yzgA, "yzgA", u - 1), (xdt, "ssd_xdt", u)):
                    self.dve("tensor_scalar", [srcn, "ssq"], ["ssd_M"], out=sc[:], in0=src[:, tc, :], scalar1=r, scalar2=None, op0=ALU.mult)
                    b = self.bank()
                    pt = self.pb[b][:].bitcast(BF16)
                    for j in range(4):
                        self.tr(["ssd_M", "ident"], [f"pb{b}"], out=pt[:, j * 128:(j + 1) * 128], in_=sc[:, j * 128:(j + 1) * 128], identity=self.ident[:])
                    for j in range(4):
                        ch = uh * 4 + j
                        self.dve("scalar_tensor_tensor", [f"pb{b}", "wcols", "mergedT"], ["mergedT"], out=self.mergedT[:, ch, ts_],
                                 in0=pt[:, j * 128:(j + 1) * 128], scalar=self.wcols[:, 4, ch:ch + 1], in1=self.mergedT[:, ch, ts_],
                                 op0=ALU.mult, op1=ALU.add)

    def mix_out(self, ti, ws):
        chunks = [0, 1, 2, 3] + ([4] if ws else [])
        for bo in range(8):
            wt, wn = self.wget(f"wo_{bo}")
            for tc in chunks:
                rows = 128 if tc < 4 else NS
                b = self.bank()
                for dc in range(NKC):
                    self.mm([wn, "mergedT"], [f"pb{b}"], out=self.pb[b][:rows, 0:256], lhsT=self.mergedT[:, dc, tc * 128:tc * 128 + rows],
                            rhs=wt[:, dc, :], start=(dc == 0), stop=(dc == NKC - 1))
                xs = self.xres[:rows, tc, bo * 256:(bo + 1) * 256]
                self.dve("tensor_tensor", [f"pb{b}", f"xres{tc}"], [f"xres{tc}"], out=xs, in0=xs, in1=self.pb[b][:rows, 0:256], op=ALU.add)

    def store_pstates(self):
        self.dma("st_pC", ["Cst"], [], out=self.o_pC.rearrange("h k v -> (h k) v").rearrange("(hk p) v -> p hk v", p=128),
                 in_=self.Cst[:].rearrange("p a b c -> p (a b) c"))
        self.dma("st_pn", ["nst"], [], out=self.o_pn.rearrange("(hk p) -> p hk", p=128), in_=self.nst[:], nc_ok=True)
        self.dma("st_pm", ["mcol"], [], out=self.o_pm.rearrange("(o h) -> o h", o=1), in_=self.mcol[0:1, :])
        for j in range(3):
            self.dma(f"st_pc{j}", ["convhist"], [], out=self.o_pconv[j].rearrange("(c p) -> p c", p=128), in_=self.convhist[:, :, j], nc_ok=True)
        stage = self.av(self.OFF_YT, [128, 4, 128], F32)
        for u in range(4):
            b = self.bank()
            for q in range(4):
                self.tr(["Sst", "identf"], [f"pb{b}"], out=self.pb[b][:, q * 128:(q + 1) * 128], in_=self.Sst[:, u, q * 128:(q + 1) * 128],
                        identity=self.identf[:])
            self.act([f"pb{b}"], ["ssd_yt"], out=stage.rearrange("p a b -> p (a b)"), in_=self.pb[b][:, :], func=AF.Copy)
            self.dma("st_pS", ["ssd_yt"], [], out=self.o_pS[u * 512:(u + 1) * 512, :].rearrange("(q p) n -> p q n", p=128), in_=stage)

    SOFF = 15360

    def alloc_smp(self):
        sb = self.sb
        self.smS = sb("smS", [NS, 64], F32)
        self.xsT = sb("xsT", [128, 20, NS], F32)
        self.dtT = sb("dtT", [128, 2, NKC, NS], F32)
        self.histT = sb("histT", [128, 20, NS * 3], F32)
        self.BCs = sb("BCs", [NS, 2, 256], BF16)
        self.ssqS = sb("ssqS", [NS, 8], F32)
        self.ones16b = sb("ones16b", [NS, 128], BF16)

    def smp_views(self):
        av, o = self.av, self.SOFF
        v = {}
        v["qS"] = av(o, [NS, DK], F32)
        v["kS"] = av(o + 1024, [NS, DK], F32)
        v["vS"] = av(o + 2048, [NS, DV], F32)
        v["gS"] = av(o + 4096, [NS, DV], F32)
        v["kwbf"] = av(o + 6144, [NS, DK], BF16)
        v["kmask"] = av(o + 6656, [NS, DK], BF16)
        v["vSbf"] = av(o + 7168, [NS, DV], BF16)
        v["qTs"] = av(o + 8192, [128, 2, NS], F32)
        v["qmask"] = av(o + 8320, [128, 2, NS * NS], F32)
        v["decbc"] = av(o + 10368, [128, NS], F32)
        v["nS"] = av(o + 10432, [NS, DK], F32)
        v["hS"] = av(o + 11456, [NS, DV], F32)
        v["yaS"] = av(o + 13504, [NS, DV], BF16)
        v["cS"] = av(40192, [NS, 32], F32)
        v["tmpS"] = av(39168, [NS, 256], F32)
        return v

    def smp_init(self):
        self._op("pool", "memset", [], ["ones16b"], ap=self.ones16b[:], constant=1.0)
        stg = self.av(24576, [NS * 3, XBC], F32)
        self.dma("ld_sst", [], ["s_stg"], out=stg, in_=self.st_conv[:, :])
        for c0 in range(0, 20, 8):
            c1 = min(20, c0 + 8)
            b = self.bank()
            for ch in range(c0, c1):
                self.tr(["s_stg", "identf"], [f"pb{b}"], out=self.pb[b][:, (ch - c0) * 48:(ch - c0 + 1) * 48], in_=stg[:, ch * 128:(ch + 1) * 128],
                        identity=self.identf[:NS * 3, :NS * 3])
            self.dve("tensor_copy", [f"pb{b}"], ["histT"], out=self.histT[:, c0:c1, :].rearrange("p a b -> p (a b)"), in_=self.pb[b][:, 0:(c1 - c0) * 48])
        src = self.st_conv.rearrange("(b j) c -> b j c", j=3)
        self.dma("st_sc0", [], [], out=self.o_sconv[:, 0:2, :], in_=src[:, 1:3, :])
        self.dve("memset", [], ["ssqS"], ap=self.ssqS[:], constant=0.0)
        self.barrier()

    def smp_proj_tok(self, wt, wn, name, c0=0, scale=None, func=None, mul=False):
        v = self.smp_views()
        P7 = ["pb7g0", "pb7g1", "pb7u0", "pb7u1"]
        t0 = T
        for dc in range(NKC):
            self.mm([wn, "xnT"], P7, out=self.pb[7][:NS, 0:256], lhsT=self.xnT[:, dc, t0:t0 + NS], rhs=wt[:, dc, 0:256],
                    start=(dc == 0), stop=(dc == NKC - 1))
        dst = v[name][:, c0:c0 + 256]
        kw = {}
        if scale is not None:
            kw["scale"] = scale
        if not mul:
            self.act(P7, ["s_" + name], out=dst, in_=self.pb[7][:NS, 0:256], func=func or AF.Copy, **kw)
        else:
            self.act(P7, ["s_tmpS"], out=v["tmpS"][:], in_=self.pb[7][:NS, 0:256], func=func or AF.Copy, **kw)
            self.dve("tensor_tensor", ["s_tmpS", "s_" + name], ["s_" + name], out=dst, in0=dst, in1=v["tmpS"][:], op=ALU.mult)

    def smp_small(self, wt, wn):
        P7 = ["pb7g0", "pb7g1", "pb7u0", "pb7u1"]
        S = self.smS
        for dc in range(NKC):
            self.mm([wn, "xnT"], P7, out=self.pb[7][:NS, 0:40], lhsT=self.xnT[:, dc, T:TW], rhs=wt[:, dc, 0:40], start=(dc == 0), stop=(dc == NKC - 1))
        R, W = ["smS", "smallb"], ["smS"]
        self.dve("tensor_copy", P7, W, out=S[:, 0:40], in_=self.pb[7][:NS, 0:40])
        self.dma("ld_sm", [], ["smS"], out=S[:, 48:52], in_=self.st_m[:, :])
        self.dve("tensor_tensor", R, W, out=S[:, 40:44], in0=S[:, 0:4], in1=self.smallb[:NS, 0, 0:4], op=ALU.add)
        z = S[:, 56:60]
        self.dve("tensor_tensor", R, W, out=z, in0=S[:, 4:8], in1=self.smallb[:NS, 1, 0:4], op=ALU.add)
        self._softplus(z, S[:, 44:48], S[:, 60:64], True, "smS", NS)
        self.dve("tensor_tensor", R, W, out=S[:, 56:60], in0=S[:, 44:48], in1=S[:, 48:52], op=ALU.add)
        self.dve("tensor_tensor", R, W, out=S[:, 52:56], in0=S[:, 56:60], in1=S[:, 40:44], op=ALU.max)
        self.dma("st_sm", ["smS"], [], out=self.o_sm[:, :], in_=S[:, 52:56])
        self.dve("tensor_tensor", R, W, out=S[:, 60:64], in0=S[:, 56:60], in1=S[:, 52:56], op=ALU.subtract)
        self.act(R, W, out=S[:, 60:64], in_=S[:, 60:64], func=AF.Exp)
        self.dve("tensor_tensor", R, W, out=S[:, 56:60], in0=S[:, 40:44], in1=S[:, 52:56], op=ALU.subtract)
        self.act(R, W, out=S[:, 56:60], in_=S[:, 56:60], func=AF.Exp)
        dtx = self.av(self.SOFF, [NS, 2, SH, HP], F32)
        tmp = self.av(self.SOFF + 16384 - 512, [NS, 96], F32)
        self.dve("tensor_tensor", R, ["s_tmp96"], out=tmp[:, 0:32], in0=S[:, 8:40], in1=self.smallb[:NS, 2, :], op=ALU.add)
        self._softplus(tmp[:, 0:32], tmp[:, 32:64], tmp[:, 64:96], False, "s_tmp96", NS)
        self.dve("tensor_tensor", ["s_tmp96", "Arow"], ["s_tmp96"], out=tmp[:, 64:96], in0=tmp[:, 32:64], in1=self.Arow[:NS, :], op=ALU.mult)
        self.act(["s_tmp96"], ["s_tmp96"], out=tmp[:, 64:96], in_=tmp[:, 64:96], func=AF.Exp)
        for i in range(2):
            self.dve("tensor_copy", ["s_tmp96"], ["s_dtx"], out=dtx[:, i, :, :], in_=tmp[:, 32 + 32 * i:64 + 32 * i].unsqueeze(2).to_broadcast([NS, SH, HP]))
            b = self.bank()
            flat = dtx[:, i, :, :].rearrange("p a b -> p (a b)")
            for ch in range(NKC):
                self.tr(["s_dtx", "identf"], [f"pb{b}"], out=self.pb[b][:, ch * NS:(ch + 1) * NS], in_=flat[:, ch * 128:(ch + 1) * 128],
                        identity=self.identf[:NS, :NS])
            self.dve("tensor_copy", [f"pb{b}"], ["dtT"], out=self.dtT[:, i, :, :].rearrange("p a b -> p (a b)"), in_=self.pb[b][:, 0:NKC * NS])

    def _softplus(self, z, out, tmp, neg, nm, rows):
        R, Wt = [nm], [nm]
        self.dve("scalar_tensor_tensor", R, Wt, out=tmp, in0=z, scalar=-1.0, in1=z, op0=ALU.mult, op1=ALU.max)
        self.act(R, Wt, out=tmp, in_=tmp, func=AF.Exp, scale=-1.0)
        self.act(R + ["onesf"], Wt, out=tmp, in_=tmp, func=AF.Ln, bias=self.onesf[:rows, 0:1], scale=1.0)
        if neg:
            self.dve("scalar_tensor_tensor", R, Wt, out=out, in0=z, scalar=0.0, in1=tmp, op0=ALU.min, op1=ALU.subtract)
        else:
            self.dve("scalar_tensor_tensor", R, Wt, out=out, in0=z, scalar=0.0, in1=tmp, op0=ALU.max, op1=ALU.add)

    def smp_mlstm(self, h):
        v = self.smp_views()
        S = self.smS
        P7 = ["pb7g0", "pb7g1", "pb7u0", "pb7u1"]
        qS, kS, vS, gS, nS, hS, cS = v["qS"], v["kS"], v["vS"], v["gS"], v["nS"], v["hS"], v["cS"]
        wend, dec, mnew = S[:, 56 + h:57 + h], S[:, 60 + h:61 + h], S[:, 52 + h:53 + h]
        self.dma("ld_sn", [], ["s_nS", "s_dtx"], out=nS[:], in_=self.st_n[:, h * DK:(h + 1) * DK])
        self.dve("tensor_scalar", ["s_nS", "smS"], ["s_nS"], out=nS[:], in0=nS[:], scalar1=dec, scalar2=None, op0=ALU.mult)
        self.dve("scalar_tensor_tensor", ["s_kS", "smS", "s_nS"], ["s_nS"], out=nS[:], in0=kS[:], scalar=wend, in1=nS[:], op0=ALU.mult, op1=ALU.add)
        self.dma("st_sn", ["s_nS"], [], out=self.o_sn[:, h * DK:(h + 1) * DK], in_=nS[:])
        self.dve("memset", [], ["s_cS"], ap=cS[:, 0:1], constant=0.0)
        self.dve("scalar_tensor_tensor", ["s_qS", "s_nS", "s_cS"], ["s_tmpS", "s_cS"], out=v["tmpS"][:], in0=qS[:], scalar=1.0, in1=nS[:],
                 op0=ALU.mult, op1=ALU.mult, accum_out=cS[:, 0:1])
        self.dve("tensor_scalar", ["s_kS", "smS"], ["s_kwbf"], out=v["kwbf"][:], in0=kS[:], scalar1=wend, scalar2=None, op0=ALU.mult)
        self.dve("tensor_copy", ["s_vS"], ["s_vSbf"], out=v["vSbf"][:], in_=vS[:])
        self.dve("tensor_scalar", ["identf", "smS"], ["s_cS"], out=cS[:, 16:32], in0=self.identf[:NS, :NS], scalar1=dec, scalar2=None, op0=ALU.mult)
        self.mm(["onesf", "s_cS"], P7, out=self.pb[7][:, 0:NS], lhsT=self.onesf[:NS, :], rhs=cS[:, 16:32], start=True, stop=True)
        self.dve("tensor_copy", P7, ["s_decbc"], out=v["decbc"][:], in_=self.pb[7][:, 0:NS])
        for kc in range(2):
            self.tr(["s_qS", "identf"], P7, out=self.pb[7][:, 32 + kc * NS:32 + (kc + 1) * NS], in_=qS[:, kc * 128:(kc + 1) * 128], identity=self.identf[:NS, :NS])
        self.dve("tensor_copy", P7, ["s_qTs"], out=v["qTs"][:].rearrange("p a b -> p (a b)"), in_=self.pb[7][:, 32:32 + 2 * NS])
        self.dve("memset", [], ["s_qmask"], ap=v["qmask"][:].rearrange("p a b -> p (a b)"), constant=0.0)
        self.dve("tensor_copy", ["s_qTs", "s_qmask"], ["s_qmask"], out=v["qmask"][:, :, 0:NS * NS:NS + 1], in_=v["qTs"][:])
        def _ldC(j):
            Cb_ = self.sarea[:, (j % 2) * 1024:(j % 2 + 1) * 1024].rearrange("p (a b) -> p a b", b=DV)
            self.dma(f"ld_C{j % 2}", [], [f"s_C{j % 2}"], out=Cb_, in_=self.st_C[j, h].rearrange("(kc p) v -> p kc v", p=128))
        _ldC(0)
        for j in range(NS):
            Cb = self.sarea[:, (j % 2) * 1024:(j % 2 + 1) * 1024].rearrange("p (a b) -> p a b", b=DV)
            cn = f"s_C{j % 2}"
            if j + 1 < NS:
                _ldC(j + 1)
            self.dve("tensor_scalar", ["s_kwbf", "identf"], ["s_kmask"], out=v["kmask"][:], in0=v["kwbf"][:], scalar1=self.identf[:NS, j:j + 1],
                     scalar2=None, op0=ALU.mult)
            for kc in range(2):
                b = self.bank()
                self.mm(["s_kmask", "s_vSbf"], [f"pb{b}"], out=self.pb[b][:, :], lhsT=v["kmask"][:, kc * 128:(kc + 1) * 128], rhs=v["vSbf"][:],
                        start=True, stop=True)
                self.dve("scalar_tensor_tensor", [f"pb{b}", cn, "s_decbc"], [cn], out=Cb[:, kc, :], in0=Cb[:, kc, :], scalar=v["decbc"][:, j:j + 1],
                         in1=self.pb[b][:, :], op0=ALU.mult, op1=ALU.add)
            self.dma(f"st_C{j % 2}", [cn], [], out=self.o_sC[j, h].rearrange("(kc p) v -> p kc v", p=128), in_=Cb)
            for kc in range(2):
                self.mm(["s_qmask", cn], P7, out=self.pb[7][:NS, :], lhsT=v["qmask"][:, kc, j * NS:(j + 1) * NS], rhs=Cb[:, kc, :],
                        start=(j == 0 and kc == 0), stop=(j == NS - 1 and kc == 1))
            yield
        C = ["s_cS", "smS"]
        self.act(C, ["s_cS"], out=cS[:, 1:2], in_=mnew, func=AF.Exp, scale=-1.0)
        self.dve("scalar_tensor_tensor", C, ["s_cS"], out=cS[:, 2:3], in0=cS[:, 0:1], scalar=-1.0, in1=cS[:, 0:1], op0=ALU.mult, op1=ALU.max)
        self.dve("tensor_tensor", C, ["s_cS"], out=cS[:, 2:3], in0=cS[:, 2:3], in1=cS[:, 1:2], op=ALU.max)
        self.dve("reciprocal", C, ["s_cS"], out=cS[:, 3:4], in_=cS[:, 2:3])
        self.act(P7, ["s_hS"], out=hS[:], in_=self.pb[7][:NS, :], func=AF.Copy)
        self.dve("memset", [], ["s_cS"], ap=cS[:, 4:5], constant=0.0)
        self.act(["s_hS"] + C, ["s_yaS", "s_cS"], out=v["yaS"][:], in_=hS[:], func=AF.Square, scale=cS[:, 3:4], accum_out=cS[:, 4:5])
        self.dve("tensor_scalar", C, ["s_cS"], out=cS[:, 5:6], in0=cS[:, 4:5], scalar1=1.0 / DV, scalar2=EPS, op0=ALU.mult, op1=ALU.add)
        self.act(C, ["s_cS"], out=cS[:, 5:6], in_=cS[:, 5:6], func=AF.Sqrt)
        self.dve("reciprocal", C, ["s_cS"], out=cS[:, 5:6], in_=cS[:, 5:6])
        self.dve("tensor_tensor", C, ["s_cS"], out=cS[:, 6:7], in0=cS[:, 5:6], in1=cS[:, 3:4], op=ALU.mult)
        self.dve("scalar_tensor_tensor", ["s_hS", "s_gS"] + C, ["s_yaS"], out=v["yaS"][:], in0=hS[:], scalar=cS[:, 6:7], in1=gS[:],
                 op0=ALU.mult, op1=ALU.mult)
        b = self.bank()
        pt = self.pb[b][:].bitcast(BF16)
        for j in range(4):
            self.tr(["s_yaS", "ident"], [f"pb{b}"], out=pt[:, j * NS:(j + 1) * NS], in_=v["yaS"][:, j * 128:(j + 1) * 128], identity=self.ident[:NS, :NS])
        for j in range(4):
            self.dve("tensor_scalar", [f"pb{b}", "wcols"], ["mergedT"], out=self.mergedT[:, h * 4 + j, T:TW], in0=pt[:, j * NS:(j + 1) * NS],
                     scalar1=self.wcols[:, 3, h * 4 + j:h * 4 + j + 1], scalar2=None, op0=ALU.mult)

    def smp_conv_chunk(self, wt, wn, oc, chunk, raw_ap, col):
        P7 = ["pb7g0", "pb7g1", "pb7u0", "pb7u1"]
        for dc in range(NKC):
            self.mm([wn, "xnT"], P7, out=self.pb[7][:, col:col + NS], lhsT=wt[:, dc, oc * 128:(oc + 1) * 128], rhs=self.xnT[:, dc, T:TW],
                    start=(dc == 0), stop=(dc == NKC - 1))
        cur = self.pb[7][:, col:col + NS]
        self.dve("tensor_copy", P7, ["s_raw"], out=raw_ap, in_=cur)
        hist = self.histT[:, chunk, :].rearrange("p (t j) -> p t j", j=3)
        w = self.convwc
        acc = self.av(self.OFF_MISC + 3264, [128, NS], F32)
        self.dve("tensor_scalar", ["histT", "convwc"], ["s_acc"], out=acc[:], in0=hist[:, :, 0], scalar1=w[:, chunk, 0:1], scalar2=None, op0=ALU.mult)
        for j in (1, 2):
            self.dve("scalar_tensor_tensor", ["histT", "convwc", "s_acc"], ["s_acc"], out=acc[:], in0=hist[:, :, j], scalar=w[:, chunk, j:j + 1],
                     in1=acc[:], op0=ALU.mult, op1=ALU.add)
        self.dve("scalar_tensor_tensor", ["s_raw", "convwc", "s_acc"], ["s_acc"], out=acc[:], in0=raw_ap, scalar=w[:, chunk, 3:4], in1=acc[:],
                 op0=ALU.mult, op1=ALU.add)
        self.act(["s_acc", "convwc"], ["xsT"], out=self.xsT[:, chunk, :], in_=acc[:], func=AF.Silu, bias=w[:, chunk, 4:5], scale=1.0)

    def smp_raw_out(self, raw, nchunk, dcol0):
        b = self.bank()
        for j in range(nchunk):
            self.tr(["s_raw", "identf"], [f"pb{b}"], out=self.pb[b][:NS, j * 128:(j + 1) * 128], in_=raw[:, j, :], identity=self.identf[:])
        stg = self.av(self.OFF_MISC + 3328, [NS, 512], F32)
        self.act([f"pb{b}"], ["s_rstg"], out=stg[:, 0:nchunk * 128], in_=self.pb[b][:NS, 0:nchunk * 128], func=AF.Copy)
        self.dma("st_rs", ["s_rstg"], [], out=self.o_sconv[:, 2, dcol0:dcol0 + nchunk * 128], in_=stg[:, 0:nchunk * 128])

    def smp_bc(self, g, wt, wn):
        raw = self.av(self.OFF_MISC + 5376, [128, 4, NS], F32)
        for oc in range(2):
            self.smp_conv_chunk(wt, wn, oc, 16 + g + 2 * oc, raw[:, oc, :], 64 + 16 * oc)
        b = self.bank()
        for oc in range(2):
            self.tr(["s_raw", "identf"], [f"pb{b}"], out=self.pb[b][:NS, oc * 128:(oc + 1) * 128], in_=raw[:, oc, :], identity=self.identf[:])
        stg = self.av(self.OFF_MISC + 3328, [NS, 512], F32)
        self.act([f"pb{b}"], ["s_rstg"], out=stg[:, 0:256], in_=self.pb[b][:NS, 0:256], func=AF.Copy)
        for oc in range(2):
            d0 = 2048 + 256 * oc + 128 * g
            self.dma("st_rs", ["s_rstg"], [], out=self.o_sconv[:, 2, d0:d0 + 128], in_=stg[:, oc * 128:(oc + 1) * 128])
        b = self.bank()
        for oc in range(2):
            self.tr(["xsT", "identf"], [f"pb{b}"], out=self.pb[b][:NS, oc * 128:(oc + 1) * 128], in_=self.xsT[:, 16 + g + 2 * oc, :], identity=self.identf[:])
        self.dve("tensor_copy", [f"pb{b}"], ["BCs"], out=self.BCs[:, g, :], in_=self.pb[b][:NS, 0:256])

    def smp_x(self, u, a, wt, wn):
        raw = self.av(self.OFF_MISC + 5376, [128, 4, NS], F32)
        for oc in range(2):
            j = 2 * a + oc
            self.smp_conv_chunk(wt, wn, oc, 4 * u + j, raw[:, j, :], 64 + 16 * oc)
        if a == 1:
            self.smp_raw_out(raw, 4, u * 512)

    def smp_ssd(self, u):
        g = u // 2
        SA = self.sarea
        t2 = SA[:, 1024:1536].rearrange("p (a b) -> p a b", b=SN)
        ysT = SA[:, 1536:1792].rearrange("p (a b) -> p a b", b=NS)
        xdtT = self.av(self.OFF_MISC + 5632, [128, 4, NS], F32)
        bcm = self.av(self.OFF_MISC + 5888, [NS, 256], BF16)
        cs = slice(4 * u, 4 * u + 4)
        self.dve("tensor_tensor", ["xsT", "dtT"], ["s_xdtT"], out=xdtT[:], in0=self.xsT[:, cs, :], in1=self.dtT[:, 0, cs, :], op=ALU.mult)
        self.dve("memset", [], ["s_ysT"], ap=ysT[:, cs, :], constant=0.0)
        def _ldS(j):
            Sb_ = SA[:, (j % 2) * 512:(j % 2 + 1) * 512].rearrange("p (a b) -> p a b", b=SN)
            self.dma(f"ld_S{j % 2}", [], [f"s_S{j % 2}"], out=Sb_, in_=self.st_S[j, u * 512:(u + 1) * 512, :].rearrange("(rc p) n -> p rc n", p=128))
        _ldS(0)
        for j in range(NS):
            Sb = SA[:, (j % 2) * 512:(j % 2 + 1) * 512].rearrange("p (a b) -> p a b", b=SN)
            sn_ = f"s_S{j % 2}"
            if j + 1 < NS:
                _ldS(j + 1)
            self.dve("tensor_scalar", ["BCs", "identf"], ["s_bcm"], out=bcm[:], in0=self.BCs[:, g, :], scalar1=self.identf[:NS, j:j + 1], scalar2=None,
                     op0=ALU.mult)
            b = self.bank()
            self.mm(["ones16b", "s_bcm"], [f"pb{b}"], out=self.pb[b][:, 0:256], lhsT=self.ones16b[:], rhs=bcm[:], start=True, stop=True)
            self.dve("tensor_tensor", [f"pb{b}", "s_xdtT"], ["s_t2"], out=t2, in0=self.pb[b][:, 0:SN].unsqueeze(1).to_broadcast([128, 4, SN]),
                     in1=xdtT[:, :, j:j + 1].to_broadcast([128, 4, SN]), op=ALU.mult)
            for rc in range(4):
                self.dve("scalar_tensor_tensor", [sn_, "dtT", "s_t2"], [sn_], out=Sb[:, rc, :], in0=Sb[:, rc, :], scalar=self.dtT[:, 1, 4 * u + rc, j:j + 1],
                         in1=t2[:, rc, :], op0=ALU.mult, op1=ALU.add)
            self.dma(f"st_S{j % 2}", [sn_], [], out=self.o_sS[j, u * 512:(u + 1) * 512, :].rearrange("(rc p) n -> p rc n", p=128), in_=Sb)
            for rc in range(4):
                self.dve("scalar_tensor_tensor", [sn_, f"pb{b}", "s_ysT"], ["s_t2", "s_ysT"], out=t2[:, rc, :], in0=Sb[:, rc, :], scalar=1.0,
                         in1=self.pb[b][:, SN:2 * SN], op0=ALU.mult, op1=ALU.mult, accum_out=ysT[:, 4 * u + rc, j:j + 1])
            yield
        for rc in range(4):
            ch = 4 * u + rc
            self.dve("scalar_tensor_tensor", ["xsT", "wcols", "s_ysT"], ["s_ysT"], out=ysT[:, ch, :], in0=self.xsT[:, ch, :], scalar=self.wcols[:, 6, ch:ch + 1],
                     in1=ysT[:, ch, :], op0=ALU.mult, op1=ALU.add)

    def smp_feat(self, u, a, wt, wn, name, func):
        P7 = ["pb7g0", "pb7g1", "pb7u0", "pb7u1"]
        g = u // 2
        SA = self.sarea
        ysT = SA[:, 1536:1792].rearrange("p (a b) -> p a b", b=NS)
        gt = self.av(self.OFF_MISC + 6400, [128, NS], F32)
        sq = self.av(self.OFF_MISC + 6464, [128, NS], F32)
        for oc in range(2):
            ch = 4 * u + 2 * a + oc
            col = 64 + 16 * oc
            for dc in range(NKC):
                self.mm([wn, "xnT"], P7, out=self.pb[7][:, col:col + NS], lhsT=wt[:, dc, oc * 128:(oc + 1) * 128], rhs=self.xnT[:, dc, T:TW],
                        start=(dc == 0), stop=(dc == NKC - 1))
            self.act(P7, ["s_gt"], out=gt[:], in_=self.pb[7][:, col:col + NS], func=func)
            self.dve("tensor_tensor", ["s_ysT", "s_gt"], ["s_ysT"], out=ysT[:, ch, :], in0=ysT[:, ch, :], in1=gt[:], op=ALU.mult)
            if name == "zS":
                self.dve("tensor_tensor", ["s_ysT"], ["s_sq"], out=sq[:], in0=ysT[:, ch, :], in1=ysT[:, ch, :], op=ALU.mult)
                b = self.bank()
                self.mm(["s_sq", "onesf"], [f"pb{b}"], out=self.pb[b][:NS, 0:2], lhsT=sq[:], rhs=self.onesf[:, 0:2], start=True, stop=True)
                self.dve("tensor_tensor", [f"pb{b}", "ssqS"], ["ssqS"], out=self.ssqS[:, g:g + 1], in0=self.ssqS[:, g:g + 1], in1=self.pb[b][:NS, 0:1], op=ALU.add)
        if name == "gbS" and a == 1 and u % 2 == 1:
            r = self.ssqS[:, 4 + g:5 + g]
            self.dve("tensor_scalar", ["ssqS"], ["ssqS"], out=r, in0=self.ssqS[:, g:g + 1], scalar1=1.0 / 1024, scalar2=EPS, op0=ALU.mult, op1=ALU.add)
            self.act(["ssqS"], ["ssqS"], out=r, in_=r, func=AF.Sqrt)
            self.dve("reciprocal", ["ssqS"], ["ssqS"], out=r, in_=r)
            dg = self.av(self.OFF_MISC + 6528, [NS, NS], F32)
            self.dve("tensor_scalar", ["identf", "ssqS"], ["s_dg"], out=dg[:], in0=self.identf[:NS, :NS], scalar1=r, scalar2=None, op0=ALU.mult)
            b = self.bank()
            self.mm(["onesf", "s_dg"], [f"pb{b}"], out=self.pb[b][:, 0:NS], lhsT=self.onesf[:NS, :], rhs=dg[:], start=True, stop=True)
            for ch in range(8 * g, 8 * g + 8):
                self.dve("tensor_tensor", ["s_ysT", f"pb{b}"], ["s_gt"], out=gt[:], in0=ysT[:, ch, :], in1=self.pb[b][:, 0:NS], op=ALU.mult)
                self.dve("scalar_tensor_tensor", ["s_gt", "wcols", "mergedT"], ["mergedT"], out=self.mergedT[:, ch, T:TW], in0=gt[:],
                         scalar=self.wcols[:, 4, ch:ch + 1], in1=self.mergedT[:, ch, T:TW], op0=ALU.mult, op1=ALU.add)


class Builder(MixMixin, Builder):
    pass


def _tile_w(w, cols=256):
    Kd, N = w.shape
    nb = N // cols
    return np.ascontiguousarray(w.reshape(Kd // 128, 128, nb, cols).transpose(2, 1, 0, 3))


def _tile_wd(w):
    out = np.zeros((24, 128, 8, 512), np.float32)
    for hf in range(2):
        f0 = 0 if hf == 0 else 24
        nfc = 24 if hf == 0 else 20
        for db in range(4):
            for kb in range(3):
                nk = min(8, nfc - kb * 8)
                fc0 = f0 + kb * 8
                blk = w[fc0 * 128:(fc0 + nk) * 128, db * 512:(db + 1) * 512].reshape(nk, 128, 512).transpose(1, 0, 2)
                out[hf * 12 + db * 3 + kb, :, :nk, :] = blk
    return out


def _tile_win(w):
    out = np.zeros((len(WIN_PLAN), 128, 16, 256), np.float32)
    for i, (tag, cols) in enumerate(WIN_PLAN):
        ok = cols >= 0
        blk = np.zeros((2048, 256), np.float32)
        blk[:, ok] = w[:, cols[ok]]
        out[i] = blk.reshape(16, 128, 256).transpose(1, 0, 2)
    return out


def _prep_common(inp):
    m = {}
    vec = np.zeros((8, D), np.float32)
    vec[0] = inp["ffn1_norm"][0]
    vec[1] = inp["mix_norm"][0]
    vec[2] = inp["ffn2_norm"][0]
    vec[3] = inp["ml_head_norm"][0]
    vec[4] = inp["ssm_norm"][0]
    vec[5] = inp["final_norm"]
    vec[6] = np.repeat(inp["ssm_D"][0], HP)
    m["vecs"] = vec
    sm = np.zeros((8, 32), np.float32)
    sm[0, 0:4] = inp["ml_i_bias"][0]
    sm[1, 0:4] = inp["ml_f_bias"][0]
    sm[2] = inp["ssm_dt_bias"][0]
    sm[3] = inp["ssm_A_log"][0]
    sm[4] = inp["ssm_D"][0]
    m["small"] = sm
    m["convw"] = np.concatenate([inp["ssm_conv_w"][0], inp["ssm_conv_b"]], 0).astype(np.float32)
    for i, pre in ((1, "ffn1"), (2, "ffn2")):
        m[f"wg{i}"] = _tile_w(inp[pre + "_w_gate"][0])
        m[f"wu{i}"] = _tile_w(inp[pre + "_w_up"][0])
        m[f"wd{i}"] = _tile_wd(inp[pre + "_w_down"][0])
    m["wo"] = _tile_w(inp["w_out"][0])
    m["win"] = _tile_win(inp["w_in"][0])
    return m


FULL = [False, False, True, True]


def kernel(**inp):
    inp = {k: np.asarray(v) for k, v in inp.items()}
    NT = len(FULL)
    npre = sum(1 for f in FULL if not f)
    nfull = NT - npre
    B = Builder({"NT": NT, "full": FULL})
    B.build()
    common = _prep_common(inp)
    xpr = inp["x_prompt"]
    xsm = inp["x_sample"][:, 0, :]
    in_maps = []
    for c in range(8):
        m = dict(common)
        b, half = c // 2, c % 2
        sl = slice(c * NS, (c + 1) * NS)
        own = xpr[b, half * nfull * T:(half + 1) * nfull * T]
        pre = xpr[b, 0:npre * T]
        m["xp"] = np.ascontiguousarray(np.concatenate([pre, own], 0))
        sm = common["small"].copy()
        sm[5, 0] = 1.0 if half == 1 else 0.0
        sm[5, 1] = 0.0 if half == 1 else NEG
        m["small"] = sm
        m["xs"] = np.ascontiguousarray(xsm[sl])
        m["st_conv"] = np.ascontiguousarray(inp["state_conv"][0, sl].reshape(NS * 3, XBC))
        m["st_C"] = np.ascontiguousarray(inp["state_mlstm_C"][0, sl])
        m["st_n"] = np.ascontiguousarray(inp["state_mlstm_n"][0, sl].reshape(NS, NH * DK))
        m["st_m"] = np.ascontiguousarray(inp["state_mlstm_m"][0, sl])
        m["st_S"] = np.ascontiguousarray(inp["state_ssm"][0, sl].reshape(NS, SH * HP, SN))
        in_maps.append({k: v for k, v in m.items() if k in B.dram})
    res = run_bass_kernel_spmd(B.nc, in_maps, core_ids=list(range(8))).results
    y_prompt = np.stack([np.concatenate([res[2 * b]["yp"], res[2 * b + 1]["yp"]], 0) for b in range(4)], 0)
    y_sample = np.concatenate([res[c]["ys"] for c in range(8)], 0)[:, None, :]
    last = [2 * b + 1 for b in range(4)]
    p_conv = np.stack([res[c]["o_pconv"] for c in last], 0)[None]
    p_C = np.stack([res[c]["o_pC"] for c in last], 0)[None]
    p_n = np.stack([res[c]["o_pn"].reshape(NH, DK) for c in last], 0)[None]
    p_m = np.stack([res[c]["o_pm"] for c in last], 0)[None]
    p_ssm = np.stack([res[c]["o_pS"].reshape(SH, HP, SN) for c in last], 0)[None]
    s_conv = np.concatenate([res[c]["o_sconv"] for c in range(8)], 0)[None]
    s_C = np.concatenate([res[c]["o_sC"] for c in range(8)], 0)[None]
    s_n = np.concatenate([res[c]["o_sn"].reshape(NS, NH, DK) for c in range(8)], 0)[None]
    s_m = np.concatenate([res[c]["o_sm"] for c in range(8)], 0)[None]
    s_ssm = np.concatenate([res[c]["o_sS"].reshape(NS, SH, HP, SN) for c in range(8)], 0)[None]
    outs = (y_prompt, y_sample, p_conv, p_C, p_n, p_m, p_ssm, s_conv, s_C, s_n, s_m, s_ssm)
    return tuple(np.ascontiguousarray(o, dtype=np.float32) for o in outs)
```

```python
import numpy as np
from contextlib import ExitStack
import concourse.bass as bass
import concourse.mybir as mybir
from concourse.alu_op_type import AluOpType as ALU
from concourse.bass_utils import run_bass_kernel_spmd

F32 = mybir.dt.float32
BF16 = mybir.dt.bfloat16
AF = mybir.ActivationFunctionType
AX = mybir.AxisListType

D = 2048
DFF = 5632
NKC = D // 128
NFC = DFF // 128
T = 512
NS = 16
TW = T + NS
EPS = 1e-6
IN_DIM = 14888
STRICT_WAR = False


class Buf:
    __slots__ = ("name", "w", "r")

    def __init__(self, name):
        self.name = name
        self.w = None
        self.r = {}


class Kern:
    ENGS = ("pe", "act", "dve", "pool", "sp")

    def __init__(self, nc, stack):
        self.nc = nc
        self.stack = stack
        self.ops = {e: [] for e in self.ENGS}
        self.seq = {e: 0 for e in self.ENGS}
        self.seen = {e: {} for e in self.ENGS}
        self.esem = {e: stack.enter_context(nc.semaphore("s_" + e)) for e in self.ENGS}
        self.dsems = {}
        self.dcount = {}
        self.bufs = {}
        self.waited = set()
        self.lastreal = {e: 0 for e in self.ENGS}

    def buf(self, name):
        b = self.bufs.get(name)
        if b is None:
            b = self.bufs[name] = Buf(name)
        return b

    def dma_sem(self, name):
        if name not in self.dsems:
            self.dsems[name] = self.stack.enter_context(self.nc.semaphore("d_" + name))
            self.dcount[name] = 0
        return name

    def _deps(self, eng, reads, writes, own=None):
        deps = {}

        def add(k, s):
            if deps.get(k, 0) < s:
                deps[k] = s
        for b in reads:
            b = self.buf(b) if isinstance(b, str) else b
            if b.w is not None:
                add(*b.w)
        for b in writes:
            b = self.buf(b) if isinstance(b, str) else b
            if b.w is not None:
                add(*b.w)
            for k, s in b.r.items():
                if k != eng or STRICT_WAR:
                    add(k, s)
        waits = []
        seen = self.seen[eng]
        for k, s in deps.items():
            if (k == eng and eng in ("pe", "sp")) or k == own:
                continue
            if seen.get(k, 0) < s:
                seen[k] = s
                waits.append((k, s))
                if k in self.ENGS:
                    self.waited.add((k, s))
        return waits

    def _commit(self, key, seq, eng, reads, writes):
        for b in reads:
            b = self.buf(b) if isinstance(b, str) else b
            if b.r.get(key, 0) < seq:
                b.r[key] = seq
        for b in writes:
            b = self.buf(b) if isinstance(b, str) else b
            b.w = (key, seq)
            b.r = {}

    def op(self, eng, fn, reads=(), writes=()):
        waits = self._deps(eng, reads, writes)
        self.seq[eng] += 1
        seq = self.seq[eng]
        self.ops[eng].append((waits, fn, None, seq))
        self.lastreal[eng] = seq
        self._commit(eng, seq, eng, reads, writes)

    def dma(self, eng, fn, sem, reads=(), writes=()):
        self.dma_sem(sem)
        waits = self._deps(eng, reads, writes, own="D:" + sem)
        self.dcount[sem] += 16
        val = self.dcount[sem]
        self.seq[eng] += 1
        self.ops[eng].append((waits, fn, sem, self.seq[eng]))
        key = "D:" + sem
        self._commit(key, val, eng, reads, writes)

    def barrier(self, engs):
        last = {e: self.lastreal[e] for e in engs if self.lastreal[e] > 0 and e != "sp"}
        dl = {"D:" + s: v for s, v in self.dcount.items() if v > 0 and not s.startswith("w")}
        for e in engs:
            waits = []
            seen = self.seen[e]
            for k, s in list(last.items()) + list(dl.items()):
                if k == e:
                    continue
                if seen.get(k, 0) < s:
                    seen[k] = s
                    waits.append((k, s))
                    if k in self.ENGS:
                        self.waited.add((k, s))
            if waits:
                self.seq[e] += 1
                self.ops[e].append((waits, None, None, self.seq[e]))

    def final_wait_all_dma(self, eng="sp"):
        waits = [("D:" + s, v) for s, v in self.dcount.items() if v > 0]
        self.seq[eng] += 1
        self.ops[eng].append((waits, None, None, self.seq[eng]))

    def emit(self):
        nc = self.nc
        handles = {"pe": "tensor", "act": "scalar", "dve": "vector", "pool": "gpsimd", "sp": "sync"}
        val = {}
        for e in self.ENGS:
            c = 0
            for (_, _, dsem, seq) in self.ops[e]:
                if (e, seq) in self.waited:
                    c += 1
                    val[(e, seq)] = c
        ops, esem, dsems, waited = self.ops, self.esem, self.dsems, self.waited

        def run(e):
            def body(h):
                for (waits, fn, dsem, seq) in ops[e]:
                    for (k, s) in waits:
                        if k.startswith("D:"):
                            h.wait_ge(dsems[k[2:]], s)
                        else:
                            h.wait_ge(esem[k], val[(k, s)])
                    if fn is None:
                        continue
                    ins = fn(h)
                    if dsem is not None:
                        ins.then_inc(dsems[dsem], 16)
                    elif (e, seq) in waited:
                        ins.then_inc(esem[e], 1)
            return body

        with nc.Block() as block:
            block.sync(run("sp"))
            block.gpsimd(run("pool"))
            block.tensor(run("pe"))
            block.scalar(run("act"))
            block.vector(run("dve"))


class Prog:
    def __init__(self, cfg):
        self.cfg = cfg
        self.nc = bass.Bass("TRN2", target_bir_lowering=False)
        self.stack = ExitStack()
        self.K = Kern(self.nc, self.stack)
        self.dram = {}

    def din(self, name, shape, dtype=F32):
        t = self.nc.dram_tensor(name, list(shape), dtype, kind="ExternalInput")
        self.dram[name] = t
        return t.ap()

    def dout(self, name, shape, dtype=F32):
        t = self.nc.dram_tensor(name, list(shape), dtype, kind="ExternalOutput")
        self.dram[name] = t
        return t.ap()

    def sb(self, name, shape, dtype):
        return self.stack.enter_context(self.nc.sbuf_tensor(name, list(shape), dtype))

    def ps(self, name, shape, dtype=F32):
        return self.stack.enter_context(self.nc.psum_tensor(name, list(shape), dtype))


DK = 256
DV = 512
NH = 4
SH = 32
HP = 64
SN = 128
XBC = 2560
NEG = -1e30
ARENA = 20480


def win_plan():
    oq, ok_, ov, oig, ofg, oog, oz, ox = 0, 1024, 2048, 4096, 4100, 4104, 6152, 8200
    oB, oC, odt, ogA, ogB = 10248, 10504, 10760, 10792, 12840
    pl = []
    ar = np.arange
    for g in range(2):
        pl.append((f"BC{g}", np.concatenate([oB + g * 128 + ar(128), oC + g * 128 + ar(128)])))
    sm = np.full(256, -1, np.int64)
    sm[0:4] = oig + ar(4); sm[4:8] = ofg + ar(4); sm[8:40] = odt + ar(32)
    pl.append(("SM", sm))
    for h in range(NH):
        pl.append((f"Q{h}", oq + h * 256 + ar(256)))
        pl.append((f"K{h}", ok_ + h * 256 + ar(256)))
        for a in range(2):
            pl.append((f"V{h}{a}", ov + h * 512 + a * 256 + ar(256)))
        for a in range(2):
            pl.append((f"OG{h}{a}", oog + h * 512 + a * 256 + ar(256)))
        for a in range(2):
            pl.append((f"GA{h}{a}", ogA + h * 512 + a * 256 + ar(256)))
    for u in range(4):
        for a in range(2):
            pl.append((f"X{u}{a}", ox + u * 512 + a * 256 + ar(256)))
        for a in range(2):
            pl.append((f"Z{u}{a}", oz + u * 512 + a * 256 + ar(256)))
        for a in range(2):
            pl.append((f"GB{u}{a}", ogB + u * 512 + a * 256 + ar(256)))
    return pl


WIN_PLAN = win_plan()
WIN_IDX = {t: i for i, (t, _) in enumerate(WIN_PLAN)}


class Builder(Prog):
    def __init__(self, cfg):
        super().__init__(cfg)
        self.NT = cfg.get("NT", 4)
        self.full = cfg.get("full", [True] * self.NT)
        self.bank_rr = 0
        self.wq = []
        self.wq_issued = 0
        self.wq_next = 0
        self.NSLOT = cfg.get("nslot", 4)
        self.uid = 0

    def _op(self, eng, meth, reads, writes, **kw):
        self.K.op(eng, lambda e: getattr(e, meth)(**kw), reads, writes)

    def dve(self, meth, reads, writes, **kw):
        self._op("dve", meth, reads, writes, **kw)

    def act(self, reads, writes, **kw):
        self._op("act", "activation", reads, writes, **kw)

    def mm(self, reads, writes, **kw):
        self._op("pe", "matmul", reads, writes, **kw)

    def tr(self, reads, writes, **kw):
        self._op("pe", "transpose", reads, writes, **kw)

    def dma(self, sem, reads, writes, out, in_, eng="sp", nc_ok=False):
        nc = self.nc
        if nc_ok:
            def f(e):
                with nc.allow_non_contiguous_dma(reason="small strided transfer"):
                    return e.dma_start(out=out, in_=in_)
        else:
            def f(e):
                return e.dma_start(out=out, in_=in_)
        self.K.dma(eng, f, sem, reads=reads, writes=writes)

    def av(self, off, shape, dtype, rows=None):
        n = int(np.prod(shape[1:]))
        e = n * (2 if dtype == F32 else 1)
        a = off // 2
        assert off % 4 == 0 and a + e <= ARENA, (off, shape, a + e)
        ap = self.arena[:shape[0], a:a + e]
        if dtype == F32:
            ap = ap.bitcast(F32)
        if len(shape) == 3:
            ap = ap.rearrange("p (a b) -> p a b", b=shape[2])
        elif len(shape) == 4:
            ap = ap.rearrange("p (a b c) -> p a b c", b=shape[2], c=shape[3])
        return ap

    def bank(self):
        b = self.bank_rr
        self.bank_rr = (b + 1) % getattr(self, "bank_mod", 7)
        return b

    def declare(self):
        NT = self.NT
        self.xp = self.din("xp", [NT * T, D])
        self.xs = self.din("xs", [NS, D])
        self.vecs = self.din("vecs", [8, D])
        self.small = self.din("small", [8, 32])
        self.convw = self.din("convw", [5, XBC])
        self.wg = [self.din(f"wg{i}", [22, 128, 16, 256]) for i in (1, 2)]
        self.wu = [self.din(f"wu{i}", [22, 128, 16, 256]) for i in (1, 2)]
        self.wd = [self.din(f"wd{i}", [24, 128, 8, 512]) for i in (1, 2)]
        self.wo = self.din("wo", [8, 128, 16, 256])
        self.win = self.din("win", [len(WIN_PLAN), 128, 16, 256])
        self.st_conv = self.din("st_conv", [NS * 3, XBC])
        self.st_C = self.din("st_C", [NS, NH, DK, DV])
        self.st_n = self.din("st_n", [NS, NH * DK])
        self.st_m = self.din("st_m", [NS, NH])
        self.st_S = self.din("st_S", [NS, SH * HP, SN])
        self.yp = self.dout("yp", [sum(1 for f in self.full if f) * T, D])
        self.ys = self.dout("ys", [NS, D])
        self.o_pconv = self.dout("o_pconv", [3, XBC])
        self.o_pC = self.dout("o_pC", [NH, DK, DV])
        self.o_pn = self.dout("o_pn", [NH * DK])
        self.o_pm = self.dout("o_pm", [NH])
        self.o_pS = self.dout("o_pS", [SH * HP, SN])
        self.o_sconv = self.dout("o_sconv", [NS, 3, XBC])
        self.o_sC = self.dout("o_sC", [NS, NH, DK, DV])
        self.o_sn = self.dout("o_sn", [NS, NH * DK])
        self.o_sm = self.dout("o_sm", [NS, NH])
        self.o_sS = self.dout("o_sS", [NS, SH * HP, SN])

    def alloc(self):
        sb = self.sb
        self.xres = sb("xres", [128, 5, D], F32)
        self.xnT = sb("xnT", [128, NKC, TW], BF16)
        self.arena = sb("arena", [128, ARENA], BF16)
        self.hT = self.arena[:, 0:24 * TW].rearrange("p (f t) -> p f t", t=TW)
        self.xsb = self.arena[:, 0:4 * D].rearrange("p (c d) -> p c d", d=D)
        self.xsbs = self.arena[:, 4 * D:5 * D]
        self.junk = self.arena[:, 5 * D:6 * D]
        self.fnrow = self.arena[:, 6 * D:8 * D].bitcast(F32)
        self.ybuf = self.arena[:, 0:2 * D].bitcast(F32)
        self.ws = [sb(f"ws{i}", [128, 4096], BF16) for i in range(self.NSLOT)]
        self.ident = sb("ident", [128, 128], BF16)
        self.identf = sb("identf", [128, 128], F32)
        self.wcols = sb("wcols", [128, 8, NKC], F32)
        self.stat = sb("stat", [128, 64], F32)
        self.sg = [self.arena[:, 12672 + i * 2 * TW:12672 + (i + 1) * 2 * TW].bitcast(F32) for i in range(2)]
        self.pb = [self.ps(f"pb{i}", [128, 512], F32) for i in range(8)]
        self.alloc_mix()

    def wplan(self, ap, kc, cols, tag):
        self.wq.append((ap, kc, cols, tag))

    def _issue_w(self, i):
        ap, kc, cols, tag = self.wq[i]
        s = i % self.NSLOT
        dst = self.ws[s][:, 0:kc * cols].rearrange("p (k c) -> p k c", c=cols)
        step = 2048 // cols
        for k0 in range(0, kc, step):
            k1 = min(kc, k0 + step)
            self.dma(f"w{s}", [], [f"ws{s}"], out=dst[:, k0:k1, :], in_=ap[:, k0:k1, :], eng="pool")

    def wget(self, tag, hold=1):
        i = self.wq_next
        assert self.wq[i][3] == tag, (self.wq[i][3], tag)
        self.wq_next += 1
        while self.wq_issued < min(len(self.wq), i - (hold - 1) + self.NSLOT):
            self._issue_w(self.wq_issued)
            self.wq_issued += 1
        s = i % self.NSLOT
        _, kc, cols, _ = self.wq[i]
        return self.ws[s][:, 0:kc * cols].rearrange("p (k c) -> p k c", c=cols), f"ws{s}"

    def consts(self):
        K = self.K
        self._op("pool", "memset", [], ["identf"], ap=self.identf[:], constant=0.0)
        self._op("pool", "affine_select", ["identf"], ["identf"], out=self.identf[:], in_=self.identf[:], pattern=[[-1, 128]],
                 compare_op=ALU.not_equal, fill=1.0, base=0, channel_multiplier=1)
        self.dve("tensor_copy", ["identf"], ["ident"], out=self.ident[:], in_=self.identf[:])
        self.dma("ld_c0", [], ["wcols"], out=self.wcols[:], in_=self.vecs.rearrange("v (c p) -> p v c", p=128), nc_ok=True)
        self.consts_mix()

    def rstd_of(self, tc, rows, col0=0):
        ss = self.stat[:rows, col0 + tc:col0 + tc + 1]
        rs = self.stat[:rows, col0 + 8 + tc:col0 + 9 + tc]
        xin = self.xres[:rows, tc, :]
        self.dve("memset", [], [f"ss{tc}"], ap=ss, constant=0.0)
        self.act([f"xres{tc}", f"ss{tc}"], ["junk", f"ss{tc}"], out=self.junk[:rows, :], in_=xin, func=AF.Square, accum_out=ss)
        self.dve("tensor_scalar", [f"ss{tc}"], [f"rs{tc}"], out=rs, in0=ss, scalar1=1.0 / D, scalar2=EPS, op0=ALU.mult, op1=ALU.add)
        self.act([f"rs{tc}"], [f"rs{tc}"], out=rs, in_=rs, func=AF.Sqrt)
        self.dve("reciprocal", [f"rs{tc}"], [f"rs{tc}"], out=rs, in_=rs)
        return rs

    def rms_to_T(self, vec_idx, with_samples):
        chunks = [0, 1, 2, 3] + ([4] if with_samples else [])
        for tc in chunks:
            rows = 128 if tc < 4 else NS
            rs = self.rstd_of(tc, rows)
            dst = self.xsb[:, tc, :] if tc < 4 else self.xsbs[:rows, :]
            self.act([f"xres{tc}", f"rs{tc}"], [f"xsb{tc}"], out=dst, in_=self.xres[:rows, tc, :], func=AF.Copy, scale=rs)
        w = TW if with_samples else T
        for dc in range(NKC):
            b = self.bank()
            pt = self.pb[b][:].bitcast(BF16)
            for tc in chunks:
                rows = 128 if tc < 4 else NS
                src = self.xsb[:, tc, dc * 128:(dc + 1) * 128] if tc < 4 else self.xsbs[:rows, dc * 128:(dc + 1) * 128]
                self.tr([f"xsb{tc}", "ident"], [f"pb{b}"], out=pt[:, tc * 128:tc * 128 + rows], in_=src, identity=self.ident[:rows, :rows])
            self.dve("tensor_scalar", [f"pb{b}", "wcols"], ["xnT"], out=self.xnT[:, dc, 0:w], in0=pt[:, 0:w],
                     scalar1=self.wcols[:, vec_idx, dc:dc + 1], scalar2=None, op0=ALU.mult)

    HALVES = ((0, 12), (12, 22))

    def plan_ffn(self, li):
        for hf, (b0, b1) in enumerate(self.HALVES):
            for fb in range(b0, b1):
                self.wplan(self.wg[li][fb], 16, 256, f"g{li}_{fb}")
                self.wplan(self.wu[li][fb], 16, 256, f"u{li}_{fb}")
            nfc = (b1 - b0) * 2
            for db in range(4):
                for kb in range(3):
                    nk = min(8, nfc - kb * 8)
                    self.wplan(self.wd[li][hf * 12 + db * 3 + kb], nk, 512, f"d{li}_{hf}_{db}_{kb}")

    def ffn(self, li, with_samples):
        P7 = ["pb7"]
        chunks = [0, 1, 2, 3] + ([4] if with_samples else [])
        for hf, (b0, b1) in enumerate(self.HALVES):
            nfc = (b1 - b0) * 2
            for fb in range(b0, b1):
                wg, wgn = self.wget(f"g{li}_{fb}")
                wu, wun = self.wget(f"u{li}_{fb}", hold=2)
                for j in range(2):
                    fc = (fb - b0) * 2 + j
                    bg, bu = self.bank(), self.bank()
                    par = fc % 2
                    for (wt, wn, b, col, sn) in ((wg, wgn, bg, par * 32, f"pb7g{par}"), (wu, wun, bu, par * 32 + 16, f"pb7u{par}")):
                        for dc in range(NKC):
                            self.mm([wn, "xnT"], [f"pb{b}"], out=self.pb[b][:, 0:T], lhsT=wt[:, dc, j * 128:(j + 1) * 128],
                                    rhs=self.xnT[:, dc, 0:T], start=(dc == 0), stop=(dc == NKC - 1))
                        if with_samples:
                            for dc in range(NKC):
                                self.mm([wn, "xnT"], [sn], out=self.pb[7][:, col:col + NS], lhsT=wt[:, dc, j * 128:(j + 1) * 128],
                                        rhs=self.xnT[:, dc, T:TW], start=(dc == 0), stop=(dc == NKC - 1))
                    sgi = fc % 2
                    sg = self.sg[sgi]
                    self.act([f"pb{bg}"], [f"sg{sgi}"], out=sg[:, 0:T], in_=self.pb[bg][:, 0:T], func=AF.Silu)
                    self.dve("tensor_tensor", [f"sg{sgi}", f"pb{bu}"], [f"hT{fc}"], out=self.hT[:, fc, 0:T], in0=sg[:, 0:T],
                             in1=self.pb[bu][:, 0:T], op=ALU.mult)
                    if with_samples:
                        self.act([f"pb7g{par}"], [f"sg{sgi}s"], out=sg[:, T:TW], in_=self.pb[7][:, par * 32:par * 32 + NS], func=AF.Silu)
                        self.dve("tensor_tensor", [f"sg{sgi}s", f"pb7u{par}"], [f"hT{fc}s"], out=self.hT[:, fc, T:TW], in0=sg[:, T:TW],
                                 in1=self.pb[7][:, par * 32 + 16:par * 32 + 16 + NS], op=ALU.mult)
            P7n = ["pb7g0", "pb7g1", "pb7u0", "pb7u1"]
            for db in range(4):
                accb = {tc: (self.bank() if tc < 4 else 7) for tc in chunks}
                for kb in range(3):
                    nk = min(8, nfc - kb * 8)
                    wd, wdn = self.wget(f"d{li}_{hf}_{db}_{kb}")
                    for tc in chunks:
                        rows = 128 if tc < 4 else NS
                        b = accb[tc]
                        for kc in range(nk):
                            fc = kb * 8 + kc
                            first = (fc == 0)
                            last = (fc == nfc - 1)
                            self.mm([wdn, f"hT{fc}" if tc < 4 else f"hT{fc}s"], [f"pb{b}"] if tc < 4 else P7n,
                                    out=self.pb[b][:rows, :], lhsT=self.hT[:, fc, tc * 128:tc * 128 + rows], rhs=wd[:, kc, :],
                                    start=first, stop=last)
                for tc in chunks:
                    rows = 128 if tc < 4 else NS
                    b = accb[tc]
                    xs = self.xres[:rows, tc, db * 512:(db + 1) * 512]
                    self.dve("scalar_tensor_tensor", ([f"pb{b}"] if tc < 4 else P7n) + [f"xres{tc}"], [f"xres{tc}"],
                             out=xs, in0=self.pb[b][:rows, :], scalar=0.5, in1=xs, op0=ALU.mult, op1=ALU.add)

    def final_store(self, ti, with_samples):
        chunks = [0, 1, 2, 3] + ([4] if with_samples else [])
        self.dma("ld_fn", [], ["fnrow"], out=self.fnrow, in_=self.vecs[5:6, :].to_broadcast([128, D]))
        for tc in chunks:
            rows = 128 if tc < 4 else NS
            rs = self.rstd_of(tc, rows)
            yb = self.ybuf
            self.dve("scalar_tensor_tensor", [f"xres{tc}", f"rs{tc}", "fnrow"], ["ybuf"], out=yb[:rows, :], in0=self.xres[:rows, tc, :],
                     scalar=rs, in1=self.fnrow[:rows, :], op0=ALU.mult, op1=ALU.mult)
            dst = self.yp[ti * T + tc * 128: ti * T + (tc + 1) * 128, :] if tc < 4 else self.ys[:, :]
            self.dma("st_y", ["ybuf"], [], out=dst, in_=yb[:rows, :])

    def load_tile(self, ti, with_samples):
        for tc in range(4):
            src = self.xp[ti * T + tc * 128: ti * T + (tc + 1) * 128, :]
            self.dma(f"ld_x{tc}", [], [f"xres{tc}"], out=self.xres[:, tc, :], in_=src)
        if with_samples:
            self.dma("ld_x4", [], ["xres4"], out=self.xres[:NS, 4, :], in_=self.xs[:, :])

    def barrier(self):
        self.K.barrier(("pe", "act", "dve", "sp"))

    def build(self):
        self.declare()
        self.alloc()
        stages = self.cfg.get("stages", ("ffn1", "mix", "ffn2"))
        for ti in range(self.NT):
            full = self.full[ti]
            if "ffn1" in stages:
                self.plan_ffn(0)
            if "mix" in stages:
                self.plan_mix(ti, full)
            if "ffn2" in stages and full:
                self.plan_ffn(1)
        self.consts()
        npre = sum(1 for f in self.full if not f)
        for ti in range(self.NT):
            full = self.full[ti]
            ws = (ti == self.NT - 1)
            self.load_tile(ti, ws)
            if "ffn1" in stages:
                self.rms_to_T(0, ws)
                self.ffn(0, ws)
                self.barrier()
            if "mix" in stages:
                self.mixer(ti, ws, full)
                self.barrier()
            if full:
                if "ffn2" in stages:
                    self.rms_to_T(2, ws)
                    self.ffn(1, ws)
                    self.barrier()
                self.final_store(ti - npre, ws)
                self.barrier()
            if (not full) and ti == npre - 1:
                self.reset_state()
                self.barrier()
        self.K.final_wait_all_dma()
        self.K.emit()
        return self


class MixMixin:
    STATE_TAGS = ("BC", "SM", "K", "V", "X")

    def alloc_mix(self):
        sb = self.sb
        self.Cst = sb("Cst", [128, NH, 2, DV], F32)
        self.nst = sb("nst", [128, NH * 2], F32)
        self.mcol = sb("mcol", [128, NH], F32)
        self.Sst = sb("Sst", [128, 4, 512], F32)
        self.convhist = sb("convhist", [128, 20, 3], F32)
        self.mergedT = sb("mergedT", [128, NKC, TW], BF16)
        self.yzgA = sb("yzgA", [128, 4, 512], BF16)
        self.Umat = sb("Umat", [128, 128], F32)
        self.Lsm = sb("Lsm", [128, 128], F32)
        self.E127 = sb("E127", [128, 128], F32)
        self.CB = sb("CB", [128, 128], F32)
        self.onesf = sb("onesf", [128, 128], F32)
        self.smallb = sb("smallb", [128, 8, 32], F32)
        self.Arow = sb("Arow", [128, 32], F32)
        self.convwc = sb("convwc", [128, 20, 5], F32)
        self.tsm = sb("tsm", [128, 4, 256], F32)
        self.ssq = sb("ssq", [128, 16], F32)
        self.BCT = sb("BCT", [128, 2, 2, T], BF16)
        self.Btok = sb("Btok", [128, 2, 4, 128], BF16)
        self.sarea = sb("sarea", [128, 2048], F32)
        self.alloc_smp()

    def consts_mix(self):
        ps_ = self._op
        for (t, nm, pat, cmp_, fill, base, cm, init) in (
                (self.Umat, "Umat", [[1, 128]], ALU.is_ge, 0.0, 0, -1, 1.0),
                (self.Lsm, "Lsm", [[-1, 128]], ALU.is_gt, 0.0, 0, 1, 1.0),
                (self.E127, "E127", [[0, 128]], ALU.is_equal, 0.0, -127, 1, 1.0),
                (self.CB, "CB", [[-1, 128]], ALU.is_ge, NEG, 0, 1, 0.0),
        ):
            ps_("pool", "memset", [], [nm], ap=t[:], constant=init)
            ps_("pool", "affine_select", [nm], [nm], out=t[:], in_=t[:], pattern=pat, compare_op=cmp_, fill=fill, base=base,
                channel_multiplier=cm)
        ps_("pool", "memset", [], ["onesf"], ap=self.onesf[:], constant=1.0)
        self.dma("ld_c1", [], ["smallb"], out=self.smallb[:].rearrange("p a b -> p (a b)"),
                 in_=self.small.rearrange("a b -> (a b)").unsqueeze(0).to_broadcast([128, 256]))
        for j in range(5):
            self.dma(f"ld_c{2 + j}", [], ["convwc"], out=self.convwc[:, :, j], in_=self.convw[j].rearrange("(c p) -> p c", p=128), nc_ok=True)
        self.act(["smallb"], ["Arow"], out=self.Arow[:], in_=self.smallb[:, 3, :], func=AF.Exp)
        self.dve("tensor_scalar", ["Arow"], ["Arow"], out=self.Arow[:], in0=self.Arow[:], scalar1=-1.0, scalar2=None, op0=ALU.mult)
        self.dve("memset", [], ["Cst"], ap=self.Cst[:].rearrange("p a b c -> p (a b c)"), constant=0.0)
        self.dve("memset", [], ["nst"], ap=self.nst[:], constant=0.0)
        self.dve("memset", [], ["mcol"], ap=self.mcol[:], constant=NEG)
        self.dve("memset", [], ["Sst"], ap=self.Sst[:].rearrange("p a b -> p (a b)"), constant=0.0)
        self.dve("memset", [], ["convhist"], ap=self.convhist[:].rearrange("p a b -> p (a b)"), constant=0.0)

    def reset_state(self):
        fl = self.smallb[:, 5, 0:1]
        for (t, nm, ap) in ((self.Cst, "Cst", self.Cst[:].rearrange("p a b c -> p (a b c)")), (self.nst, "nst", self.nst[:]),
                            (self.Sst, "Sst", self.Sst[:].rearrange("p a b -> p (a b)")),
                            (self.convhist, "convhist", self.convhist[:].rearrange("p a b -> p (a b)"))):
            self.dve("tensor_scalar", [nm, "smallb"], [nm], out=ap, in0=ap, scalar1=fl, scalar2=None, op0=ALU.mult)
        self.dve("tensor_scalar", ["mcol", "smallb"], ["mcol"], out=self.mcol[:], in0=self.mcol[:], scalar1=fl, scalar2=self.smallb[:, 5, 1:2],
                 op0=ALU.mult, op1=ALU.add)

    def plan_mix(self, ti, full):
        for i, (tag, _) in enumerate(WIN_PLAN):
            if full or tag.startswith(self.STATE_TAGS):
                self.wplan(self.win[i], 16, 256, f"in_{tag}")
        if full:
            for b in range(8):
                self.wplan(self.wo[b], 16, 256, f"wo_{b}")

    def proj_feat(self, wt, wn, oc, b, cols=T, c0=0, bankcols=None):
        bc = bankcols if bankcols is not None else (0, cols)
        for dc in range(NKC):
            self.mm([wn, "xnT"], [f"pb{b}"], out=self.pb[b][:, bc[0]:bc[0] + cols], lhsT=wt[:, dc, oc * 128:(oc + 1) * 128],
                    rhs=self.xnT[:, dc, c0:c0 + cols], start=(dc == 0), stop=(dc == NKC - 1))

    def proj_tok(self, wt, wn, tc, b, ncols=256, wc0=0, rows=128, bc0=0):
        t0 = tc * 128
        for dc in range(NKC):
            self.mm([wn, "xnT"], [f"pb{b}"], out=self.pb[b][:rows, bc0:bc0 + ncols], lhsT=self.xnT[:, dc, t0:t0 + rows],
                    rhs=wt[:, dc, wc0:wc0 + ncols], start=(dc == 0), stop=(dc == NKC - 1))

    def conv_silu(self, xpre, nm_pre, chunk, out_ap, nm_out, width=T):
        acc = self.av(self.OFF_CONVACC, [128, T], F32)
        w = self.convwc
        self.dve("tensor_scalar", [nm_pre, "convwc"], ["convacc"], out=acc[:, 0:width], in0=xpre[:, 0:width], scalar1=w[:, chunk, 0:1],
                 scalar2=None, op0=ALU.mult)
        for j in (1, 2, 3):
            self.dve("scalar_tensor_tensor", [nm_pre, "convwc", "convacc"], ["convacc"], out=acc[:, 0:width], in0=xpre[:, j:j + width],
                     scalar=w[:, chunk, j:j + 1], in1=acc[:, 0:width], op0=ALU.mult, op1=ALU.add)
        self.act(["convacc", "convwc"], [nm_out], out=out_ap, in_=acc[:, 0:width], func=AF.Silu, bias=w[:, chunk, 4:5], scale=1.0)

    OFF_CONVACC = 0
    OFF_XPRE = 2048
    OFF_BUFY = 10304
    OFF_XDT = 18496
    OFF_LR = 22592
    OFF_M = 26688
    OFF_YT = 28736
    OFF_YI = 30784
    OFF_MISC = 32832

    def mixer(self, ti, ws, full):
        if ws and full:
            self.smp_init()
        self.rms_to_T(1, ws)
        self.mix_bc(ti, ws, full)
        self.mix_small(ti, ws, full)
        self.barrier()
        self.mix_mlstm_all(ti, ws, full)
        self.barrier()
        for u in range(4):
            self.mix_ssd(ti, u, ws, full)
        if full:
            self.mix_out(ti, ws)
        if ti == self.NT - 1:
            self.store_pstates()

    def mix_bc(self, ti, ws, full):
        for g in range(2):
            wt, wn = self.wget("in_BC%d" % g)
            for oc in range(2):
                chunk = 16 + g + 2 * oc
                b = self.bank()
                self.proj_feat(wt, wn, oc, b)
                xpre = self.av(self.OFF_XPRE, [128, T + 3], F32)
                self.dve("tensor_copy", ["convhist"], ["xpre"], out=xpre[:, 0:3], in_=self.convhist[:, chunk, :])
                self.act([f"pb{b}"], ["xpre"], out=xpre[:, 3:T + 3], in_=self.pb[b][:, 0:T], func=AF.Copy)
                self.dve("tensor_copy", ["xpre"], ["convhist"], out=self.convhist[:, chunk, :], in_=xpre[:, T:T + 3])
                if oc == 0 or full:
                    self.conv_silu(xpre, "xpre", chunk, self.BCT[:, g, oc, :], f"BCT{g}{oc}")
            if ws and full:
                self.smp_bc(g, wt, wn)
            b = self.bank()
            pt = self.pb[b][:].bitcast(BF16)
            for tc in range(4):
                self.tr([f"BCT{g}0", "ident"], [f"pb{b}"], out=pt[:, tc * 128:(tc + 1) * 128], in_=self.BCT[:, g, 0, tc * 128:(tc + 1) * 128],
                        identity=self.ident[:])
            self.dve("tensor_copy", [f"pb{b}"], [f"Btok{g}"], out=self.Btok[:, g, :, :].rearrange("p a b -> p (a b)"), in_=pt[:, 0:T])

    def mix_small(self, ti, ws, full):
        wt, wn = self.wget("in_SM")
        sm = self.tsm
        for tc in range(4):
            b = self.bank()
            self.proj_tok(wt, wn, tc, b, ncols=40)
            self.dve("tensor_copy", [f"pb{b}"], ["tsm"], out=sm[:, tc, 0:40], in_=self.pb[b][:, 0:40])
        if ws and full:
            self.smp_small(wt, wn)
        R, Wt = ["tsm", "smallb", "Arow"], ["tsm"]
        bi = self.smallb[:, 0, 0:4].unsqueeze(1).to_broadcast([128, 4, 4])
        bf = self.smallb[:, 1, 0:4].unsqueeze(1).to_broadcast([128, 4, 4])
        bdt = self.smallb[:, 2, :].unsqueeze(1).to_broadcast([128, 4, 32])
        Ab = self.Arow[:].unsqueeze(1).to_broadcast([128, 4, 32])
        self.dve("tensor_tensor", R, Wt, out=sm[:, :, 40:44], in0=sm[:, :, 0:4], in1=bi, op=ALU.add)
        z = sm[:, :, 184:188]
        self.dve("tensor_tensor", R, Wt, out=z, in0=sm[:, :, 4:8], in1=bf, op=ALU.add)
        self.softplus_neg(z, sm[:, :, 44:48], sm[:, :, 188:192], neg=True)
        z2 = sm[:, :, 188:220]
        self.dve("tensor_tensor", R, Wt, out=z2, in0=sm[:, :, 8:40], in1=bdt, op=ALU.add)
        self.softplus_neg(z2, sm[:, :, 56:88], sm[:, :, 220:252], neg=False)
        self.dve("tensor_tensor", R, Wt, out=sm[:, :, 88:120], in0=sm[:, :, 56:88], in1=Ab, op=ALU.mult)
        for tc in range(4):
            b = self.bank()
            self.mm(["tsm", "Umat"], [f"pb{b}"], out=self.pb[b][:, 0:4], lhsT=self.Umat[:], rhs=sm[:, tc, 44:48], start=True, stop=True)
            self.mm(["tsm", "Umat"], [f"pb{b}"], out=self.pb[b][:, 32:64], lhsT=self.Umat[:], rhs=sm[:, tc, 88:120], start=True, stop=True)
            self.dve("tensor_copy", [f"pb{b}"], ["tsm"], out=sm[:, tc, 48:52], in_=self.pb[b][:, 0:4])
            self.dve("tensor_copy", [f"pb{b}"], ["tsm"], out=sm[:, tc, 120:152], in_=self.pb[b][:, 32:64])
            b2 = self.bank()
            self.mm(["tsm", "E127"], [f"pb{b2}"], out=self.pb[b2][:, 0:32], lhsT=self.E127[:], rhs=sm[:, tc, 120:152], start=True, stop=True)
            self.dve("tensor_copy", [f"pb{b2}"], ["tsm"], out=sm[:, tc, 152:184], in_=self.pb[b2][:, 0:32])
        self.dve("tensor_tensor", ["tsm"], ["tsm"], out=sm[:, :, 52:56], in0=sm[:, :, 40:44], in1=sm[:, :, 48:52], op=ALU.subtract)
        self.act(["tsm"], ["tsm"], out=sm[:, :, 220:252], in_=sm[:, :, 120:152], func=AF.Exp)

    def softplus_neg(self, z, out, tmp, neg):
        R, Wt = ["tsm"], ["tsm"]
        self.dve("scalar_tensor_tensor", R, Wt, out=tmp, in0=z, scalar=-1.0, in1=z, op0=ALU.mult, op1=ALU.max)
        self.act(R, Wt, out=tmp, in_=tmp, func=AF.Exp, scale=-1.0)
        self.act(R, Wt, out=tmp, in_=tmp, func=AF.Ln, bias=self.onesf[:, 0:1], scale=1.0)
        if neg:
            self.dve("scalar_tensor_tensor", R, Wt, out=out, in0=z, scalar=0.0, in1=tmp, op0=ALU.min, op1=ALU.subtract)
        else:
            self.dve("scalar_tensor_tensor", R, Wt, out=out, in0=z, scalar=0.0, in1=tmp, op0=ALU.max, op1=ALU.add)

    ML_SET = 15360
    ML_CH = 30720

    def ml_views(self, p):
        av, o = self.av, p * self.ML_SET
        v = {"p": p}
        v["qT"] = av(o, [128, 2, T], BF16)
        v["kT"] = av(o + 2048, [128, 2, T], BF16)
        v["ktok"] = av(o + 4096, [128, 4, DK], BF16)
        v["vtok"] = av(o + 6144, [128, 4, DV], BF16)
        v["gateA"] = av(o + 10240, [128, 4, DV], BF16)
        v["sgt"] = av(o + 14336, [128, 256], F32)
        c0 = self.ML_CH
        v["Cbf"] = av(c0, [128, 2, DV], BF16)
        v["nbf"] = av(c0 + 2048, [128, 2], BF16)
        v["diagA"] = av(c0 + 2112, [128, 128], F32)
        v["Rm"] = av(c0 + 2624, [128, 128], F32)
        v["wI"] = av(c0 + 3136, [128, 128], F32)
        v["sw"] = av(c0 + 3648, [128, 128], BF16)
        v["swT"] = av(c0 + 3904, [128, 128], BF16)
        v["vw"] = av(c0 + 4160, [128, DV], BF16)
        v["hA"] = av(c0 + 5184, [128, DV], F32)
        v["yab"] = av(c0 + 7232, [128, DV], BF16)
        v["c"] = av(c0 + 8256, [128, 32], F32)
        v["wendbf"] = av(c0 + 8384, [128, 2], BF16)
        return v

    def ml_proj_gen(self, h, ws, full, p):
        v = self.ml_views(p)
        qT, kT, ktok, vtok, gateA, sgt = v["qT"], v["kT"], v["ktok"], v["vtok"], v["gateA"], v["sgt"]
        N = lambda s_: f"ml_{s_}{p}"
        if full:
            wt, wn = self.wget(f"in_Q{h}")
            for oc in range(2):
                b = self.bank()
                self.proj_feat(wt, wn, oc, b)
                self.act([f"pb{b}"], [N("qT")], out=qT[:, oc, :], in_=self.pb[b][:, 0:T], func=AF.Copy, scale=DK ** -0.5)
                yield
            if ws:
                self.smp_proj_tok(wt, wn, "qS", scale=DK ** -0.5)
        wt, wn = self.wget(f"in_K{h}")
        if full:
            for oc in range(2):
                b = self.bank()
                self.proj_feat(wt, wn, oc, b)
                self.act([f"pb{b}"], [N("kT")], out=kT[:, oc, :], in_=self.pb[b][:, 0:T], func=AF.Copy)
                yield
        for tc in range(4):
            b = self.bank()
            self.proj_tok(wt, wn, tc, b)
            self.dve("tensor_copy", [f"pb{b}"], [N("ktok")], out=ktok[:, tc, :], in_=self.pb[b][:, 0:256])
            yield
        if ws and full:
            self.smp_proj_tok(wt, wn, "kS")
        for a in range(2):
            wt, wn = self.wget(f"in_V{h}{a}")
            for tc in range(4):
                b = self.bank()
                self.proj_tok(wt, wn, tc, b)
                self.act([f"pb{b}"], [N("vtok")], out=vtok[:, tc, a * 256:(a + 1) * 256], in_=self.pb[b][:, 0:256], func=AF.Copy)
                yield
            if ws and full:
                self.smp_proj_tok(wt, wn, "vS", c0=a * 256)
        if full:
            for a in range(2):
                wt, wn = self.wget(f"in_OG{h}{a}")
                for tc in range(4):
                    b = self.bank()
                    self.proj_tok(wt, wn, tc, b)
                    self.act([f"pb{b}"], [N("gate")], out=gateA[:, tc, a * 256:(a + 1) * 256], in_=self.pb[b][:, 0:256], func=AF.Sigmoid)
                    yield
                if ws:
                    self.smp_proj_tok(wt, wn, "gS", c0=a * 256, func=AF.Sigmoid)
            for a in range(2):
                wt, wn = self.wget(f"in_GA{h}{a}")
                for tc in range(4):
                    b = self.bank()
                    self.proj_tok(wt, wn, tc, b)
                    self.act([f"pb{b}"], [N("sgt")], out=sgt[:], in_=self.pb[b][:, 0:256], func=AF.Sigmoid)
                    self.dve("tensor_tensor", [N("sgt"), N("gate")], [N("gate")], out=gateA[:, tc, a * 256:(a + 1) * 256],
                             in0=gateA[:, tc, a * 256:(a + 1) * 256], in1=sgt[:], op=ALU.mult)
                    yield
                if ws:
                    self.smp_proj_tok(wt, wn, "gS", c0=a * 256, func=AF.Sigmoid, mul=True)

    def ml_chunk_gen(self, h, ws, full, p):
        v = self.ml_views(p)
        qT, kT, ktok, vtok, gateA = v["qT"], v["kT"], v["ktok"], v["vtok"], v["gateA"]
        Cbf, nbf, diagA, Rm, wI, sw, swT, vw, hA, yab, c, wendbf = (v[k] for k in
                                                                    ("Cbf", "nbf", "diagA", "Rm", "wI", "sw", "swT", "vw", "hA", "yab", "c", "wendbf"))
        N = lambda s_: f"ml_{s_}{p}"
        sm = self.tsm
        if full:
            self.act(["Cst"], ["ml_Cbf"], out=Cbf[:].rearrange("p a b -> p (a b)"), in_=self.Cst[:, h, :, :].rearrange("p a b -> p (a b)"), func=AF.Copy)
            self.dve("tensor_copy", ["nst"], ["ml_nbf"], out=nbf[:], in_=self.nst[:, 2 * h:2 * h + 2])
        mprev = self.mcol[:, h:h + 1]
        C = ["ml_c"]
        for tc in range(4):
            ts_ = slice(tc * 128, (tc + 1) * 128)
            a_ = sm[:, tc, 52 + h:53 + h]
            bcol = sm[:, tc, 48 + h:49 + h]
            self.dve("tensor_scalar", ["identf", "tsm"], ["ml_diag"], out=diagA[:], in0=self.identf[:], scalar1=a_, scalar2=None, op0=ALU.mult)
            b = self.bank()
            self.mm(["onesf", "ml_diag"], [f"pb{b}"], out=self.pb[b][:, 0:128], lhsT=self.onesf[:], rhs=diagA[:], start=True, stop=True)
            self.dve("tensor_tensor", [f"pb{b}", "CB"], ["ml_Rm"], out=Rm[:], in0=self.pb[b][:, 0:128], in1=self.CB[:], op=ALU.add)
            self.dve("tensor_reduce", ["ml_Rm"], C, out=c[:, 0:1], in_=Rm[:], axis=AX.X, op=ALU.max)
            self.dve("tensor_tensor", C + ["mcol"], C, out=c[:, 1:2], in0=c[:, 0:1], in1=mprev, op=ALU.max)
            self.dve("tensor_scalar", C, C, out=c[:, 2:3], in0=c[:, 1:2], scalar1=-1.0, scalar2=None, op0=ALU.mult)
            yield
            if full:
                self.act(["ml_Rm"] + C, ["ml_wI"], out=wI[:], in_=Rm[:], func=AF.Exp, bias=c[:, 2:3], scale=1.0)
                b = self.bank()
                for kc in range(2):
                    self.mm([N("qT"), N("kT")], [f"pb{b}"], out=self.pb[b][:, 0:128], lhsT=qT[:, kc, ts_], rhs=kT[:, kc, ts_],
                            start=(kc == 0), stop=(kc == 1))
                self.dve("memset", [], C, ap=c[:, 3:4], constant=0.0)
                self.dve("scalar_tensor_tensor", [f"pb{b}", "ml_wI"] + C, ["ml_sw"] + C, out=sw[:], in0=self.pb[b][:, 0:128], scalar=1.0,
                         in1=wI[:], op0=ALU.mult, op1=ALU.mult, accum_out=c[:, 3:4])
                yield
                b = self.bank()
                pt = self.pb[b][:].bitcast(BF16)
                self.tr(["ml_sw", "ident"], [f"pb{b}"], out=pt[:, 0:128], in_=sw[:], identity=self.ident[:])
                self.dve("tensor_copy", [f"pb{b}"], ["ml_swT"], out=swT[:], in_=pt[:, 0:128])
                bA, bB, bC = self.bank(), self.bank(), self.bank()
                for kc in range(2):
                    self.mm([N("qT"), "ml_Cbf"], [f"pb{bB}"], out=self.pb[bB][:, :], lhsT=qT[:, kc, ts_], rhs=Cbf[:, kc, :],
                            start=(kc == 0), stop=(kc == 1))
                for kc in range(2):
                    self.mm([N("qT"), "ml_nbf"], [f"pb{bC}"], out=self.pb[bC][:, 0:1], lhsT=qT[:, kc, ts_], rhs=nbf[:, kc:kc + 1],
                            start=(kc == 0), stop=(kc == 1))
                yield
                self.mm(["ml_swT", N("vtok")], [f"pb{bA}"], out=self.pb[bA][:, :], lhsT=swT[:], rhs=vtok[:, tc, :], start=True, stop=True)
                self.act(C + ["mcol"], C, out=c[:, 4:5], in_=mprev, func=AF.Exp, bias=c[:, 2:3], scale=1.0)
                self.dve("scalar_tensor_tensor", [f"pb{bC}"] + C, C, out=c[:, 5:6], in0=self.pb[bC][:, 0:1], scalar=c[:, 4:5], in1=c[:, 3:4],
                         op0=ALU.mult, op1=ALU.add)
                self.dve("tensor_tensor", C + ["tsm"], C, out=c[:, 6:7], in0=bcol, in1=c[:, 1:2], op=ALU.add)
                self.act(C, C, out=c[:, 7:8], in_=c[:, 6:7], func=AF.Exp, scale=-1.0)
                self.dve("scalar_tensor_tensor", C, C, out=c[:, 8:9], in0=c[:, 5:6], scalar=-1.0, in1=c[:, 5:6], op0=ALU.mult, op1=ALU.max)
                self.dve("tensor_tensor", C, C, out=c[:, 8:9], in0=c[:, 8:9], in1=c[:, 7:8], op=ALU.max)
                self.dve("reciprocal", C, C, out=c[:, 9:10], in_=c[:, 8:9])
                self.act([f"pb{bA}"], ["ml_hA"], out=hA[:], in_=self.pb[bA][:, :], func=AF.Copy)
                self.dve("scalar_tensor_tensor", [f"pb{bB}", "ml_hA"] + C, ["ml_hA"], out=hA[:], in0=self.pb[bB][:, :], scalar=c[:, 4:5],
                         in1=hA[:], op0=ALU.mult, op1=ALU.add)
                yield
                self.dve("memset", [], C, ap=c[:, 10:11], constant=0.0)
                self.act(["ml_hA"] + C, ["ml_yab"] + C, out=yab[:], in_=hA[:], func=AF.Square, scale=c[:, 9:10], accum_out=c[:, 10:11])
                self.dve("tensor_scalar", C, C, out=c[:, 11:12], in0=c[:, 10:11], scalar1=1.0 / DV, scalar2=EPS, op0=ALU.mult, op1=ALU.add)
                self.act(C, C, out=c[:, 11:12], in_=c[:, 11:12], func=AF.Sqrt)
                self.dve("reciprocal", C, C, out=c[:, 11:12], in_=c[:, 11:12])
                self.dve("tensor_tensor", C, C, out=c[:, 12:13], in0=c[:, 11:12], in1=c[:, 9:10], op=ALU.mult)
                self.dve("scalar_tensor_tensor", ["ml_hA", N("gate")] + C, ["ml_yab"], out=yab[:], in0=hA[:], scalar=c[:, 12:13],
                         in1=gateA[:, tc, :], op0=ALU.mult, op1=ALU.mult)
                yield
                b = self.bank()
                pt = self.pb[b][:].bitcast(BF16)
                for j in range(4):
                    self.tr(["ml_yab", "ident"], [f"pb{b}"], out=pt[:, j * 128:(j + 1) * 128], in_=yab[:, j * 128:(j + 1) * 128],
                            identity=self.ident[:])
                for j in range(4):
                    self.dve("tensor_scalar", [f"pb{b}", "wcols"], ["mergedT"], out=self.mergedT[:, h * 4 + j, ts_], in0=pt[:, j * 128:(j + 1) * 128],
                             scalar1=self.wcols[:, 3, h * 4 + j:h * 4 + j + 1], scalar2=None, op0=ALU.mult)
            b = self.bank()
            self.mm(["E127"] + C, [f"pb{b}"], out=self.pb[b][:, 0:2], lhsT=self.E127[:], rhs=c[:, 1:3], start=True, stop=True)
            self.mm(["E127", "tsm"], [f"pb{b}"], out=self.pb[b][:, 2:4], lhsT=self.E127[:], rhs=sm[:, tc, 48 + h:50 + h], start=True, stop=True)
            self.dve("tensor_copy", [f"pb{b}"], C, out=c[:, 14:18], in_=self.pb[b][:, 0:4])
            self.act(C + ["tsm"], C, out=c[:, 17:18], in_=a_, func=AF.Exp, bias=c[:, 15:16], scale=1.0)
            self.act(C + ["mcol"], C, out=c[:, 18:19], in_=mprev, func=AF.Exp, bias=c[:, 15:16], scale=1.0)
            self.dve("tensor_scalar", [N("vtok")] + C, ["ml_vw"], out=vw[:], in0=vtok[:, tc, :], scalar1=c[:, 17:18], scalar2=None, op0=ALU.mult)
            self.dve("tensor_copy", C, ["ml_wendbf"], out=wendbf[:, 0:1], in_=c[:, 17:18])
            yield
            for kc in range(2):
                b = self.bank()
                self.mm([N("ktok"), "ml_vw"], [f"pb{b}"], out=self.pb[b][:, :], lhsT=ktok[:, tc, kc * 128:(kc + 1) * 128], rhs=vw[:],
                        start=True, stop=True)
                self.dve("scalar_tensor_tensor", [f"pb{b}", "Cst"] + C, ["Cst"], out=self.Cst[:, h, kc, :], in0=self.Cst[:, h, kc, :],
                         scalar=c[:, 18:19], in1=self.pb[b][:, :], op0=ALU.mult, op1=ALU.add)
            b = self.bank()
            for kc in range(2):
                self.mm([N("ktok"), "ml_wendbf"], [f"pb{b}"], out=self.pb[b][:, 8 * kc:8 * kc + 1], lhsT=ktok[:, tc, kc * 128:(kc + 1) * 128],
                        rhs=wendbf[:, 0:1], start=True, stop=True)
            for kc in range(2):
                self.dve("scalar_tensor_tensor", [f"pb{b}", "nst"] + C, ["nst"], out=self.nst[:, 2 * h + kc:2 * h + kc + 1],
                         in0=self.nst[:, 2 * h + kc:2 * h + kc + 1], scalar=c[:, 18:19], in1=self.pb[b][:, 8 * kc:8 * kc + 1],
                         op0=ALU.mult, op1=ALU.add)
            self.dve("tensor_tensor", C, ["mcol"], out=self.mcol[:, h:h + 1], in0=c[:, 14:15], in1=c[:, 16:17], op=ALU.add)
            if full and tc < 3:
                self.act(["Cst"], ["ml_Cbf"], out=Cbf[:].rearrange("p a b -> p (a b)"), in_=self.Cst[:, h, :, :].rearrange("p a b -> p (a b)"),
                         func=AF.Copy)
                self.dve("tensor_copy", ["nst"], ["ml_nbf"], out=nbf[:], in_=self.nst[:, 2 * h:2 * h + 2])
            yield

    @staticmethod
    def interleave(main, filler, ratio=2):
        fdone = filler is None
        for _ in main:
            for _k in range(ratio):
                if not fdone:
                    try:
                        next(filler)
                    except StopIteration:
                        fdone = True
        if not fdone:
            for _ in filler:
                pass

    def mix_mlstm_all(self, ti, ws, full):
        if ws and full:
            for h in range(NH):
                for _ in self.ml_proj_gen(h, ws, full, 0):
                    pass
                self.interleave(self.ml_chunk_gen(h, ws, full, 0), self.smp_mlstm(h), ratio=1)
            return
        for _ in self.ml_proj_gen(0, ws, full, 0):
            pass
        for h in range(NH):
            nxt = self.ml_proj_gen(h + 1, ws, full, (h + 1) % 2) if h + 1 < NH else None
            self.interleave(self.ml_chunk_gen(h, ws, full, h % 2), nxt, ratio=self.cfg.get("ml_ratio", 1))

    def mix_ssd(self, ti, u, ws, full):
        av = self.av
        g = u // 2
        sm = self.tsm
        xpre = av(self.OFF_XPRE, [128, 4, T + 3], F32)
        xtok = av(self.OFF_XPRE, [128, 4, T], F32)
        bufY = av(self.OFF_BUFY, [128, 4, T], F32)
        xdt = av(self.OFF_XDT, [128, 4, T], BF16)
        Lr = av(self.OFF_LR, [128, 8, 128], F32)
        M = av(self.OFF_M, [128, 8, 128], BF16)
        yt = av(self.OFF_YT, [128, T], F32)
        yi = av(self.OFF_YI, [128, T], F32)
        STbf = av(self.OFF_MISC, [128, T], BF16)
        wend = av(self.OFF_MISC + 1024, [128, 8], F32)
        decb = av(self.OFF_MISC + 1056, [128, 8], F32)
        tcol = av(self.OFF_MISC + 1088, [128, 8], F32)
        xw = av(self.OFF_MISC + 1216, [128, T], BF16)
        zt = av(self.OFF_MISC + 2240, [128, 256], F32)
        BX, BY = "ssd_bufX", "ssd_bufY"
        for a in range(2):
            wt, wn = self.wget(f"in_X{u}{a}")
            for oc in range(2):
                j = 2 * a + oc
                chunk = 4 * u + j
                b = self.bank()
                self.proj_feat(wt, wn, oc, b)
                self.dve("tensor_copy", ["convhist"], [BX], out=xpre[:, j, 0:3], in_=self.convhist[:, chunk, :])
                self.act([f"pb{b}"], [BX], out=xpre[:, j, 3:T + 3], in_=self.pb[b][:, 0:T], func=AF.Copy)
                self.dve("tensor_copy", [BX], ["convhist"], out=self.convhist[:, chunk, :], in_=xpre[:, j, T:T + 3])
                self.conv_silu(xpre[:, j, :], BX, chunk, bufY[:, j, :], BY)
            if ws and full:
                self.smp_x(u, a, wt, wn)
        for tc in range(4):
            b = self.bank()
            for j in range(4):
                self.tr([BY, "identf"], [f"pb{b}"], out=self.pb[b][:, j * 128:(j + 1) * 128], in_=bufY[:, j, tc * 128:(tc + 1) * 128],
                        identity=self.identf[:])
            self.act([f"pb{b}"], [BX], out=xtok[:, tc, :], in_=self.pb[b][:, :], func=AF.Copy)
        dtb = sm[:, :, 56 + 8 * u:64 + 8 * u].unsqueeze(3).to_broadcast([128, 4, 8, HP])
        self.dve("tensor_tensor", [BX, "tsm"], ["ssd_xdt"], out=xdt.rearrange("p a (h q) -> p a h q", q=HP),
                 in0=xtok.rearrange("p a (h q) -> p a h q", q=HP), in1=dtb, op=ALU.mult)
        self.cbTm = av(self.OFF_CONVACC, [128, 4, 128], F32)
        if full:
            for tc in range(4):
                ts_ = slice(tc * 128, (tc + 1) * 128)
                b = self.bank()
                self.mm([f"BCT{g}0", f"BCT{g}1"], [f"pb{b}"], out=self.pb[b][:, 0:128], lhsT=self.BCT[:, g, 0, ts_], rhs=self.BCT[:, g, 1, ts_],
                        start=True, stop=True)
                self.dve("tensor_tensor", [f"pb{b}", "Umat"], ["convacc"], out=self.cbTm[:, tc, :], in0=self.pb[b][:, 0:128], in1=self.Umat[:], op=ALU.mult)
        if full and u % 2 == 0:
            self.dve("memset", [], ["ssq"], ap=self.ssq[:, 0:8], constant=0.0)
        if full:
            self.act(["Sst"], ["ssd_STbf"], out=STbf[:], in_=self.Sst[:, u, :], func=AF.Copy)
        self.interleave(self.ssd_chunk_gen(u, full, locals()), self.smp_ssd(u) if (ws and full) else None, ratio=self.cfg.get("ssd_ratio", 2))
        if not full:
            return
        self.ssd_gates(u, ws, locals())

    def ssd_chunk_gen(self, u, full, L):
        g, sm, BX, BY = L["g"], L["sm"], L["BX"], L["BY"]
        Lr, M, yt, yi, STbf, wend, decb, xw, xdt, xtok, bufY = (L[k] for k in ("Lr", "M", "yt", "yi", "STbf", "wend", "decb", "xw", "xdt", "xtok", "bufY"))
        for tc in range(4):
            ts_ = slice(tc * 128, (tc + 1) * 128)
            if full:
                self.dve("tensor_tensor", ["Umat", "tsm"], ["ssd_Lr"], out=Lr[:], in0=self.Umat[:].unsqueeze(1).to_broadcast([128, 8, 128]),
                         in1=sm[:, tc, 88 + 8 * u:96 + 8 * u].unsqueeze(2).to_broadcast([128, 8, 128]), op=ALU.mult)
                for q in range(2):
                    b = self.bank()
                    self.mm(["Lsm", "ssd_Lr"], [f"pb{b}"], out=self.pb[b][:, :], lhsT=self.Lsm[:],
                            rhs=Lr[:, 4 * q:4 * q + 4, :].rearrange("p a b -> p (a b)"), start=True, stop=True)
                    self.act([f"pb{b}"], ["ssd_M"], out=M[:, 4 * q:4 * q + 4, :].rearrange("p a b -> p (a b)"), in_=self.pb[b][:, :], func=AF.Exp)
                self.dve("tensor_tensor", ["ssd_M", "convacc"], ["ssd_M"], out=M[:], in0=M[:],
                         in1=self.cbTm[:, tc, :].unsqueeze(1).to_broadcast([128, 8, 128]), op=ALU.mult)
                yield
                bI, bE = self.bank(), self.bank()
                for hh in range(8):
                    self.mm(["ssd_M", "ssd_xdt"], [f"pb{bI}"], out=self.pb[bI][:, hh * HP:(hh + 1) * HP], lhsT=M[:, hh, :],
                            rhs=xdt[:, tc, hh * HP:(hh + 1) * HP], start=True, stop=True)
                self.mm([f"BCT{g}1", "ssd_STbf"], [f"pb{bE}"], out=self.pb[bE][:, :], lhsT=self.BCT[:, g, 1, ts_], rhs=STbf[:], start=True, stop=True)
                ebb = sm[:, tc, 220 + 8 * u:228 + 8 * u].unsqueeze(2).to_broadcast([128, 8, HP])
                self.dve("tensor_tensor", [f"pb{bE}", "tsm"], ["ssd_yi"], out=yi.rearrange("p (h q) -> p h q", q=HP),
                         in0=self.pb[bE][:, :].rearrange("p (h q) -> p h q", q=HP), in1=ebb, op=ALU.mult)
                self.dve("tensor_tensor", [f"pb{bI}", "ssd_yi"], ["ssd_yt"], out=yt[:], in0=self.pb[bI][:, :], in1=yi[:], op=ALU.add)
                Db = self.smallb[:, 4, 8 * u:8 * u + 8].unsqueeze(2).to_broadcast([128, 8, HP])
                self.dve("tensor_tensor", [BX, "smallb"], ["ssd_yi"], out=yi.rearrange("p (h q) -> p h q", q=HP),
                         in0=xtok[:, tc, :].rearrange("p (h q) -> p h q", q=HP), in1=Db, op=ALU.mult)
                self.dve("tensor_tensor", ["ssd_yt", "ssd_yi"], [BY], out=bufY[:, tc, :], in0=yt[:], in1=yi[:], op=ALU.add)
                yield
            self.dve("tensor_tensor", ["tsm"], ["ssd_wend"], out=wend[:], in0=sm[:, tc, 152 + 8 * u:160 + 8 * u],
                     in1=sm[:, tc, 120 + 8 * u:128 + 8 * u], op=ALU.subtract)
            self.act(["ssd_wend"], ["ssd_wend"], out=wend[:], in_=wend[:], func=AF.Exp)
            self.dve("tensor_tensor", ["ssd_xdt", "ssd_wend"], ["ssd_xw"], out=xw.rearrange("p (h q) -> p h q", q=HP),
                     in0=xdt[:, tc, :].rearrange("p (h q) -> p h q", q=HP), in1=wend[:].unsqueeze(2).to_broadcast([128, 8, HP]), op=ALU.mult)
            bS = self.bank()
            self.mm([f"Btok{g}", "ssd_xw"], [f"pb{bS}"], out=self.pb[bS][:, :], lhsT=self.Btok[:, g, tc, :], rhs=xw[:], start=True, stop=True)
            self.act(["tsm"], ["ssd_decb"], out=decb[:], in_=sm[:, tc, 152 + 8 * u:160 + 8 * u], func=AF.Exp)
            Sv = self.Sst[:, u, :].rearrange("p (h q) -> p h q", q=HP)
            self.dve("tensor_tensor", ["Sst", "ssd_decb"], ["Sst"], out=Sv, in0=Sv, in1=decb[:].unsqueeze(2).to_broadcast([128, 8, HP]), op=ALU.mult)
            self.dve("tensor_tensor", ["Sst", f"pb{bS}"], ["Sst"], out=self.Sst[:, u, :], in0=self.Sst[:, u, :], in1=self.pb[bS][:, :], op=ALU.add)
            if full and tc < 3:
                self.act(["Sst"], ["ssd_STbf"], out=STbf[:], in_=self.Sst[:, u, :], func=AF.Copy)
            yield

    def ssd_gates(self, u, ws, L):
        av = self.av
        g, BX, BY = L["g"], L["BX"], L["BY"]
        bufY, xdt, zt, tcol = L["bufY"], L["xdt"], L["zt"], L["tcol"]
        for a in range(2):
            wt, wn = self.wget(f"in_Z{u}{a}")
            for tc in range(4):
                b = self.bank()
                self.proj_tok(wt, wn, tc, b)
                self.act([f"pb{b}"], ["ssd_zt"], out=zt[:], in_=self.pb[b][:, 0:256], func=AF.Silu)
                self.dve("memset", [], ["ssd_tcol"], ap=tcol[:, 0:1], constant=0.0)
                ysl = bufY[:, tc, a * 256:(a + 1) * 256]
                self.dve("scalar_tensor_tensor", [BY, "ssd_zt", "ssd_tcol"], [BY, "ssd_tcol"], out=ysl, in0=ysl, scalar=1.0, in1=zt[:],
                         op0=ALU.mult, op1=ALU.mult)
                self.act([BY, "ssd_tcol"], ["ssd_zt", "ssd_tcol"], out=zt[:], in_=ysl, func=AF.Square, accum_out=tcol[:, 0:1])
                self.dve("tensor_tensor", ["ssq", "ssd_tcol"], ["ssq"], out=self.ssq[:, tc:tc + 1], in0=self.ssq[:, tc:tc + 1], in1=tcol[:, 0:1], op=ALU.add)
            if ws:
                self.smp_feat(u, a, wt, wn, "zS", AF.Silu)
        yz2 = self.yzgA if u % 2 == 0 else xdt
        yzn = "yzgA" if u % 2 == 0 else "ssd_xdt"
        for a in range(2):
            wt, wn = self.wget(f"in_GB{u}{a}")
            for tc in range(4):
                b = self.bank()
                self.proj_tok(wt, wn, tc, b)
                self.act([f"pb{b}"], ["ssd_zt"], out=zt[:], in_=self.pb[b][:, 0:256], func=AF.Sigmoid)
                self.dve("tensor_tensor", [BY, "ssd_zt"], [yzn], out=yz2[:, tc, a * 256:(a + 1) * 256], in0=bufY[:, tc, a * 256:(a + 1) * 256],
                         in1=zt[:], op=ALU.mult)
            if ws:
                self.smp_feat(u, a, wt, wn, "gbS", AF.Sigmoid)
        if u % 2 == 1:
            sc = av(self.OFF_M, [128, T], BF16)
            for tc in range(4):
                ts_ = slice(tc * 128, (tc + 1) * 128)
                r = self.ssq[:, 8 + tc:9 + tc]
                self.dve("tensor_scalar", ["ssq"], ["ssq"], out=r, in0=self.ssq[:, tc:tc + 1], scalar1=1.0 / 1024, scalar2=EPS, op0=ALU.mult, op1=ALU.add)
                self.act(["ssq"], ["ssq"], out=r, in_=r, func=AF.Sqrt)
                self.dve("reciprocal", ["ssq"], ["ssq"], out=r, in_=r)
                for (src, srcn, uh) in ((self.yzgA, "yzgA", u - 1), (xdt, "ssd_xdt", u)):
                    self.dve("tensor_scalar", [srcn, "ssq"], ["ssd_M"], out=sc[:], in0=src[:, tc, :], scalar1=r, scalar2=None, op0=ALU.mult)
                    b = self.bank()
                    pt = self.pb[b][:].bitcast(BF16)
                    for j in range(4):
                        self.tr(["ssd_M", "ident"], [f"pb{b}"], out=pt[:, j * 128:(j + 1) * 128], in_=sc[:, j * 128:(j + 1) * 128], identity=self.ident[:])
                    for j in range(4):
                        ch = uh * 4 + j
                        self.dve("scalar_tensor_tensor", [f"pb{b}", "wcols", "mergedT"], ["mergedT"], out=self.mergedT[:, ch, ts_],
                                 in0=pt[:, j * 128:(j + 1) * 128], scalar=self.wcols[:, 4, ch:ch + 1], in1=self.mergedT[:, ch, ts_],
                                 op0=ALU.mult, op1=ALU.add)

    def mix_out(self, ti, ws):
        chunks = [0, 1, 2, 3] + ([4] if ws else [])
        for bo in range(8):
            wt, wn = self.wget(f"wo_{bo}")
            for tc in chunks:
                rows = 128 if tc < 4 else NS
                b = self.bank()
                for dc in range(NKC):
                    self.mm([wn, "mergedT"], [f"pb{b}"], out=self.pb[b][:rows, 0:256], lhsT=self.mergedT[:, dc, tc * 128:tc * 128 + rows],
                            rhs=wt[:, dc, :], start=(dc == 0), stop=(dc == NKC - 1))
                xs = self.xres[:rows, tc, bo * 256:(bo + 1) * 256]
                self.dve("tensor_tensor", [f"pb{b}", f"xres{tc}"], [f"xres{tc}"], out=xs, in0=xs, in1=self.pb[b][:rows, 0:256], op=ALU.add)

    def store_pstates(self):
        self.dma("st_pC", ["Cst"], [], out=self.o_pC.rearrange("h k v -> (h k) v").rearrange("(hk p) v -> p hk v", p=128),
                 in_=self.Cst[:].rearrange("p a b c -> p (a b) c"))
        self.dma("st_pn", ["nst"], [], out=self.o_pn.rearrange("(hk p) -> p hk", p=128), in_=self.nst[:], nc_ok=True)
        self.dma("st_pm", ["mcol"], [], out=self.o_pm.rearrange("(o h) -> o h", o=1), in_=self.mcol[0:1, :])
        for j in range(3):
            self.dma(f"st_pc{j}", ["convhist"], [], out=self.o_pconv[j].rearrange("(c p) -> p c", p=128), in_=self.convhist[:, :, j], nc_ok=True)
        stage = self.av(self.OFF_YT, [128, 4, 128], F32)
        for u in range(4):
            b = self.bank()
            for q in range(4):
                self.tr(["Sst", "identf"], [f"pb{b}"], out=self.pb[b][:, q * 128:(q + 1) * 128], in_=self.Sst[:, u, q * 128:(q + 1) * 128],
                        identity=self.identf[:])
            self.act([f"pb{b}"], ["ssd_yt"], out=stage.rearrange("p a b -> p (a b)"), in_=self.pb[b][:, :], func=AF.Copy)
            self.dma("st_pS", ["ssd_yt"], [], out=self.o_pS[u * 512:(u + 1) * 512, :].rearrange("(q p) n -> p q n", p=128), in_=stage)

    SOFF = 15360

    def alloc_smp(self):
        sb = self.sb
        self.smS = sb("smS", [NS, 64], F32)
        self.xsT = sb("xsT", [128, 20, NS], F32)
        self.dtT = sb("dtT", [128, 2, NKC, NS], F32)
        self.histT = sb("histT", [128, 20, NS * 3], F32)
        self.BCs = sb("BCs", [NS, 2, 256], BF16)
        self.ssqS = sb("ssqS", [NS, 8], F32)
        self.ones16b = sb("ones16b", [NS, 128], BF16)

    def smp_views(self):
        av, o = self.av, self.SOFF
        v = {}
        v["qS"] = av(o, [NS, DK], F32)
        v["kS"] = av(o + 1024, [NS, DK], F32)
        v["vS"] = av(o + 2048, [NS, DV], F32)
        v["gS"] = av(o + 4096, [NS, DV], F32)
        v["kwbf"] = av(o + 6144, [NS, DK], BF16)
        v["kmask"] = av(o + 6656, [NS, DK], BF16)
        v["vSbf"] = av(o + 7168, [NS, DV], BF16)
        v["qTs"] = av(o + 8192, [128, 2, NS], F32)
        v["qmask"] = av(o + 8320, [128, 2, NS * NS], F32)
        v["decbc"] = av(o + 10368, [128, NS], F32)
        v["nS"] = av(o + 10432, [NS, DK], F32)
        v["hS"] = av(o + 11456, [NS, DV], F32)
        v["yaS"] = av(o + 13504, [NS, DV], BF16)
        v["cS"] = av(40192, [NS, 32], F32)
        v["tmpS"] = av(39168, [NS, 256], F32)
        return v

    def smp_init(self):
        self._op("pool", "memset", [], ["ones16b"], ap=self.ones16b[:], constant=1.0)
        stg = self.av(24576, [NS * 3, XBC], F32)
        self.dma("ld_sst", [], ["s_stg"], out=stg, in_=self.st_conv[:, :])
        for c0 in range(0, 20, 8):
            c1 = min(20, c0 + 8)
            b = self.bank()
            for ch in range(c0, c1):
                self.tr(["s_stg", "identf"], [f"pb{b}"], out=self.pb[b][:, (ch - c0) * 48:(ch - c0 + 1) * 48], in_=stg[:, ch * 128:(ch + 1) * 128],
                        identity=self.identf[:NS * 3, :NS * 3])
            self.dve("tensor_copy", [f"pb{b}"], ["histT"], out=self.histT[:, c0:c1, :].rearrange("p a b -> p (a b)"), in_=self.pb[b][:, 0:(c1 - c0) * 48])
        src = self.st_conv.rearrange("(b j) c -> b j c", j=3)
        self.dma("st_sc0", [], [], out=self.o_sconv[:, 0:2, :], in_=src[:, 1:3, :])
        self.dve("memset", [], ["ssqS"], ap=self.ssqS[:], constant=0.0)
        self.barrier()

    def smp_proj_tok(self, wt, wn, name, c0=0, scale=None, func=None, mul=False):
        v = self.smp_views()
        P7 = ["pb7g0", "pb7g1", "pb7u0", "pb7u1"]
        t0 = T
        for dc in range(NKC):
            self.mm([wn, "xnT"], P7, out=self.pb[7][:NS, 0:256], lhsT=self.xnT[:, dc, t0:t0 + NS], rhs=wt[:, dc, 0:256],
                    start=(dc == 0), stop=(dc == NKC - 1))
        dst = v[name][:, c0:c0 + 256]
        kw = {}
        if scale is not None:
            kw["scale"] = scale
        if not mul:
            self.act(P7, ["s_" + name], out=dst, in_=self.pb[7][:NS, 0:256], func=func or AF.Copy, **kw)
        else:
            self.act(P7, ["s_tmpS"], out=v["tmpS"][:], in_=self.pb[7][:NS, 0:256], func=func or AF.Copy, **kw)
            self.dve("tensor_tensor", ["s_tmpS", "s_" + name], ["s_" + name], out=dst, in0=dst, in1=v["tmpS"][:], op=ALU.mult)

    def smp_small(self, wt, wn):
        P7 = ["pb7g0", "pb7g1", "pb7u0", "pb7u1"]
        S = self.smS
        for dc in range(NKC):
            self.mm([wn, "xnT"], P7, out=self.pb[7][:NS, 0:40], lhsT=self.xnT[:, dc, T:TW], rhs=wt[:, dc, 0:40], start=(dc == 0), stop=(dc == NKC - 1))
        R, W = ["smS", "smallb"], ["smS"]
        self.dve("tensor_copy", P7, W, out=S[:, 0:40], in_=self.pb[7][:NS, 0:40])
        self.dma("ld_sm", [], ["smS"], out=S[:, 48:52], in_=self.st_m[:, :])
        self.dve("tensor_tensor", R, W, out=S[:, 40:44], in0=S[:, 0:4], in1=self.smallb[:NS, 0, 0:4], op=ALU.add)
        z = S[:, 56:60]
        self.dve("tensor_tensor", R, W, out=z, in0=S[:, 4:8], in1=self.smallb[:NS, 1, 0:4], op=ALU.add)
        self._softplus(z, S[:, 44:48], S[:, 60:64], True, "smS", NS)
        self.dve("tensor_tensor", R, W, out=S[:, 56:60], in0=S[:, 44:48], in1=S[:, 48:52], op=ALU.add)
        self.dve("tensor_tensor", R, W, out=S[:, 52:56], in0=S[:, 56:60], in1=S[:, 40:44], op=ALU.max)
        self.dma("st_sm", ["smS"], [], out=self.o_sm[:, :], in_=S[:, 52:56])
        self.dve("tensor_tensor", R, W, out=S[:, 60:64], in0=S[:, 56:60], in1=S[:, 52:56], op=ALU.subtract)
        self.act(R, W, out=S[:, 60:64], in_=S[:, 60:64], func=AF.Exp)
        self.dve("tensor_tensor", R, W, out=S[:, 56:60], in0=S[:, 40:44], in1=S[:, 52:56], op=ALU.subtract)
        self.act(R, W, out=S[:, 56:60], in_=S[:, 56:60], func=AF.Exp)
        dtx = self.av(self.SOFF, [NS, 2, SH, HP], F32)
        tmp = self.av(self.SOFF + 16384 - 512, [NS, 96], F32)
        self.dve("tensor_tensor", R, ["s_tmp96"], out=tmp[:, 0:32], in0=S[:, 8:40], in1=self.smallb[:NS, 2, :], op=ALU.add)
        self._softplus(tmp[:, 0:32], tmp[:, 32:64], tmp[:, 64:96], False, "s_tmp96", NS)
        self.dve("tensor_tensor", ["s_tmp96", "Arow"], ["s_tmp96"], out=tmp[:, 64:96], in0=tmp[:, 32:64], in1=self.Arow[:NS, :], op=ALU.mult)
        self.act(["s_tmp96"], ["s_tmp96"], out=tmp[:, 64:96], in_=tmp[:, 64:96], func=AF.Exp)
        for i in range(2):
            self.dve("tensor_copy", ["s_tmp96"], ["s_dtx"], out=dtx[:, i, :, :], in_=tmp[:, 32 + 32 * i:64 + 32 * i].unsqueeze(2).to_broadcast([NS, SH, HP]))
            b = self.bank()
            flat = dtx[:, i, :, :].rearrange("p a b -> p (a b)")
            for ch in range(NKC):
                self.tr(["s_dtx", "identf"], [f"pb{b}"], out=self.pb[b][:, ch * NS:(ch + 1) * NS], in_=flat[:, ch * 128:(ch + 1) * 128],
                        identity=self.identf[:NS, :NS])
            self.dve("tensor_copy", [f"pb{b}"], ["dtT"], out=self.dtT[:, i, :, :].rearrange("p a b -> p (a b)"), in_=self.pb[b][:, 0:NKC * NS])

    def _softplus(self, z, out, tmp, neg, nm, rows):
        R, Wt = [nm], [nm]
        self.dve("scalar_tensor_tensor", R, Wt, out=tmp, in0=z, scalar=-1.0, in1=z, op0=ALU.mult, op1=ALU.max)
        self.act(R, Wt, out=tmp, in_=tmp, func=AF.Exp, scale=-1.0)
        self.act(R + ["onesf"], Wt, out=tmp, in_=tmp, func=AF.Ln, bias=self.onesf[:rows, 0:1], scale=1.0)
        if neg:
            self.dve("scalar_tensor_tensor", R, Wt, out=out, in0=z, scalar=0.0, in1=tmp, op0=ALU.min, op1=ALU.subtract)
        else:
            self.dve("scalar_tensor_tensor", R, Wt, out=out, in0=z, scalar=0.0, in1=tmp, op0=ALU.max, op1=ALU.add)

    def smp_mlstm(self, h):
        v = self.smp_views()
        S = self.smS
        P7 = ["pb7g0", "pb7g1", "pb7u0", "pb7u1"]
        qS, kS, vS, gS, nS, hS, cS = v["qS"], v["kS"], v["vS"], v["gS"], v["nS"], v["hS"], v["cS"]
        wend, dec, mnew = S[:, 56 + h:57 + h], S[:, 60 + h:61 + h], S[:, 52 + h:53 + h]
        self.dma("ld_sn", [], ["s_nS", "s_dtx"], out=nS[:], in_=self.st_n[:, h * DK:(h + 1) * DK])
        self.dve("tensor_scalar", ["s_nS", "smS"], ["s_nS"], out=nS[:], in0=nS[:], scalar1=dec, scalar2=None, op0=ALU.mult)
        self.dve("scalar_tensor_tensor", ["s_kS", "smS", "s_nS"], ["s_nS"], out=nS[:], in0=kS[:], scalar=wend, in1=nS[:], op0=ALU.mult, op1=ALU.add)
        self.dma("st_sn", ["s_nS"], [], out=self.o_sn[:, h * DK:(h + 1) * DK], in_=nS[:])
        self.dve("memset", [], ["s_cS"], ap=cS[:, 0:1], constant=0.0)
        self.dve("scalar_tensor_tensor", ["s_qS", "s_nS", "s_cS"], ["s_tmpS", "s_cS"], out=v["tmpS"][:], in0=qS[:], scalar=1.0, in1=nS[:],
                 op0=ALU.mult, op1=ALU.mult, accum_out=cS[:, 0:1])
        self.dve("tensor_scalar", ["s_kS", "smS"], ["s_kwbf"], out=v["kwbf"][:], in0=kS[:], scalar1=wend, scalar2=None, op0=ALU.mult)
        self.dve("tensor_copy", ["s_vS"], ["s_vSbf"], out=v["vSbf"][:], in_=vS[:])
        self.dve("tensor_scalar", ["identf", "smS"], ["s_cS"], out=cS[:, 16:32], in0=self.identf[:NS, :NS], scalar1=dec, scalar2=None, op0=ALU.mult)
        self.mm(["onesf", "s_cS"], P7, out=self.pb[7][:, 0:NS], lhsT=self.onesf[:NS, :], rhs=cS[:, 16:32], start=True, stop=True)
        self.dve("tensor_copy", P7, ["s_decbc"], out=v["decbc"][:], in_=self.pb[7][:, 0:NS])
        for kc in range(2):
            self.tr(["s_qS", "identf"], P7, out=self.pb[7][:, 32 + kc * NS:32 + (kc + 1) * NS], in_=qS[:, kc * 128:(kc + 1) * 128], identity=self.identf[:NS, :NS])
        self.dve("tensor_copy", P7, ["s_qTs"], out=v["qTs"][:].rearrange("p a b -> p (a b)"), in_=self.pb[7][:, 32:32 + 2 * NS])
        self.dve("memset", [], ["s_qmask"], ap=v["qmask"][:].rearrange("p a b -> p (a b)"), constant=0.0)
        self.dve("tensor_copy", ["s_qTs", "s_qmask"], ["s_qmask"], out=v["qmask"][:, :, 0:NS * NS:NS + 1], in_=v["qTs"][:])
        kmask2 = self.av(self.SOFF + 14528, [NS, DK], BF16)
        KB = (5, 6)

        def kmm(j, kc):
            km, kmn = (v["kmask"], "s_kmask0") if j % 2 == 0 else (kmask2, "s_kmask1")
            if kc == 0:
                self.dve("tensor_scalar", ["s_kwbf", "identf"], [kmn], out=km[:], in0=v["kwbf"][:], scalar1=self.identf[:NS, j:j + 1],
                         scalar2=None, op0=ALU.mult)
            self.mm([kmn, "s_vSbf"], [f"pb{KB[kc]}"], out=self.pb[KB[kc]][:, :], lhsT=km[:, kc * 128:(kc + 1) * 128], rhs=v["vSbf"][:],
                    start=True, stop=True)

        def _ldC(jj):
            Cb_ = self.sarea[:, (jj % 2) * 1024:(jj % 2 + 1) * 1024].rearrange("p (a b) -> p a b", b=DV)
            self.dma(f"ld_C{jj % 2}", [], [f"s_C{jj % 2}"], out=Cb_, in_=self.st_C[jj, h].rearrange("(kc p) v -> p kc v", p=128))
        self.bank_mod = 5
        self.bank_rr %= 5
        _ldC(0)
        kmm(0, 0)
        kmm(0, 1)
        for j in range(NS):
            Cb = self.sarea[:, (j % 2) * 1024:(j % 2 + 1) * 1024].rearrange("p (a b) -> p a b", b=DV)
            cn = f"s_C{j % 2}"
            if j + 1 < NS:
                _ldC(j + 1)
            for kc in range(2):
                b = KB[kc]
                self.dve("scalar_tensor_tensor", [f"pb{b}", cn, "s_decbc"], [cn], out=Cb[:, kc, :], in0=Cb[:, kc, :], scalar=v["decbc"][:, j:j + 1],
                         in1=self.pb[b][:, :], op0=ALU.mult, op1=ALU.add)
                if j + 1 < NS:
                    kmm(j + 1, kc)
            self.dma(f"st_C{j % 2}", [cn], [], out=self.o_sC[j, h].rearrange("(kc p) v -> p kc v", p=128), in_=Cb)
            for kc in range(2):
                self.mm(["s_qmask", cn], P7, out=self.pb[7][:NS, :], lhsT=v["qmask"][:, kc, j * NS:(j + 1) * NS], rhs=Cb[:, kc, :],
                        start=(j == 0 and kc == 0), stop=(j == NS - 1 and kc == 1))
            yield
        self.bank_mod = 7
        C = ["s_cS", "smS"]
        self.act(C, ["s_cS"], out=cS[:, 1:2], in_=mnew, func=AF.Exp, scale=-1.0)
        self.dve("scalar_tensor_tensor", C, ["s_cS"], out=cS[:, 2:3], in0=cS[:, 0:1], scalar=-1.0, in1=cS[:, 0:1], op0=ALU.mult, op1=ALU.max)
        self.dve("tensor_tensor", C, ["s_cS"], out=cS[:, 2:3], in0=cS[:, 2:3], in1=cS[:, 1:2], op=ALU.max)
        self.dve("reciprocal", C, ["s_cS"], out=cS[:, 3:4], in_=cS[:, 2:3])
        self.act(P7, ["s_hS"], out=hS[:], in_=self.pb[7][:NS, :], func=AF.Copy)
        self.dve("memset", [], ["s_cS"], ap=cS[:, 4:5], constant=0.0)
        self.act(["s_hS"] + C, ["s_yaS", "s_cS"], out=v["yaS"][:], in_=hS[:], func=AF.Square, scale=cS[:, 3:4], accum_out=cS[:, 4:5])
        self.dve("tensor_scalar", C, ["s_cS"], out=cS[:, 5:6], in0=cS[:, 4:5], scalar1=1.0 / DV, scalar2=EPS, op0=ALU.mult, op1=ALU.add)
        self.act(C, ["s_cS"], out=cS[:, 5:6], in_=cS[:, 5:6], func=AF.Sqrt)
        self.dve("reciprocal", C, ["s_cS"], out=cS[:, 5:6], in_=cS[:, 5:6])
        self.dve("tensor_tensor", C, ["s_cS"], out=cS[:, 6:7], in0=cS[:, 5:6], in1=cS[:, 3:4], op=ALU.mult)
        self.dve("scalar_tensor_tensor", ["s_hS", "s_gS"] + C, ["s_yaS"], out=v["yaS"][:], in0=hS[:], scalar=cS[:, 6:7], in1=gS[:],
                 op0=ALU.mult, op1=ALU.mult)
        b = self.bank()
        pt = self.pb[b][:].bitcast(BF16)
        for j in range(4):
            self.tr(["s_yaS", "ident"], [f"pb{b}"], out=pt[:, j * NS:(j + 1) * NS], in_=v["yaS"][:, j * 128:(j + 1) * 128], identity=self.ident[:NS, :NS])
        for j in range(4):
            self.dve("tensor_scalar", [f"pb{b}", "wcols"], ["mergedT"], out=self.mergedT[:, h * 4 + j, T:TW], in0=pt[:, j * NS:(j + 1) * NS],
                     scalar1=self.wcols[:, 3, h * 4 + j:h * 4 + j + 1], scalar2=None, op0=ALU.mult)

    def smp_conv_chunk(self, wt, wn, oc, chunk, raw_ap, col):
        P7 = ["pb7g0", "pb7g1", "pb7u0", "pb7u1"]
        for dc in range(NKC):
            self.mm([wn, "xnT"], P7, out=self.pb[7][:, col:col + NS], lhsT=wt[:, dc, oc * 128:(oc + 1) * 128], rhs=self.xnT[:, dc, T:TW],
                    start=(dc == 0), stop=(dc == NKC - 1))
        cur = self.pb[7][:, col:col + NS]
        self.dve("tensor_copy", P7, ["s_raw"], out=raw_ap, in_=cur)
        hist = self.histT[:, chunk, :].rearrange("p (t j) -> p t j", j=3)
        w = self.convwc
        acc = self.av(self.OFF_MISC + 3264, [128, NS], F32)
        self.dve("tensor_scalar", ["histT", "convwc"], ["s_acc"], out=acc[:], in0=hist[:, :, 0], scalar1=w[:, chunk, 0:1], scalar2=None, op0=ALU.mult)
        for j in (1, 2):
            self.dve("scalar_tensor_tensor", ["histT", "convwc", "s_acc"], ["s_acc"], out=acc[:], in0=hist[:, :, j], scalar=w[:, chunk, j:j + 1],
                     in1=acc[:], op0=ALU.mult, op1=ALU.add)
        self.dve("scalar_tensor_tensor", ["s_raw", "convwc", "s_acc"], ["s_acc"], out=acc[:], in0=raw_ap, scalar=w[:, chunk, 3:4], in1=acc[:],
                 op0=ALU.mult, op1=ALU.add)
        self.act(["s_acc", "convwc"], ["xsT"], out=self.xsT[:, chunk, :], in_=acc[:], func=AF.Silu, bias=w[:, chunk, 4:5], scale=1.0)

    def smp_raw_out(self, raw, nchunk, dcol0):
        b = self.bank()
        for j in range(nchunk):
            self.tr(["s_raw", "identf"], [f"pb{b}"], out=self.pb[b][:NS, j * 128:(j + 1) * 128], in_=raw[:, j, :], identity=self.identf[:])
        stg = self.av(self.OFF_MISC + 3328, [NS, 512], F32)
        self.act([f"pb{b}"], ["s_rstg"], out=stg[:, 0:nchunk * 128], in_=self.pb[b][:NS, 0:nchunk * 128], func=AF.Copy)
        self.dma("st_rs", ["s_rstg"], [], out=self.o_sconv[:, 2, dcol0:dcol0 + nchunk * 128], in_=stg[:, 0:nchunk * 128])

    def smp_bc(self, g, wt, wn):
        raw = self.av(self.OFF_MISC + 5376, [128, 4, NS], F32)
        for oc in range(2):
            self.smp_conv_chunk(wt, wn, oc, 16 + g + 2 * oc, raw[:, oc, :], 64 + 16 * oc)
        b = self.bank()
        for oc in range(2):
            self.tr(["s_raw", "identf"], [f"pb{b}"], out=self.pb[b][:NS, oc * 128:(oc + 1) * 128], in_=raw[:, oc, :], identity=self.identf[:])
        stg = self.av(self.OFF_MISC + 3328, [NS, 512], F32)
        self.act([f"pb{b}"], ["s_rstg"], out=stg[:, 0:256], in_=self.pb[b][:NS, 0:256], func=AF.Copy)
        for oc in range(2):
            d0 = 2048 + 256 * oc + 128 * g
            self.dma("st_rs", ["s_rstg"], [], out=self.o_sconv[:, 2, d0:d0 + 128], in_=stg[:, oc * 128:(oc + 1) * 128])
        b = self.bank()
        for oc in range(2):
            self.tr(["xsT", "identf"], [f"pb{b}"], out=self.pb[b][:NS, oc * 128:(oc + 1) * 128], in_=self.xsT[:, 16 + g + 2 * oc, :], identity=self.identf[:])
        self.dve("tensor_copy", [f"pb{b}"], ["BCs"], out=self.BCs[:, g, :], in_=self.pb[b][:NS, 0:256])

    def smp_x(self, u, a, wt, wn):
        raw = self.av(self.OFF_MISC + 5376, [128, 4, NS], F32)
        for oc in range(2):
            j = 2 * a + oc
            self.smp_conv_chunk(wt, wn, oc, 4 * u + j, raw[:, j, :], 64 + 16 * oc)
        if a == 1:
            self.smp_raw_out(raw, 4, u * 512)

    def smp_ssd(self, u):
        g = u // 2
        SA = self.sarea
        t2 = SA[:, 1024:1536].rearrange("p (a b) -> p a b", b=SN)
        ysT = SA[:, 1536:1792].rearrange("p (a b) -> p a b", b=NS)
        xdtT = self.av(self.OFF_MISC + 5632, [128, 4, NS], F32)
        bcm = self.av(self.OFF_MISC + 5888, [NS, 256], BF16)
        cs = slice(4 * u, 4 * u + 4)
        self.dve("tensor_tensor", ["xsT", "dtT"], ["s_xdtT"], out=xdtT[:], in0=self.xsT[:, cs, :], in1=self.dtT[:, 0, cs, :], op=ALU.mult)
        self.dve("memset", [], ["s_ysT"], ap=ysT[:, cs, :], constant=0.0)
        for j in range(NS):
            Sb = SA[:, (j % 2) * 512:(j % 2 + 1) * 512].rearrange("p (a b) -> p a b", b=SN)
            sn_ = f"s_S{j % 2}"
            self.dma(f"ld_S{j % 2}", [], [sn_], out=Sb, in_=self.st_S[j, u * 512:(u + 1) * 512, :].rearrange("(rc p) n -> p rc n", p=128))
            self.dve("tensor_scalar", ["BCs", "identf"], ["s_bcm"], out=bcm[:], in0=self.BCs[:, g, :], scalar1=self.identf[:NS, j:j + 1], scalar2=None,
                     op0=ALU.mult)
            b = self.bank()
            self.mm(["ones16b", "s_bcm"], [f"pb{b}"], out=self.pb[b][:, 0:256], lhsT=self.ones16b[:], rhs=bcm[:], start=True, stop=True)
            self.dve("tensor_tensor", [f"pb{b}", "s_xdtT"], ["s_t2"], out=t2, in0=self.pb[b][:, 0:SN].unsqueeze(1).to_broadcast([128, 4, SN]),
                     in1=xdtT[:, :, j:j + 1].to_broadcast([128, 4, SN]), op=ALU.mult)
            for rc in range(4):
                self.dve("scalar_tensor_tensor", [sn_, "dtT", "s_t2"], [sn_], out=Sb[:, rc, :], in0=Sb[:, rc, :], scalar=self.dtT[:, 1, 4 * u + rc, j:j + 1],
                         in1=t2[:, rc, :], op0=ALU.mult, op1=ALU.add)
            self.dma(f"st_S{j % 2}", [sn_], [], out=self.o_sS[j, u * 512:(u + 1) * 512, :].rearrange("(rc p) n -> p rc n", p=128), in_=Sb)
            for rc in range(4):
                self.dve("scalar_tensor_tensor", [sn_, f"pb{b}", "s_ysT"], ["s_t2", "s_ysT"], out=t2[:, rc, :], in0=Sb[:, rc, :], scalar=1.0,
                         in1=self.pb[b][:, SN:2 * SN], op0=ALU.mult, op1=ALU.mult, accum_out=ysT[:, 4 * u + rc, j:j + 1])
            yield
        for rc in range(4):
            ch = 4 * u + rc
            self.dve("scalar_tensor_tensor", ["xsT", "wcols", "s_ysT"], ["s_ysT"], out=ysT[:, ch, :], in0=self.xsT[:, ch, :], scalar=self.wcols[:, 6, ch:ch + 1],
                     in1=ysT[:, ch, :], op0=ALU.mult, op1=ALU.add)

    def smp_feat(self, u, a, wt, wn, name, func):
        P7 = ["pb7g0", "pb7g1", "pb7u0", "pb7u1"]
        g = u // 2
        SA = self.sarea
        ysT = SA[:, 1536:1792].rearrange("p (a b) -> p a b", b=NS)
        gt = self.av(self.OFF_MISC + 6400, [128, NS], F32)
        sq = self.av(self.OFF_MISC + 6464, [128, NS], F32)
        for oc in range(2):
            ch = 4 * u + 2 * a + oc
            col = 64 + 16 * oc
            for dc in range(NKC):
                self.mm([wn, "xnT"], P7, out=self.pb[7][:, col:col + NS], lhsT=wt[:, dc, oc * 128:(oc + 1) * 128], rhs=self.xnT[:, dc, T:TW],
                        start=(dc == 0), stop=(dc == NKC - 1))
            self.act(P7, ["s_gt"], out=gt[:], in_=self.pb[7][:, col:col + NS], func=func)
            self.dve("tensor_tensor", ["s_ysT", "s_gt"], ["s_ysT"], out=ysT[:, ch, :], in0=ysT[:, ch, :], in1=gt[:], op=ALU.mult)
            if name == "zS":
                self.dve("tensor_tensor", ["s_ysT"], ["s_sq"], out=sq[:], in0=ysT[:, ch, :], in1=ysT[:, ch, :], op=ALU.mult)
                b = self.bank()
                self.mm(["s_sq", "onesf"], [f"pb{b}"], out=self.pb[b][:NS, 0:2], lhsT=sq[:], rhs=self.onesf[:, 0:2], start=True, stop=True)
                self.dve("tensor_tensor", [f"pb{b}", "ssqS"], ["ssqS"], out=self.ssqS[:, g:g + 1], in0=self.ssqS[:, g:g + 1], in1=self.pb[b][:NS, 0:1], op=ALU.add)
        if name == "gbS" and a == 1 and u % 2 == 1:
            r = self.ssqS[:, 4 + g:5 + g]
            self.dve("tensor_scalar", ["ssqS"], ["ssqS"], out=r, in0=self.ssqS[:, g:g + 1], scalar1=1.0 / 1024, scalar2=EPS, op0=ALU.mult, op1=ALU.add)
            self.act(["ssqS"], ["ssqS"], out=r, in_=r, func=AF.Sqrt)
            self.dve("reciprocal", ["ssqS"], ["ssqS"], out=r, in_=r)
            dg = self.av(self.OFF_MISC + 6528, [NS, NS], F32)
            self.dve("tensor_scalar", ["identf", "ssqS"], ["s_dg"], out=dg[:], in0=self.identf[:NS, :NS], scalar1=r, scalar2=None, op0=ALU.mult)
            b = self.bank()
            self.mm(["onesf", "s_dg"], [f"pb{b}"], out=self.pb[b][:, 0:NS], lhsT=self.onesf[:NS, :], rhs=dg[:], start=True, stop=True)
            for ch in range(8 * g, 8 * g + 8):
                self.dve("tensor_tensor", ["s_ysT", f"pb{b}"], ["s_gt"], out=gt[:], in0=ysT[:, ch, :], in1=self.pb[b][:, 0:NS], op=ALU.mult)
                self.dve("scalar_tensor_tensor", ["s_gt", "wcols", "mergedT"], ["mergedT"], out=self.mergedT[:, ch, T:TW], in0=gt[:],
                         scalar=self.wcols[:, 4, ch:ch + 1], in1=self.mergedT[:, ch, T:TW], op0=ALU.mult, op1=ALU.add)


class Builder(MixMixin, Builder):
    pass


def _tile_w(w, cols=256):
    Kd, N = w.shape
    nb = N // cols
    return np.ascontiguousarray(w.reshape(Kd // 128, 128, nb, cols).transpose(2, 1, 0, 3))


def _tile_wd(w):
    out = np.zeros((24, 128, 8, 512), np.float32)
    for hf in range(2):
        f0 = 0 if hf == 0 else 24
        nfc = 24 if hf == 0 else 20
        for db in range(4):
            for kb in range(3):
                nk = min(8, nfc - kb * 8)
                fc0 = f0 + kb * 8
                blk = w[fc0 * 128:(fc0 + nk) * 128, db * 512:(db + 1) * 512].reshape(nk, 128, 512).transpose(1, 0, 2)
                out[hf * 12 + db * 3 + kb, :, :nk, :] = blk
    return out


def _tile_win(w):
    out = np.zeros((len(WIN_PLAN), 128, 16, 256), np.float32)
    for i, (tag, cols) in enumerate(WIN_PLAN):
        ok = cols >= 0
        blk = np.zeros((2048, 256), np.float32)
        blk[:, ok] = w[:, cols[ok]]
        out[i] = blk.reshape(16, 128, 256).transpose(1, 0, 2)
    return out


def _prep_common(inp):
    m = {}
    vec = np.zeros((8, D), np.float32)
    vec[0] = inp["ffn1_norm"][0]
    vec[1] = inp["mix_norm"][0]
    vec[2] = inp["ffn2_norm"][0]
    vec[3] = inp["ml_head_norm"][0]
    vec[4] = inp["ssm_norm"][0]
    vec[5] = inp["final_norm"]
    vec[6] = np.repeat(inp["ssm_D"][0], HP)
    m["vecs"] = vec
    sm = np.zeros((8, 32), np.float32)
    sm[0, 0:4] = inp["ml_i_bias"][0]
    sm[1, 0:4] = inp["ml_f_bias"][0]
    sm[2] = inp["ssm_dt_bias"][0]
    sm[3] = inp["ssm_A_log"][0]
    sm[4] = inp["ssm_D"][0]
    m["small"] = sm
    m["convw"] = np.concatenate([inp["ssm_conv_w"][0], inp["ssm_conv_b"]], 0).astype(np.float32)
    for i, pre in ((1, "ffn1"), (2, "ffn2")):
        m[f"wg{i}"] = _tile_w(inp[pre + "_w_gate"][0])
        m[f"wu{i}"] = _tile_w(inp[pre + "_w_up"][0])
        m[f"wd{i}"] = _tile_wd(inp[pre + "_w_down"][0])
    m["wo"] = _tile_w(inp["w_out"][0])
    m["win"] = _tile_win(inp["w_in"][0])
    return m


FULL = [False, False, True, True]


def kernel(**inp):
    inp = {k: np.asarray(v) for k, v in inp.items()}
    NT = len(FULL)
    npre = sum(1 for f in FULL if not f)
    nfull = NT - npre
    B = Builder({"NT": NT, "full": FULL})
    B.build()
    common = _prep_common(inp)
    xpr = inp["x_prompt"]
    xsm = inp["x_sample"][:, 0, :]
    in_maps = []
    for c in range(8):
        m = dict(common)
        b, half = c // 2, c % 2
        sl = slice(c * NS, (c + 1) * NS)
        own = xpr[b, half * nfull * T:(half + 1) * nfull * T]
        pre = xpr[b, 0:npre * T]
        m["xp"] = np.ascontiguousarray(np.concatenate([pre, own], 0))
        sm = common["small"].copy()
        sm[5, 0] = 1.0 if half == 1 else 0.0
        sm[5, 1] = 0.0 if half == 1 else NEG
        m["small"] = sm
        m["xs"] = np.ascontiguousarray(xsm[sl])
        m["st_conv"] = np.ascontiguousarray(inp["state_conv"][0, sl].reshape(NS * 3, XBC))
        m["st_C"] = np.ascontiguousarray(inp["state_mlstm_C"][0, sl])
        m["st_n"] = np.ascontiguousarray(inp["state_mlstm_n"][0, sl].reshape(NS, NH * DK))
        m["st_m"] = np.ascontiguousarray(inp["state_mlstm_m"][0, sl])
        m["st_S"] = np.ascontiguousarray(inp["state_ssm"][0, sl].reshape(NS, SH * HP, SN))
        in_maps.append({k: v for k, v in m.items() if k in B.dram})
    res = run_bass_kernel_spmd(B.nc, in_maps, core_ids=list(range(8))).results
    y_prompt = np.stack([np.concatenate([res[2 * b]["yp"], res[2 * b + 1]["yp"]], 0) for b in range(4)], 0)
    y_sample = np.concatenate([res[c]["ys"] for c in range(8)], 0)[:, None, :]
    last = [2 * b + 1 for b in range(4)]
    p_conv = np.stack([res[c]["o_pconv"] for c in last], 0)[None]
    p_C = np.stack([res[c]["o_pC"] for c in last], 0)[None]
    p_n = np.stack([res[c]["o_pn"].reshape(NH, DK) for c in last], 0)[None]
    p_m = np.stack([res[c]["o_pm"] for c in last], 0)[None]
    p_ssm = np.stack([res[c]["o_pS"].reshape(SH, HP, SN) for c in last], 0)[None]
    s_conv = np.concatenate([res[c]["o_sconv"] for c in range(8)], 0)[None]
    s_C = np.concatenate([res[c]["o_sC"] for c in range(8)], 0)[None]
    s_n = np.concatenate([res[c]["o_sn"].reshape(NS, NH, DK) for c in range(8)], 0)[None]
    s_m = np.concatenate([res[c]["o_sm"] for c in range(8)], 0)[None]
    s_ssm = np.concatenate([res[c]["o_sS"].reshape(NS, SH, HP, SN) for c in range(8)], 0)[None]
    outs = (y_prompt, y_sample, p_conv, p_C, p_n, p_m, p_ssm, s_conv, s_C, s_n, s_m, s_ssm)
    return tuple(np.ascontiguousarray(o, dtype=np.float32) for o in outs)
```

```python
import numpy as np
from contextlib import ExitStack
import concourse.bass as bass
import concourse.mybir as mybir
from concourse.alu_op_type import AluOpType as ALU
from concourse.bass_utils import run_bass_kernel_spmd

F32 = mybir.dt.float32
BF16 = mybir.dt.bfloat16
AF = mybir.ActivationFunctionType
AX = mybir.AxisListType

D = 2048
DFF = 5632
NKC = D // 128
NFC = DFF // 128
T = 512
NS = 16
TW = T + NS
EPS = 1e-6
IN_DIM = 14888
STRICT_WAR = False


class Buf:
    __slots__ = ("name", "w", "r")

    def __init__(self, name):
        self.name = name
        self.w = None
        self.r = {}


class Kern:
    ENGS = ("pe", "act", "dve", "pool", "sp")

    def __init__(self, nc, stack):
        self.nc = nc
        self.stack = stack
        self.ops = {e: [] for e in self.ENGS}
        self.seq = {e: 0 for e in self.ENGS}
        self.seen = {e: {} for e in self.ENGS}
        self.esem = {e: stack.enter_context(nc.semaphore("s_" + e)) for e in self.ENGS}
        self.dsems = {}
        self.dcount = {}
        self.bufs = {}
        self.waited = set()
        self.lastreal = {e: 0 for e in self.ENGS}

    def buf(self, name):
        b = self.bufs.get(name)
        if b is None:
            b = self.bufs[name] = Buf(name)
        return b

    def dma_sem(self, name):
        if name not in self.dsems:
            self.dsems[name] = self.stack.enter_context(self.nc.semaphore("d_" + name))
            self.dcount[name] = 0
        return name

    def _deps(self, eng, reads, writes, own=None):
        deps = {}

        def add(k, s):
            if deps.get(k, 0) < s:
                deps[k] = s
        for b in reads:
            b = self.buf(b) if isinstance(b, str) else b
            if b.w is not None:
                add(*b.w)
        for b in writes:
            b = self.buf(b) if isinstance(b, str) else b
            if b.w is not None:
                add(*b.w)
            for k, s in b.r.items():
                if k != eng or STRICT_WAR:
                    add(k, s)
        waits = []
        seen = self.seen[eng]
        for k, s in deps.items():
            if (k == eng and eng in ("pe", "sp")) or k == own:
                continue
            if seen.get(k, 0) < s:
                seen[k] = s
                waits.append((k, s))
                if k in self.ENGS:
                    self.waited.add((k, s))
        return waits

    def _commit(self, key, seq, eng, reads, writes):
        for b in reads:
            b = self.buf(b) if isinstance(b, str) else b
            if b.r.get(key, 0) < seq:
                b.r[key] = seq
        for b in writes:
            b = self.buf(b) if isinstance(b, str) else b
            b.w = (key, seq)
            b.r = {}

    def op(self, eng, fn, reads=(), writes=()):
        waits = self._deps(eng, reads, writes)
        self.seq[eng] += 1
        seq = self.seq[eng]
        self.ops[eng].append((waits, fn, None, seq))
        self.lastreal[eng] = seq
        self._commit(eng, seq, eng, reads, writes)

    def dma(self, eng, fn, sem, reads=(), writes=()):
        self.dma_sem(sem)
        waits = self._deps(eng, reads, writes, own="D:" + sem)
        self.dcount[sem] += 16
        val = self.dcount[sem]
        self.seq[eng] += 1
        self.ops[eng].append((waits, fn, sem, self.seq[eng]))
        key = "D:" + sem
        self._commit(key, val, eng, reads, writes)

    def barrier(self, engs):
        last = {e: self.lastreal[e] for e in engs if self.lastreal[e] > 0 and e != "sp"}
        dl = {"D:" + s: v for s, v in self.dcount.items() if v > 0 and not s.startswith("w")}
        for e in engs:
            waits = []
            seen = self.seen[e]
            for k, s in list(last.items()) + list(dl.items()):
                if k == e:
                    continue
                if seen.get(k, 0) < s:
                    seen[k] = s
                    waits.append((k, s))
                    if k in self.ENGS:
                        self.waited.add((k, s))
            if waits:
                self.seq[e] += 1
                self.ops[e].append((waits, None, None, self.seq[e]))

    def final_wait_all_dma(self, eng="sp"):
        waits = [("D:" + s, v) for s, v in self.dcount.items() if v > 0]
        self.seq[eng] += 1
        self.ops[eng].append((waits, None, None, self.seq[eng]))

    def emit(self):
        nc = self.nc
        handles = {"pe": "tensor", "act": "scalar", "dve": "vector", "pool": "gpsimd", "sp": "sync"}
        val = {}
        for e in self.ENGS:
            c = 0
            for (_, _, dsem, seq) in self.ops[e]:
                if (e, seq) in self.waited:
                    c += 1
                    val[(e, seq)] = c
        ops, esem, dsems, waited = self.ops, self.esem, self.dsems, self.waited

        def run(e):
            def body(h):
                for (waits, fn, dsem, seq) in ops[e]:
                    for (k, s) in waits:
                        if k.startswith("D:"):
                            h.wait_ge(dsems[k[2:]], s)
                        else:
                            h.wait_ge(esem[k], val[(k, s)])
                    if fn is None:
                        continue
                    ins = fn(h)
                    if dsem is not None:
                        ins.then_inc(dsems[dsem], 16)
                    elif (e, seq) in waited:
                        ins.then_inc(esem[e], 1)
            return body

        with nc.Block() as block:
            block.sync(run("sp"))
            block.gpsimd(run("pool"))
            block.tensor(run("pe"))
            block.scalar(run("act"))
            block.vector(run("dve"))


class Prog:
    def __init__(self, cfg):
        self.cfg = cfg
        self.nc = bass.Bass("TRN2", target_bir_lowering=False)
        self.stack = ExitStack()
        self.K = Kern(self.nc, self.stack)
        self.dram = {}

    def din(self, name, shape, dtype=F32):
        t = self.nc.dram_tensor(name, list(shape), dtype, kind="ExternalInput")
        self.dram[name] = t
        return t.ap()

    def dout(self, name, shape, dtype=F32):
        t = self.nc.dram_tensor(name, list(shape), dtype, kind="ExternalOutput")
        self.dram[name] = t
        return t.ap()

    def sb(self, name, shape, dtype):
        return self.stack.enter_context(self.nc.sbuf_tensor(name, list(shape), dtype))

    def ps(self, name, shape, dtype=F32):
        return self.stack.enter_context(self.nc.psum_tensor(name, list(shape), dtype))


DK = 256
DV = 512
NH = 4
SH = 32
HP = 64
SN = 128
XBC = 2560
NEG = -1e30
ARENA = 20480


def win_plan():
    oq, ok_, ov, oig, ofg, oog, oz, ox = 0, 1024, 2048, 4096, 4100, 4104, 6152, 8200
    oB, oC, odt, ogA, ogB = 10248, 10504, 10760, 10792, 12840
    pl = []
    ar = np.arange
    for g in range(2):
        pl.append((f"BC{g}", np.concatenate([oB + g * 128 + ar(128), oC + g * 128 + ar(128)])))
    sm = np.full(256, -1, np.int64)
    sm[0:4] = oig + ar(4); sm[4:8] = ofg + ar(4); sm[8:40] = odt + ar(32)
    pl.append(("SM", sm))
    for h in range(NH):
        pl.append((f"Q{h}", oq + h * 256 + ar(256)))
        pl.append((f"K{h}", ok_ + h * 256 + ar(256)))
        for a in range(2):
            pl.append((f"V{h}{a}", ov + h * 512 + a * 256 + ar(256)))
        for a in range(2):
            pl.append((f"OG{h}{a}", oog + h * 512 + a * 256 + ar(256)))
        for a in range(2):
            pl.append((f"GA{h}{a}", ogA + h * 512 + a * 256 + ar(256)))
    for u in range(4):
        for a in range(2):
            pl.append((f"X{u}{a}", ox + u * 512 + a * 256 + ar(256)))
        for a in range(2):
            pl.append((f"Z{u}{a}", oz + u * 512 + a * 256 + ar(256)))
        for a in range(2):
            pl.append((f"GB{u}{a}", ogB + u * 512 + a * 256 + ar(256)))
    return pl


WIN_PLAN = win_plan()
WIN_IDX = {t: i for i, (t, _) in enumerate(WIN_PLAN)}


class Builder(Prog):
    def __init__(self, cfg):
        super().__init__(cfg)
        self.NT = cfg.get("NT", 4)
        self.full = cfg.get("full", [True] * self.NT)
        self.bank_rr = 0
        self.wq = []
        self.wq_issued = 0
        self.wq_next = 0
        self.NSLOT = cfg.get("nslot", 4)
        self.uid = 0

    def _op(self, eng, meth, reads, writes, **kw):
        self.K.op(eng, lambda e: getattr(e, meth)(**kw), reads, writes)

    def dve(self, meth, reads, writes, **kw):
        self._op("dve", meth, reads, writes, **kw)

    def act(self, reads, writes, **kw):
        self._op("act", "activation", reads, writes, **kw)

    def mm(self, reads, writes, **kw):
        self._op("pe", "matmul", reads, writes, **kw)

    def tr(self, reads, writes, **kw):
        self._op("pe", "transpose", reads, writes, **kw)

    def dma(self, sem, reads, writes, out, in_, eng="sp", nc_ok=False):
        nc = self.nc
        if nc_ok:
            def f(e):
                with nc.allow_non_contiguous_dma(reason="small strided transfer"):
                    return e.dma_start(out=out, in_=in_)
        else:
            def f(e):
                return e.dma_start(out=out, in_=in_)
        self.K.dma(eng, f, sem, reads=reads, writes=writes)

    def av(self, off, shape, dtype, rows=None):
        n = int(np.prod(shape[1:]))
        e = n * (2 if dtype == F32 else 1)
        a = off // 2
        assert off % 4 == 0 and a + e <= ARENA, (off, shape, a + e)
        ap = self.arena[:shape[0], a:a + e]
        if dtype == F32:
            ap = ap.bitcast(F32)
        if len(shape) == 3:
            ap = ap.rearrange("p (a b) -> p a b", b=shape[2])
        elif len(shape) == 4:
            ap = ap.rearrange("p (a b c) -> p a b c", b=shape[2], c=shape[3])
        return ap

    def bank(self):
        b = self.bank_rr
        self.bank_rr = (b + 1) % getattr(self, "bank_mod", 7)
        return b

    def declare(self):
        NT = self.NT
        self.xp = self.din("xp", [NT * T, D])
        self.xs = self.din("xs", [NS, D])
        self.vecs = self.din("vecs", [8, D])
        self.small = self.din("small", [8, 32])
        self.convw = self.din("convw", [5, XBC])
        self.wg = [self.din(f"wg{i}", [22, 128, 16, 256]) for i in (1, 2)]
        self.wu = [self.din(f"wu{i}", [22, 128, 16, 256]) for i in (1, 2)]
        self.wd = [self.din(f"wd{i}", [24, 128, 8, 512]) for i in (1, 2)]
        self.wo = self.din("wo", [8, 128, 16, 256])
        self.win = self.din("win", [len(WIN_PLAN), 128, 16, 256])
        self.st_conv = self.din("st_conv", [NS * 3, XBC])
        self.st_C = self.din("st_C", [NS, NH, DK, DV])
        self.st_n = self.din("st_n", [NS, NH * DK])
        self.st_m = self.din("st_m", [NS, NH])
        self.st_S = self.din("st_S", [NS, SH * HP, SN])
        self.yp = self.dout("yp", [sum(1 for f in self.full if f) * T, D])
        self.ys = self.dout("ys", [NS, D])
        self.o_pconv = self.dout("o_pconv", [3, XBC])
        self.o_pC = self.dout("o_pC", [NH, DK, DV])
        self.o_pn = self.dout("o_pn", [NH * DK])
        self.o_pm = self.dout("o_pm", [NH])
        self.o_pS = self.dout("o_pS", [SH * HP, SN])
        self.o_sconv = self.dout("o_sconv", [NS, 3, XBC])
        self.o_sC = self.dout("o_sC", [NS, NH, DK, DV])
        self.o_sn = self.dout("o_sn", [NS, NH * DK])
        self.o_sm = self.dout("o_sm", [NS, NH])
        self.o_sS = self.dout("o_sS", [NS, SH * HP, SN])

    def alloc(self):
        sb = self.sb
        self.xres = sb("xres", [128, 5, D], F32)
        self.xnT = sb("xnT", [128, NKC, TW], BF16)
        self.arena = sb("arena", [128, ARENA], BF16)
        self.hT = self.arena[:, 0:24 * TW].rearrange("p (f t) -> p f t", t=TW)
        self.xsb = self.arena[:, 0:4 * D].rearrange("p (c d) -> p c d", d=D)
        self.xsbs = self.arena[:, 4 * D:5 * D]
        self.junk = self.arena[:, 5 * D:6 * D]
        self.fnrow = self.arena[:, 6 * D:8 * D].bitcast(F32)
        self.ybuf = self.arena[:, 0:2 * D].bitcast(F32)
        self.ws = [sb(f"ws{i}", [128, 4096], BF16) for i in range(self.NSLOT)]
        self.ident = sb("ident", [128, 128], BF16)
        self.identf = sb("identf", [128, 128], F32)
        self.wcols = sb("wcols", [128, 8, NKC], F32)
        self.stat = sb("stat", [128, 64], F32)
        self.sg = [self.arena[:, 12672 + i * 2 * TW:12672 + (i + 1) * 2 * TW].bitcast(F32) for i in range(2)]
        self.pb = [self.ps(f"pb{i}", [128, 512], F32) for i in range(8)]
        self.alloc_mix()

    def wplan(self, ap, kc, cols, tag):
        self.wq.append((ap, kc, cols, tag))

    def _issue_w(self, i):
        ap, kc, cols, tag = self.wq[i]
        s = i % self.NSLOT
        dst = self.ws[s][:, 0:kc * cols].rearrange("p (k c) -> p k c", c=cols)
        step = 2048 // cols
        for k0 in range(0, kc, step):
            k1 = min(kc, k0 + step)
            self.dma(f"w{s}", [], [f"ws{s}"], out=dst[:, k0:k1, :], in_=ap[:, k0:k1, :], eng="pool")

    def wget(self, tag, hold=1):
        i = self.wq_next
        assert self.wq[i][3] == tag, (self.wq[i][3], tag)
        self.wq_next += 1
        while self.wq_issued < min(len(self.wq), i - (hold - 1) + self.NSLOT):
            self._issue_w(self.wq_issued)
            self.wq_issued += 1
        s = i % self.NSLOT
        _, kc, cols, _ = self.wq[i]
        return self.ws[s][:, 0:kc * cols].rearrange("p (k c) -> p k c", c=cols), f"ws{s}"

    def consts(self):
        K = self.K
        self._op("pool", "memset", [], ["identf"], ap=self.identf[:], constant=0.0)
        self._op("pool", "affine_select", ["identf"], ["identf"], out=self.identf[:], in_=self.identf[:], pattern=[[-1, 128]],
                 compare_op=ALU.not_equal, fill=1.0, base=0, channel_multiplier=1)
        self.dve("tensor_copy", ["identf"], ["ident"], out=self.ident[:], in_=self.identf[:])
        self.dma("ld_c0", [], ["wcols"], out=self.wcols[:], in_=self.vecs.rearrange("v (c p) -> p v c", p=128), nc_ok=True)
        self.consts_mix()

    def rstd_of(self, tc, rows, col0=0):
        ss = self.stat[:rows, col0 + tc:col0 + tc + 1]
        rs = self.stat[:rows, col0 + 8 + tc:col0 + 9 + tc]
        xin = self.xres[:rows, tc, :]
        self.dve("memset", [], [f"ss{tc}"], ap=ss, constant=0.0)
        self.act([f"xres{tc}", f"ss{tc}"], ["junk", f"ss{tc}"], out=self.junk[:rows, :], in_=xin, func=AF.Square, accum_out=ss)
        self.dve("tensor_scalar", [f"ss{tc}"], [f"rs{tc}"], out=rs, in0=ss, scalar1=1.0 / D, scalar2=EPS, op0=ALU.mult, op1=ALU.add)
        self.act([f"rs{tc}"], [f"rs{tc}"], out=rs, in_=rs, func=AF.Sqrt)
        self.dve("reciprocal", [f"rs{tc}"], [f"rs{tc}"], out=rs, in_=rs)
        return rs

    def rms_to_T(self, vec_idx, with_samples):
        chunks = [0, 1, 2, 3] + ([4] if with_samples else [])
        for tc in chunks:
            rows = 128 if tc < 4 else NS
            rs = self.rstd_of(tc, rows)
            dst = self.xsb[:, tc, :] if tc < 4 else self.xsbs[:rows, :]
            self.act([f"xres{tc}", f"rs{tc}"], [f"xsb{tc}"], out=dst, in_=self.xres[:rows, tc, :], func=AF.Copy, scale=rs)
        w = TW if with_samples else T
        for dc in range(NKC):
            b = self.bank()
            pt = self.pb[b][:].bitcast(BF16)
            for tc in chunks:
                rows = 128 if tc < 4 else NS
                src = self.xsb[:, tc, dc * 128:(dc + 1) * 128] if tc < 4 else self.xsbs[:rows, dc * 128:(dc + 1) * 128]
                self.tr([f"xsb{tc}", "ident"], [f"pb{b}"], out=pt[:, tc * 128:tc * 128 + rows], in_=src, identity=self.ident[:rows, :rows])
            self.dve("tensor_scalar", [f"pb{b}", "wcols"], ["xnT"], out=self.xnT[:, dc, 0:w], in0=pt[:, 0:w],
                     scalar1=self.wcols[:, vec_idx, dc:dc + 1], scalar2=None, op0=ALU.mult)

    HALVES = ((0, 12), (12, 22))

    def plan_ffn(self, li):
        for hf, (b0, b1) in enumerate(self.HALVES):
            for fb in range(b0, b1):
                self.wplan(self.wg[li][fb], 16, 256, f"g{li}_{fb}")
                self.wplan(self.wu[li][fb], 16, 256, f"u{li}_{fb}")
            nfc = (b1 - b0) * 2
            for db in range(4):
                for kb in range(3):
                    nk = min(8, nfc - kb * 8)
                    self.wplan(self.wd[li][hf * 12 + db * 3 + kb], nk, 512, f"d{li}_{hf}_{db}_{kb}")

    def ffn(self, li, with_samples):
        P7 = ["pb7"]
        chunks = [0, 1, 2, 3] + ([4] if with_samples else [])
        for hf, (b0, b1) in enumerate(self.HALVES):
            nfc = (b1 - b0) * 2
            for fb in range(b0, b1):
                wg, wgn = self.wget(f"g{li}_{fb}")
                wu, wun = self.wget(f"u{li}_{fb}", hold=2)
                for j in range(2):
                    fc = (fb - b0) * 2 + j
                    bg, bu = self.bank(), self.bank()
                    par = fc % 2
                    for (wt, wn, b, col, sn) in ((wg, wgn, bg, par * 32, f"pb7g{par}"), (wu, wun, bu, par * 32 + 16, f"pb7u{par}")):
                        for dc in range(NKC):
                            self.mm([wn, "xnT"], [f"pb{b}"], out=self.pb[b][:, 0:T], lhsT=wt[:, dc, j * 128:(j + 1) * 128],
                                    rhs=self.xnT[:, dc, 0:T], start=(dc == 0), stop=(dc == NKC - 1))
                        if with_samples:
                            for dc in range(NKC):
                                self.mm([wn, "xnT"], [sn], out=self.pb[7][:, col:col + NS], lhsT=wt[:, dc, j * 128:(j + 1) * 128],
                                        rhs=self.xnT[:, dc, T:TW], start=(dc == 0), stop=(dc == NKC - 1))
                    sgi = fc % 2
                    sg = self.sg[sgi]
                    self.act([f"pb{bg}"], [f"sg{sgi}"], out=sg[:, 0:T], in_=self.pb[bg][:, 0:T], func=AF.Silu)
                    self.dve("tensor_tensor", [f"sg{sgi}", f"pb{bu}"], [f"hT{fc}"], out=self.hT[:, fc, 0:T], in0=sg[:, 0:T],
                             in1=self.pb[bu][:, 0:T], op=ALU.mult)
                    if with_samples:
                        self.act([f"pb7g{par}"], [f"sg{sgi}s"], out=sg[:, T:TW], in_=self.pb[7][:, par * 32:par * 32 + NS], func=AF.Silu)
                        self.dve("tensor_tensor", [f"sg{sgi}s", f"pb7u{par}"], [f"hT{fc}s"], out=self.hT[:, fc, T:TW], in0=sg[:, T:TW],
                                 in1=self.pb[7][:, par * 32 + 16:par * 32 + 16 + NS], op=ALU.mult)
            P7n = ["pb7g0", "pb7g1", "pb7u0", "pb7u1"]
            for db in range(4):
                accb = {tc: (self.bank() if tc < 4 else 7) for tc in chunks}
                for kb in range(3):
                    nk = min(8, nfc - kb * 8)
                    wd, wdn = self.wget(f"d{li}_{hf}_{db}_{kb}")
                    for tc in chunks:
                        rows = 128 if tc < 4 else NS
                        b = accb[tc]
                        for kc in range(nk):
                            fc = kb * 8 + kc
                            first = (fc == 0)
                            last = (fc == nfc - 1)
                            self.mm([wdn, f"hT{fc}" if tc < 4 else f"hT{fc}s"], [f"pb{b}"] if tc < 4 else P7n,
                                    out=self.pb[b][:rows, :], lhsT=self.hT[:, fc, tc * 128:tc * 128 + rows], rhs=wd[:, kc, :],
                                    start=first, stop=last)
                for tc in chunks:
                    rows = 128 if tc < 4 else NS
                    b = accb[tc]
                    xs = self.xres[:rows, tc, db * 512:(db + 1) * 512]
                    self.dve("scalar_tensor_tensor", ([f"pb{b}"] if tc < 4 else P7n) + [f"xres{tc}"], [f"xres{tc}"],
                             out=xs, in0=self.pb[b][:rows, :], scalar=0.5, in1=xs, op0=ALU.mult, op1=ALU.add)

    def final_store(self, ti, with_samples):
        chunks = [0, 1, 2, 3] + ([4] if with_samples else [])
        self.dma("ld_fn", [], ["fnrow"], out=self.fnrow, in_=self.vecs[5:6, :].to_broadcast([128, D]))
        for tc in chunks:
            rows = 128 if tc < 4 else NS
            rs = self.rstd_of(tc, rows)
            yb = self.ybuf
            self.dve("scalar_tensor_tensor", [f"xres{tc}", f"rs{tc}", "fnrow"], ["ybuf"], out=yb[:rows, :], in0=self.xres[:rows, tc, :],
                     scalar=rs, in1=self.fnrow[:rows, :], op0=ALU.mult, op1=ALU.mult)
            dst = self.yp[ti * T + tc * 128: ti * T + (tc + 1) * 128, :] if tc < 4 else self.ys[:, :]
            self.dma("st_y", ["ybuf"], [], out=dst, in_=yb[:rows, :])

    def load_tile(self, ti, with_samples):
        for tc in range(4):
            src = self.xp[ti * T + tc * 128: ti * T + (tc + 1) * 128, :]
            self.dma(f"ld_x{tc}", [], [f"xres{tc}"], out=self.xres[:, tc, :], in_=src)
        if with_samples:
            self.dma("ld_x4", [], ["xres4"], out=self.xres[:NS, 4, :], in_=self.xs[:, :])

    def barrier(self):
        self.K.barrier(("pe", "act", "dve", "sp"))

    def build(self):
        self.declare()
        self.alloc()
        stages = self.cfg.get("stages", ("ffn1", "mix", "ffn2"))
        for ti in range(self.NT):
            full = self.full[ti]
            if "ffn1" in stages:
                self.plan_ffn(0)
            if "mix" in stages:
                self.plan_mix(ti, full)
            if "ffn2" in stages and full:
                self.plan_ffn(1)
        self.consts()
        npre = sum(1 for f in self.full if not f)
        for ti in range(self.NT):
            full = self.full[ti]
            ws = (ti == self.NT - 1)
            self.load_tile(ti, ws)
            if "ffn1" in stages:
                self.rms_to_T(0, ws)
                self.ffn(0, ws)
                self.barrier()
            if "mix" in stages:
                self.mixer(ti, ws, full)
                self.barrier()
            if full:
                if "ffn2" in stages:
                    self.rms_to_T(2, ws)
                    self.ffn(1, ws)
                    self.barrier()
                self.final_store(ti - npre, ws)
                self.barrier()
            if (not full) and ti == npre - 1:
                self.reset_state()
                self.barrier()
        self.K.final_wait_all_dma()
        self.K.emit()
        return self


class MixMixin:
    STATE_TAGS = ("BC", "SM", "K", "V", "X")

    def alloc_mix(self):
        sb = self.sb
        self.Cst = sb("Cst", [128, NH, 2, DV], F32)
        self.nst = sb("nst", [128, NH * 2], F32)
        self.mcol = sb("mcol", [128, NH], F32)
        self.Sst = sb("Sst", [128, 4, 512], F32)
        self.convhist = sb("convhist", [128, 20, 3], F32)
        self.mergedT = sb("mergedT", [128, NKC, TW], BF16)
        self.yzgA = sb("yzgA", [128, 4, 512], BF16)
        self.Umat = sb("Umat", [128, 128], F32)
        self.Lsm = sb("Lsm", [128, 128], F32)
        self.E127 = sb("E127", [128, 128], F32)
        self.CB = sb("CB", [128, 128], F32)
        self.onesf = sb("onesf", [128, 128], F32)
        self.smallb = sb("smallb", [128, 8, 32], F32)
        self.Arow = sb("Arow", [128, 32], F32)
        self.convwc = sb("convwc", [128, 20, 5], F32)
        self.tsm = sb("tsm", [128, 4, 256], F32)
        self.ssq = sb("ssq", [128, 16], F32)
        self.BCT = sb("BCT", [128, 2, 2, T], BF16)
        self.Btok = sb("Btok", [128, 2, 4, 128], BF16)
        self.sarea = sb("sarea", [128, 2048], F32)
        self.alloc_smp()

    def consts_mix(self):
        ps_ = self._op
        for (t, nm, pat, cmp_, fill, base, cm, init) in (
                (self.Umat, "Umat", [[1, 128]], ALU.is_ge, 0.0, 0, -1, 1.0),
                (self.Lsm, "Lsm", [[-1, 128]], ALU.is_gt, 0.0, 0, 1, 1.0),
                (self.E127, "E127", [[0, 128]], ALU.is_equal, 0.0, -127, 1, 1.0),
                (self.CB, "CB", [[-1, 128]], ALU.is_ge, NEG, 0, 1, 0.0),
        ):
            ps_("pool", "memset", [], [nm], ap=t[:], constant=init)
            ps_("pool", "affine_select", [nm], [nm], out=t[:], in_=t[:], pattern=pat, compare_op=cmp_, fill=fill, base=base,
                channel_multiplier=cm)
        ps_("pool", "memset", [], ["onesf"], ap=self.onesf[:], constant=1.0)
        self.dma("ld_c1", [], ["smallb"], out=self.smallb[:].rearrange("p a b -> p (a b)"),
                 in_=self.small.rearrange("a b -> (a b)").unsqueeze(0).to_broadcast([128, 256]))
        for j in range(5):
            self.dma(f"ld_c{2 + j}", [], ["convwc"], out=self.convwc[:, :, j], in_=self.convw[j].rearrange("(c p) -> p c", p=128), nc_ok=True)
        self.act(["smallb"], ["Arow"], out=self.Arow[:], in_=self.smallb[:, 3, :], func=AF.Exp)
        self.dve("tensor_scalar", ["Arow"], ["Arow"], out=self.Arow[:], in0=self.Arow[:], scalar1=-1.0, scalar2=None, op0=ALU.mult)
        self.dve("memset", [], ["Cst"], ap=self.Cst[:].rearrange("p a b c -> p (a b c)"), constant=0.0)
        self.dve("memset", [], ["nst"], ap=self.nst[:], constant=0.0)
        self.dve("memset", [], ["mcol"], ap=self.mcol[:], constant=NEG)
        self.dve("memset", [], ["Sst"], ap=self.Sst[:].rearrange("p a b -> p (a b)"), constant=0.0)
        self.dve("memset", [], ["convhist"], ap=self.convhist[:].rearrange("p a b -> p (a b)"), constant=0.0)

    def reset_state(self):
        fl = self.smallb[:, 5, 0:1]
        for (t, nm, ap) in ((self.Cst, "Cst", self.Cst[:].rearrange("p a b c -> p (a b c)")), (self.nst, "nst", self.nst[:]),
                            (self.Sst, "Sst", self.Sst[:].rearrange("p a b -> p (a b)")),
                            (self.convhist, "convhist", self.convhist[:].rearrange("p a b -> p (a b)"))):
            self.dve("tensor_scalar", [nm, "smallb"], [nm], out=ap, in0=ap, scalar1=fl, scalar2=None, op0=ALU.mult)
        self.dve("tensor_scalar", ["mcol", "smallb"], ["mcol"], out=self.mcol[:], in0=self.mcol[:], scalar1=fl, scalar2=self.smallb[:, 5, 1:2],
                 op0=ALU.mult, op1=ALU.add)

    def plan_mix(self, ti, full):
        for i, (tag, _) in enumerate(WIN_PLAN):
            if full or tag.startswith(self.STATE_TAGS):
                self.wplan(self.win[i], 16, 256, f"in_{tag}")
        if full:
            for b in range(8):
                self.wplan(self.wo[b], 16, 256, f"wo_{b}")

    def proj_feat(self, wt, wn, oc, b, cols=T, c0=0, bankcols=None):
        bc = bankcols if bankcols is not None else (0, cols)
        for dc in range(NKC):
            self.mm([wn, "xnT"], [f"pb{b}"], out=self.pb[b][:, bc[0]:bc[0] + cols], lhsT=wt[:, dc, oc * 128:(oc + 1) * 128],
                    rhs=self.xnT[:, dc, c0:c0 + cols], start=(dc == 0), stop=(dc == NKC - 1))

    def proj_tok(self, wt, wn, tc, b, ncols=256, wc0=0, rows=128, bc0=0):
        t0 = tc * 128
        for dc in range(NKC):
            self.mm([wn, "xnT"], [f"pb{b}"], out=self.pb[b][:rows, bc0:bc0 + ncols], lhsT=self.xnT[:, dc, t0:t0 + rows],
                    rhs=wt[:, dc, wc0:wc0 + ncols], start=(dc == 0), stop=(dc == NKC - 1))

    def conv_silu(self, xpre, nm_pre, chunk, out_ap, nm_out, width=T):
        acc = self.av(self.OFF_CONVACC, [128, T], F32)
        w = self.convwc
        self.dve("tensor_scalar", [nm_pre, "convwc"], ["convacc"], out=acc[:, 0:width], in0=xpre[:, 0:width], scalar1=w[:, chunk, 0:1],
                 scalar2=None, op0=ALU.mult)
        for j in (1, 2, 3):
            self.dve("scalar_tensor_tensor", [nm_pre, "convwc", "convacc"], ["convacc"], out=acc[:, 0:width], in0=xpre[:, j:j + width],
                     scalar=w[:, chunk, j:j + 1], in1=acc[:, 0:width], op0=ALU.mult, op1=ALU.add)
        self.act(["convacc", "convwc"], [nm_out], out=out_ap, in_=acc[:, 0:width], func=AF.Silu, bias=w[:, chunk, 4:5], scale=1.0)

    OFF_CONVACC = 0
    OFF_XPRE = 2048
    OFF_BUFY = 10304
    OFF_XDT = 18496
    OFF_LR = 22592
    OFF_M = 26688
    OFF_YT = 28736
    OFF_YI = 30784
    OFF_MISC = 32832

    def mixer(self, ti, ws, full):
        if ws and full:
            self.smp_init()
        self.rms_to_T(1, ws)
        self.mix_bc(ti, ws, full)
        self.mix_small(ti, ws, full)
        self.barrier()
        self.mix_mlstm_all(ti, ws, full)
        self.barrier()
        for u in range(4):
            self.mix_ssd(ti, u, ws, full)
        if full:
            self.mix_out(ti, ws)
        if ti == self.NT - 1:
            self.store_pstates()

    def mix_bc(self, ti, ws, full):
        for g in range(2):
            wt, wn = self.wget("in_BC%d" % g)
            for oc in range(2):
                chunk = 16 + g + 2 * oc
                b = self.bank()
                self.proj_feat(wt, wn, oc, b)
                xpre = self.av(self.OFF_XPRE, [128, T + 3], F32)
                self.dve("tensor_copy", ["convhist"], ["xpre"], out=xpre[:, 0:3], in_=self.convhist[:, chunk, :])
                self.act([f"pb{b}"], ["xpre"], out=xpre[:, 3:T + 3], in_=self.pb[b][:, 0:T], func=AF.Copy)
                self.dve("tensor_copy", ["xpre"], ["convhist"], out=self.convhist[:, chunk, :], in_=xpre[:, T:T + 3])
                if oc == 0 or full:
                    self.conv_silu(xpre, "xpre", chunk, self.BCT[:, g, oc, :], f"BCT{g}{oc}")
            if ws and full:
                self.smp_bc(g, wt, wn)
            b = self.bank()
            pt = self.pb[b][:].bitcast(BF16)
            for tc in range(4):
                self.tr([f"BCT{g}0", "ident"], [f"pb{b}"], out=pt[:, tc * 128:(tc + 1) * 128], in_=self.BCT[:, g, 0, tc * 128:(tc + 1) * 128],
                        identity=self.ident[:])
            self.dve("tensor_copy", [f"pb{b}"], [f"Btok{g}"], out=self.Btok[:, g, :, :].rearrange("p a b -> p (a b)"), in_=pt[:, 0:T])

    def mix_small(self, ti, ws, full):
        wt, wn = self.wget("in_SM")
        sm = self.tsm
        for tc in range(4):
            b = self.bank()
            self.proj_tok(wt, wn, tc, b, ncols=40)
            self.dve("tensor_copy", [f"pb{b}"], ["tsm"], out=sm[:, tc, 0:40], in_=self.pb[b][:, 0:40])
        if ws and full:
            self.smp_small(wt, wn)
        R, Wt = ["tsm", "smallb", "Arow"], ["tsm"]
        bi = self.smallb[:, 0, 0:4].unsqueeze(1).to_broadcast([128, 4, 4])
        bf = self.smallb[:, 1, 0:4].unsqueeze(1).to_broadcast([128, 4, 4])
        bdt = self.smallb[:, 2, :].unsqueeze(1).to_broadcast([128, 4, 32])
        Ab = self.Arow[:].unsqueeze(1).to_broadcast([128, 4, 32])
        self.dve("tensor_tensor", R, Wt, out=sm[:, :, 40:44], in0=sm[:, :, 0:4], in1=bi, op=ALU.add)
        z = sm[:, :, 184:188]
        self.dve("tensor_tensor", R, Wt, out=z, in0=sm[:, :, 4:8], in1=bf, op=ALU.add)
        self.softplus_neg(z, sm[:, :, 44:48], sm[:, :, 188:192], neg=True)
        z2 = sm[:, :, 188:220]
        self.dve("tensor_tensor", R, Wt, out=z2, in0=sm[:, :, 8:40], in1=bdt, op=ALU.add)
        self.softplus_neg(z2, sm[:, :, 56:88], sm[:, :, 220:252], neg=False)
        self.dve("tensor_tensor", R, Wt, out=sm[:, :, 88:120], in0=sm[:, :, 56:88], in1=Ab, op=ALU.mult)
        for tc in range(4):
            b = self.bank()
            self.mm(["tsm", "Umat"], [f"pb{b}"], out=self.pb[b][:, 0:4], lhsT=self.Umat[:], rhs=sm[:, tc, 44:48], start=True, stop=True)
            self.mm(["tsm", "Umat"], [f"pb{b}"], out=self.pb[b][:, 32:64], lhsT=self.Umat[:], rhs=sm[:, tc, 88:120], start=True, stop=True)
            self.dve("tensor_copy", [f"pb{b}"], ["tsm"], out=sm[:, tc, 48:52], in_=self.pb[b][:, 0:4])
            self.dve("tensor_copy", [f"pb{b}"], ["tsm"], out=sm[:, tc, 120:152], in_=self.pb[b][:, 32:64])
            b2 = self.bank()
            self.mm(["tsm", "E127"], [f"pb{b2}"], out=self.pb[b2][:, 0:32], lhsT=self.E127[:], rhs=sm[:, tc, 120:152], start=True, stop=True)
            self.dve("tensor_copy", [f"pb{b2}"], ["tsm"], out=sm[:, tc, 152:184], in_=self.pb[b2][:, 0:32])
        self.dve("tensor_tensor", ["tsm"], ["tsm"], out=sm[:, :, 52:56], in0=sm[:, :, 40:44], in1=sm[:, :, 48:52], op=ALU.subtract)
        self.act(["tsm"], ["tsm"], out=sm[:, :, 220:252], in_=sm[:, :, 120:152], func=AF.Exp)

    def softplus_neg(self, z, out, tmp, neg):
        R, Wt = ["tsm"], ["tsm"]
        self.dve("scalar_tensor_tensor", R, Wt, out=tmp, in0=z, scalar=-1.0, in1=z, op0=ALU.mult, op1=ALU.max)
        self.act(R, Wt, out=tmp, in_=tmp, func=AF.Exp, scale=-1.0)
        self.act(R, Wt, out=tmp, in_=tmp, func=AF.Ln, bias=self.onesf[:, 0:1], scale=1.0)
        if neg:
            self.dve("scalar_tensor_tensor", R, Wt, out=out, in0=z, scalar=0.0, in1=tmp, op0=ALU.min, op1=ALU.subtract)
        else:
            self.dve("scalar_tensor_tensor", R, Wt, out=out, in0=z, scalar=0.0, in1=tmp, op0=ALU.max, op1=ALU.add)

    ML_SET = 15360
    ML_CH = 30720

    def ml_views(self, p):
        av, o = self.av, p * self.ML_SET
        v = {"p": p}
        v["qT"] = av(o, [128, 2, T], BF16)
        v["kT"] = av(o + 2048, [128, 2, T], BF16)
        v["ktok"] = av(o + 4096, [128, 4, DK], BF16)
        v["vtok"] = av(o + 6144, [128, 4, DV], BF16)
        v["gateA"] = av(o + 10240, [128, 4, DV], BF16)
        v["sgt"] = av(o + 14336, [128, 256], F32)
        c0 = self.ML_CH
        v["Cbf"] = av(c0, [128, 2, DV], BF16)
        v["nbf"] = av(c0 + 2048, [128, 2], BF16)
        v["diagA"] = av(c0 + 2112, [128, 128], F32)
        v["Rm"] = av(c0 + 2624, [128, 128], F32)
        v["wI"] = av(c0 + 3136, [128, 128], F32)
        v["sw"] = av(c0 + 3648, [128, 128], BF16)
        v["swT"] = av(c0 + 3904, [128, 128], BF16)
        v["vw"] = av(c0 + 4160, [128, DV], BF16)
        v["hA"] = av(c0 + 5184, [128, DV], F32)
        v["yab"] = av(c0 + 7232, [128, DV], BF16)
        v["c"] = av(c0 + 8256, [128, 32], F32)
        v["wendbf"] = av(c0 + 8384, [128, 2], BF16)
        return v

    def ml_proj_gen(self, h, ws, full, p):
        v = self.ml_views(p)
        qT, kT, ktok, vtok, gateA, sgt = v["qT"], v["kT"], v["ktok"], v["vtok"], v["gateA"], v["sgt"]
        N = lambda s_: f"ml_{s_}{p}"
        if full:
            wt, wn = self.wget(f"in_Q{h}")
            for oc in range(2):
                b = self.bank()
                self.proj_feat(wt, wn, oc, b)
                self.act([f"pb{b}"], [N("qT")], out=qT[:, oc, :], in_=self.pb[b][:, 0:T], func=AF.Copy, scale=DK ** -0.5)
                yield
            if ws:
                self.smp_proj_tok(wt, wn, "qS", scale=DK ** -0.5)
        wt, wn = self.wget(f"in_K{h}")
        if full:
            for oc in range(2):
                b = self.bank()
                self.proj_feat(wt, wn, oc, b)
                self.act([f"pb{b}"], [N("kT")], out=kT[:, oc, :], in_=self.pb[b][:, 0:T], func=AF.Copy)
                yield
        if full:
            for tc in range(4):
                b = self.bank()
                pt = self.pb[b][:].bitcast(BF16)
                for kc in range(2):
                    self.tr([N("kT"), "ident"], [f"pb{b}"], out=pt[:, kc * 128:(kc + 1) * 128], in_=kT[:, kc, tc * 128:(tc + 1) * 128],
                            identity=self.ident[:])
                self.dve("tensor_copy", [f"pb{b}"], [N("ktok")], out=ktok[:, tc, :], in_=pt[:, 0:256])
                if tc % 2 == 1:
                    yield
        else:
            for tc in range(4):
                b = self.bank()
                self.proj_tok(wt, wn, tc, b)
                self.dve("tensor_copy", [f"pb{b}"], [N("ktok")], out=ktok[:, tc, :], in_=self.pb[b][:, 0:256])
                yield
        if ws and full:
            self.smp_proj_tok(wt, wn, "kS")
        for a in range(2):
            wt, wn = self.wget(f"in_V{h}{a}")
            for tc in range(4):
                b = self.bank()
                self.proj_tok(wt, wn, tc, b)
                self.act([f"pb{b}"], [N("vtok")], out=vtok[:, tc, a * 256:(a + 1) * 256], in_=self.pb[b][:, 0:256], func=AF.Copy)
                yield
            if ws and full:
                self.smp_proj_tok(wt, wn, "vS", c0=a * 256)
        if full:
            for a in range(2):
                wt, wn = self.wget(f"in_OG{h}{a}")
                for tc in range(4):
                    b = self.bank()
                    self.proj_tok(wt, wn, tc, b)
                    self.act([f"pb{b}"], [N("gate")], out=gateA[:, tc, a * 256:(a + 1) * 256], in_=self.pb[b][:, 0:256], func=AF.Sigmoid)
                    yield
                if ws:
                    self.smp_proj_tok(wt, wn, "gS", c0=a * 256, func=AF.Sigmoid)
            for a in range(2):
                wt, wn = self.wget(f"in_GA{h}{a}")
                for tc in range(4):
                    b = self.bank()
                    self.proj_tok(wt, wn, tc, b)
                    self.act([f"pb{b}"], [N("sgt")], out=sgt[:], in_=self.pb[b][:, 0:256], func=AF.Sigmoid)
                    self.dve("tensor_tensor", [N("sgt"), N("gate")], [N("gate")], out=gateA[:, tc, a * 256:(a + 1) * 256],
                             in0=gateA[:, tc, a * 256:(a + 1) * 256], in1=sgt[:], op=ALU.mult)
                    yield
                if ws:
                    self.smp_proj_tok(wt, wn, "gS", c0=a * 256, func=AF.Sigmoid, mul=True)

    def ml_chunk_gen(self, h, ws, full, p):
        v = self.ml_views(p)
        qT, kT, ktok, vtok, gateA = v["qT"], v["kT"], v["ktok"], v["vtok"], v["gateA"]
        Cbf, nbf, diagA, Rm, wI, sw, swT, vw, hA, yab, c, wendbf = (v[k] for k in
                                                                    ("Cbf", "nbf", "diagA", "Rm", "wI", "sw", "swT", "vw", "hA", "yab", "c", "wendbf"))
        N = lambda s_: f"ml_{s_}{p}"
        sm = self.tsm
        if full:
            self.act(["Cst"], ["ml_Cbf"], out=Cbf[:].rearrange("p a b -> p (a b)"), in_=self.Cst[:, h, :, :].rearrange("p a b -> p (a b)"), func=AF.Copy)
            self.dve("tensor_copy", ["nst"], ["ml_nbf"], out=nbf[:], in_=self.nst[:, 2 * h:2 * h + 2])
        mprev = self.mcol[:, h:h + 1]
        C = ["ml_c"]
        for tc in range(4):
            ts_ = slice(tc * 128, (tc + 1) * 128)
            a_ = sm[:, tc, 52 + h:53 + h]
            bcol = sm[:, tc, 48 + h:49 + h]
            self.dve("tensor_scalar", ["identf", "tsm"], ["ml_diag"], out=diagA[:], in0=self.identf[:], scalar1=a_, scalar2=None, op0=ALU.mult)
            b = self.bank()
            self.mm(["onesf", "ml_diag"], [f"pb{b}"], out=self.pb[b][:, 0:128], lhsT=self.onesf[:], rhs=diagA[:], start=True, stop=True)
            self.dve("tensor_tensor", [f"pb{b}", "CB"], ["ml_Rm"], out=Rm[:], in0=self.pb[b][:, 0:128], in1=self.CB[:], op=ALU.add)
            self.dve("tensor_reduce", ["ml_Rm"], C, out=c[:, 0:1], in_=Rm[:], axis=AX.X, op=ALU.max)
            self.dve("tensor_tensor", C + ["mcol"], C, out=c[:, 1:2], in0=c[:, 0:1], in1=mprev, op=ALU.max)
            self.dve("tensor_scalar", C, C, out=c[:, 2:3], in0=c[:, 1:2], scalar1=-1.0, scalar2=None, op0=ALU.mult)
            yield
            if full:
                self.act(["ml_Rm"] + C, ["ml_wI"], out=wI[:], in_=Rm[:], func=AF.Exp, bias=c[:, 2:3], scale=1.0)
                b = self.bank()
                for kc in range(2):
                    self.mm([N("qT"), N("kT")], [f"pb{b}"], out=self.pb[b][:, 0:128], lhsT=qT[:, kc, ts_], rhs=kT[:, kc, ts_],
                            start=(kc == 0), stop=(kc == 1))
                self.dve("memset", [], C, ap=c[:, 3:4], constant=0.0)
                self.dve("scalar_tensor_tensor", [f"pb{b}", "ml_wI"] + C, ["ml_sw"] + C, out=sw[:], in0=self.pb[b][:, 0:128], scalar=1.0,
                         in1=wI[:], op0=ALU.mult, op1=ALU.mult, accum_out=c[:, 3:4])
                yield
                b = self.bank()
                pt = self.pb[b][:].bitcast(BF16)
                self.tr(["ml_sw", "ident"], [f"pb{b}"], out=pt[:, 0:128], in_=sw[:], identity=self.ident[:])
                self.dve("tensor_copy", [f"pb{b}"], ["ml_swT"], out=swT[:], in_=pt[:, 0:128])
                bA, bB, bC = self.bank(), self.bank(), self.bank()
                for kc in range(2):
                    self.mm([N("qT"), "ml_Cbf"], [f"pb{bB}"], out=self.pb[bB][:, :], lhsT=qT[:, kc, ts_], rhs=Cbf[:, kc, :],
                            start=(kc == 0), stop=(kc == 1))
                for kc in range(2):
                    self.mm([N("qT"), "ml_nbf"], [f"pb{bC}"], out=self.pb[bC][:, 0:1], lhsT=qT[:, kc, ts_], rhs=nbf[:, kc:kc + 1],
                            start=(kc == 0), stop=(kc == 1))
                yield
                self.mm(["ml_swT", N("vtok")], [f"pb{bA}"], out=self.pb[bA][:, :], lhsT=swT[:], rhs=vtok[:, tc, :], start=True, stop=True)
                self.act(C + ["mcol"], C, out=c[:, 4:5], in_=mprev, func=AF.Exp, bias=c[:, 2:3], scale=1.0)
                self.dve("scalar_tensor_tensor", [f"pb{bC}"] + C, C, out=c[:, 5:6], in0=self.pb[bC][:, 0:1], scalar=c[:, 4:5], in1=c[:, 3:4],
                         op0=ALU.mult, op1=ALU.add)
                self.dve("tensor_tensor", C + ["tsm"], C, out=c[:, 6:7], in0=bcol, in1=c[:, 1:2], op=ALU.add)
                self.act(C, C, out=c[:, 7:8], in_=c[:, 6:7], func=AF.Exp, scale=-1.0)
                self.dve("scalar_tensor_tensor", C, C, out=c[:, 8:9], in0=c[:, 5:6], scalar=-1.0, in1=c[:, 5:6], op0=ALU.mult, op1=ALU.max)
                self.dve("tensor_tensor", C, C, out=c[:, 8:9], in0=c[:, 8:9], in1=c[:, 7:8], op=ALU.max)
                self.dve("reciprocal", C, C, out=c[:, 9:10], in_=c[:, 8:9])
                self.act([f"pb{bA}"], ["ml_hA"], out=hA[:], in_=self.pb[bA][:, :], func=AF.Copy)
                self.dve("scalar_tensor_tensor", [f"pb{bB}", "ml_hA"] + C, ["ml_hA"], out=hA[:], in0=self.pb[bB][:, :], scalar=c[:, 4:5],
                         in1=hA[:], op0=ALU.mult, op1=ALU.add)
                yield
                self.dve("memset", [], C, ap=c[:, 10:11], constant=0.0)
                self.act(["ml_hA"] + C, ["ml_yab"] + C, out=yab[:], in_=hA[:], func=AF.Square, scale=c[:, 9:10], accum_out=c[:, 10:11])
                self.dve("tensor_scalar", C, C, out=c[:, 11:12], in0=c[:, 10:11], scalar1=1.0 / DV, scalar2=EPS, op0=ALU.mult, op1=ALU.add)
                self.act(C, C, out=c[:, 11:12], in_=c[:, 11:12], func=AF.Sqrt)
                self.dve("reciprocal", C, C, out=c[:, 11:12], in_=c[:, 11:12])
                self.dve("tensor_tensor", C, C, out=c[:, 12:13], in0=c[:, 11:12], in1=c[:, 9:10], op=ALU.mult)
                self.dve("scalar_tensor_tensor", ["ml_hA", N("gate")] + C, ["ml_yab"], out=yab[:], in0=hA[:], scalar=c[:, 12:13],
                         in1=gateA[:, tc, :], op0=ALU.mult, op1=ALU.mult)
                yield
                b = self.bank()
                pt = self.pb[b][:].bitcast(BF16)
                for j in range(4):
                    self.tr(["ml_yab", "ident"], [f"pb{b}"], out=pt[:, j * 128:(j + 1) * 128], in_=yab[:, j * 128:(j + 1) * 128],
                            identity=self.ident[:])
                for j in range(4):
                    self.dve("tensor_scalar", [f"pb{b}", "wcols"], ["mergedT"], out=self.mergedT[:, h * 4 + j, ts_], in0=pt[:, j * 128:(j + 1) * 128],
                             scalar1=self.wcols[:, 3, h * 4 + j:h * 4 + j + 1], scalar2=None, op0=ALU.mult)
            b = self.bank()
            self.mm(["E127"] + C, [f"pb{b}"], out=self.pb[b][:, 0:2], lhsT=self.E127[:], rhs=c[:, 1:3], start=True, stop=True)
            self.mm(["E127", "tsm"], [f"pb{b}"], out=self.pb[b][:, 2:4], lhsT=self.E127[:], rhs=sm[:, tc, 48 + h:50 + h], start=True, stop=True)
            self.dve("tensor_copy", [f"pb{b}"], C, out=c[:, 14:18], in_=self.pb[b][:, 0:4])
            self.act(C + ["tsm"], C, out=c[:, 17:18], in_=a_, func=AF.Exp, bias=c[:, 15:16], scale=1.0)
            self.act(C + ["mcol"], C, out=c[:, 18:19], in_=mprev, func=AF.Exp, bias=c[:, 15:16], scale=1.0)
            self.dve("tensor_scalar", [N("vtok")] + C, ["ml_vw"], out=vw[:], in0=vtok[:, tc, :], scalar1=c[:, 17:18], scalar2=None, op0=ALU.mult)
            self.dve("tensor_copy", C, ["ml_wendbf"], out=wendbf[:, 0:1], in_=c[:, 17:18])
            yield
            for kc in range(2):
                b = self.bank()
                self.mm([N("ktok"), "ml_vw"], [f"pb{b}"], out=self.pb[b][:, :], lhsT=ktok[:, tc, kc * 128:(kc + 1) * 128], rhs=vw[:],
                        start=True, stop=True)
                self.dve("scalar_tensor_tensor", [f"pb{b}", "Cst"] + C, ["Cst"], out=self.Cst[:, h, kc, :], in0=self.Cst[:, h, kc, :],
                         scalar=c[:, 18:19], in1=self.pb[b][:, :], op0=ALU.mult, op1=ALU.add)
            b = self.bank()
            for kc in range(2):
                self.mm([N("ktok"), "ml_wendbf"], [f"pb{b}"], out=self.pb[b][:, 8 * kc:8 * kc + 1], lhsT=ktok[:, tc, kc * 128:(kc + 1) * 128],
                        rhs=wendbf[:, 0:1], start=True, stop=True)
            for kc in range(2):
                self.dve("scalar_tensor_tensor", [f"pb{b}", "nst"] + C, ["nst"], out=self.nst[:, 2 * h + kc:2 * h + kc + 1],
                         in0=self.nst[:, 2 * h + kc:2 * h + kc + 1], scalar=c[:, 18:19], in1=self.pb[b][:, 8 * kc:8 * kc + 1],
                         op0=ALU.mult, op1=ALU.add)
            self.dve("tensor_tensor", C, ["mcol"], out=self.mcol[:, h:h + 1], in0=c[:, 14:15], in1=c[:, 16:17], op=ALU.add)
            if full and tc < 3:
                self.act(["Cst"], ["ml_Cbf"], out=Cbf[:].rearrange("p a b -> p (a b)"), in_=self.Cst[:, h, :, :].rearrange("p a b -> p (a b)"),
                         func=AF.Copy)
                self.dve("tensor_copy", ["nst"], ["ml_nbf"], out=nbf[:], in_=self.nst[:, 2 * h:2 * h + 2])
            yield

    @staticmethod
    def interleave(main, filler, ratio=2):
        fdone = filler is None
        for _ in main:
            for _k in range(ratio):
                if not fdone:
                    try:
                        next(filler)
                    except StopIteration:
                        fdone = True
        if not fdone:
            for _ in filler:
                pass

    def mix_mlstm_all(self, ti, ws, full):
        if ws and full:
            for h in range(NH):
                for _ in self.ml_proj_gen(h, ws, full, 0):
                    pass
                self.interleave(self.ml_chunk_gen(h, ws, full, 0), self.smp_mlstm(h), ratio=1)
            return
        for _ in self.ml_proj_gen(0, ws, full, 0):
            pass
        for h in range(NH):
            nxt = self.ml_proj_gen(h + 1, ws, full, (h + 1) % 2) if h + 1 < NH else None
            self.interleave(self.ml_chunk_gen(h, ws, full, h % 2), nxt, ratio=self.cfg.get("ml_ratio", 1))

    def mix_ssd(self, ti, u, ws, full):
        av = self.av
        g = u // 2
        sm = self.tsm
        xpre = av(self.OFF_XPRE, [128, 4, T + 3], F32)
        xtok = av(self.OFF_XPRE, [128, 4, T], F32)
        bufY = av(self.OFF_BUFY, [128, 4, T], F32)
        xdt = av(self.OFF_XDT, [128, 4, T], BF16)
        Lr = av(self.OFF_LR, [128, 8, 128], F32)
        M = av(self.OFF_M, [128, 8, 128], BF16)
        yt = av(self.OFF_YT, [128, T], F32)
        yi = av(self.OFF_YI, [128, T], F32)
        STbf = av(self.OFF_MISC, [128, T], BF16)
        wend = av(self.OFF_MISC + 1024, [128, 8], F32)
        decb = av(self.OFF_MISC + 1056, [128, 8], F32)
        tcol = av(self.OFF_MISC + 1088, [128, 8], F32)
        xw = av(self.OFF_MISC + 1216, [128, T], BF16)
        zt = av(self.OFF_MISC + 2240, [128, 256], F32)
        BX, BY = "ssd_bufX", "ssd_bufY"
        for a in range(2):
            wt, wn = self.wget(f"in_X{u}{a}")
            for oc in range(2):
                j = 2 * a + oc
                chunk = 4 * u + j
                b = self.bank()
                self.proj_feat(wt, wn, oc, b)
                self.dve("tensor_copy", ["convhist"], [BX], out=xpre[:, j, 0:3], in_=self.convhist[:, chunk, :])
                self.act([f"pb{b}"], [BX], out=xpre[:, j, 3:T + 3], in_=self.pb[b][:, 0:T], func=AF.Copy)
                self.dve("tensor_copy", [BX], ["convhist"], out=self.convhist[:, chunk, :], in_=xpre[:, j, T:T + 3])
                self.conv_silu(xpre[:, j, :], BX, chunk, bufY[:, j, :], BY)
            if ws and full:
                self.smp_x(u, a, wt, wn)
        for tc in range(4):
            b = self.bank()
            for j in range(4):
                self.tr([BY, "identf"], [f"pb{b}"], out=self.pb[b][:, j * 128:(j + 1) * 128], in_=bufY[:, j, tc * 128:(tc + 1) * 128],
                        identity=self.identf[:])
            self.act([f"pb{b}"], [BX], out=xtok[:, tc, :], in_=self.pb[b][:, :], func=AF.Copy)
        dtb = sm[:, :, 56 + 8 * u:64 + 8 * u].unsqueeze(3).to_broadcast([128, 4, 8, HP])
        self.dve("tensor_tensor", [BX, "tsm"], ["ssd_xdt"], out=xdt.rearrange("p a (h q) -> p a h q", q=HP),
                 in0=xtok.rearrange("p a (h q) -> p a h q", q=HP), in1=dtb, op=ALU.mult)
        self.cbTm = av(self.OFF_CONVACC, [128, 4, 128], F32)
        if full:
            for tc in range(4):
                ts_ = slice(tc * 128, (tc + 1) * 128)
                b = self.bank()
                self.mm([f"BCT{g}0", f"BCT{g}1"], [f"pb{b}"], out=self.pb[b][:, 0:128], lhsT=self.BCT[:, g, 0, ts_], rhs=self.BCT[:, g, 1, ts_],
                        start=True, stop=True)
                self.dve("tensor_tensor", [f"pb{b}", "Umat"], ["convacc"], out=self.cbTm[:, tc, :], in0=self.pb[b][:, 0:128], in1=self.Umat[:], op=ALU.mult)
        if full and u % 2 == 0:
            self.dve("memset", [], ["ssq"], ap=self.ssq[:, 0:8], constant=0.0)
        if full:
            self.act(["Sst"], ["ssd_STbf"], out=STbf[:], in_=self.Sst[:, u, :], func=AF.Copy)
        self.interleave(self.ssd_chunk_gen(u, full, locals()), self.smp_ssd(u) if (ws and full) else None, ratio=self.cfg.get("ssd_ratio", 2))
        if not full:
            return
        self.ssd_gates(u, ws, locals())

    def ssd_chunk_gen(self, u, full, L):
        g, sm, BX, BY = L["g"], L["sm"], L["BX"], L["BY"]
        Lr, M, yt, yi, STbf, wend, decb, xw, xdt, xtok, bufY = (L[k] for k in ("Lr", "M", "yt", "yi", "STbf", "wend", "decb", "xw", "xdt", "xtok", "bufY"))
        for tc in range(4):
            ts_ = slice(tc * 128, (tc + 1) * 128)
            if full:
                self.dve("tensor_tensor", ["Umat", "tsm"], ["ssd_Lr"], out=Lr[:], in0=self.Umat[:].unsqueeze(1).to_broadcast([128, 8, 128]),
                         in1=sm[:, tc, 88 + 8 * u:96 + 8 * u].unsqueeze(2).to_broadcast([128, 8, 128]), op=ALU.mult)
                for q in range(2):
                    b = self.bank()
                    self.mm(["Lsm", "ssd_Lr"], [f"pb{b}"], out=self.pb[b][:, :], lhsT=self.Lsm[:],
                            rhs=Lr[:, 4 * q:4 * q + 4, :].rearrange("p a b -> p (a b)"), start=True, stop=True)
                    self.act([f"pb{b}"], ["ssd_M"], out=M[:, 4 * q:4 * q + 4, :].rearrange("p a b -> p (a b)"), in_=self.pb[b][:, :], func=AF.Exp)
                self.dve("tensor_tensor", ["ssd_M", "convacc"], ["ssd_M"], out=M[:], in0=M[:],
                         in1=self.cbTm[:, tc, :].unsqueeze(1).to_broadcast([128, 8, 128]), op=ALU.mult)
                yield
                bI, bE = self.bank(), self.bank()
                for hh in range(8):
                    self.mm(["ssd_M", "ssd_xdt"], [f"pb{bI}"], out=self.pb[bI][:, hh * HP:(hh + 1) * HP], lhsT=M[:, hh, :],
                            rhs=xdt[:, tc, hh * HP:(hh + 1) * HP], start=True, stop=True)
                self.mm([f"BCT{g}1", "ssd_STbf"], [f"pb{bE}"], out=self.pb[bE][:, :], lhsT=self.BCT[:, g, 1, ts_], rhs=STbf[:], start=True, stop=True)
                ebb = sm[:, tc, 220 + 8 * u:228 + 8 * u].unsqueeze(2).to_broadcast([128, 8, HP])
                self.dve("tensor_tensor", [f"pb{bE}", "tsm"], ["ssd_yi"], out=yi.rearrange("p (h q) -> p h q", q=HP),
                         in0=self.pb[bE][:, :].rearrange("p (h q) -> p h q", q=HP), in1=ebb, op=ALU.mult)
                self.dve("tensor_tensor", [f"pb{bI}", "ssd_yi"], ["ssd_yt"], out=yt[:], in0=self.pb[bI][:, :], in1=yi[:], op=ALU.add)
                Db = self.smallb[:, 4, 8 * u:8 * u + 8].unsqueeze(2).to_broadcast([128, 8, HP])
                self.dve("tensor_tensor", [BX, "smallb"], ["ssd_yi"], out=yi.rearrange("p (h q) -> p h q", q=HP),
                         in0=xtok[:, tc, :].rearrange("p (h q) -> p h q", q=HP), in1=Db, op=ALU.mult)
                self.dve("tensor_tensor", ["ssd_yt", "ssd_yi"], [BY], out=bufY[:, tc, :], in0=yt[:], in1=yi[:], op=ALU.add)
                yield
            self.dve("tensor_tensor", ["tsm"], ["ssd_wend"], out=wend[:], in0=sm[:, tc, 152 + 8 * u:160 + 8 * u],
                     in1=sm[:, tc, 120 + 8 * u:128 + 8 * u], op=ALU.subtract)
            self.act(["ssd_wend"], ["ssd_wend"], out=wend[:], in_=wend[:], func=AF.Exp)
            self.dve("tensor_tensor", ["ssd_xdt", "ssd_wend"], ["ssd_xw"], out=xw.rearrange("p (h q) -> p h q", q=HP),
                     in0=xdt[:, tc, :].rearrange("p (h q) -> p h q", q=HP), in1=wend[:].unsqueeze(2).to_broadcast([128, 8, HP]), op=ALU.mult)
            bS = self.bank()
            self.mm([f"Btok{g}", "ssd_xw"], [f"pb{bS}"], out=self.pb[bS][:, :], lhsT=self.Btok[:, g, tc, :], rhs=xw[:], start=True, stop=True)
            self.act(["tsm"], ["ssd_decb"], out=decb[:], in_=sm[:, tc, 152 + 8 * u:160 + 8 * u], func=AF.Exp)
            Sv = self.Sst[:, u, :].rearrange("p (h q) -> p h q", q=HP)
            self.dve("tensor_tensor", ["Sst", "ssd_decb"], ["Sst"], out=Sv, in0=Sv, in1=decb[:].unsqueeze(2).to_broadcast([128, 8, HP]), op=ALU.mult)
            self.dve("tensor_tensor", ["Sst", f"pb{bS}"], ["Sst"], out=self.Sst[:, u, :], in0=self.Sst[:, u, :], in1=self.pb[bS][:, :], op=ALU.add)
            if full and tc < 3:
                self.act(["Sst"], ["ssd_STbf"], out=STbf[:], in_=self.Sst[:, u, :], func=AF.Copy)
            yield

    def ssd_gates(self, u, ws, L):
        av = self.av
        g, BX, BY = L["g"], L["BX"], L["BY"]
        bufY, xdt, zt, tcol = L["bufY"], L["xdt"], L["zt"], L["tcol"]
        for a in range(2):
            wt, wn = self.wget(f"in_Z{u}{a}")
            for tc in range(4):
                b = self.bank()
                self.proj_tok(wt, wn, tc, b)
                self.act([f"pb{b}"], ["ssd_zt"], out=zt[:], in_=self.pb[b][:, 0:256], func=AF.Silu)
                self.dve("memset", [], ["ssd_tcol"], ap=tcol[:, 0:1], constant=0.0)
                ysl = bufY[:, tc, a * 256:(a + 1) * 256]
                self.dve("scalar_tensor_tensor", [BY, "ssd_zt", "ssd_tcol"], [BY, "ssd_tcol"], out=ysl, in0=ysl, scalar=1.0, in1=zt[:],
                         op0=ALU.mult, op1=ALU.mult)
                self.act([BY, "ssd_tcol"], ["ssd_zt", "ssd_tcol"], out=zt[:], in_=ysl, func=AF.Square, accum_out=tcol[:, 0:1])
                self.dve("tensor_tensor", ["ssq", "ssd_tcol"], ["ssq"], out=self.ssq[:, tc:tc + 1], in0=self.ssq[:, tc:tc + 1], in1=tcol[:, 0:1], op=ALU.add)
            if ws:
                self.smp_feat(u, a, wt, wn, "zS", AF.Silu)
        yz2 = self.yzgA if u % 2 == 0 else xdt
        yzn = "yzgA" if u % 2 == 0 else "ssd_xdt"
        for a in range(2):
            wt, wn = self.wget(f"in_GB{u}{a}")
            for tc in range(4):
                b = self.bank()
                self.proj_tok(wt, wn, tc, b)
                self.act([f"pb{b}"], ["ssd_zt"], out=zt[:], in_=self.pb[b][:, 0:256], func=AF.Sigmoid)
                self.dve("tensor_tensor", [BY, "ssd_zt"], [yzn], out=yz2[:, tc, a * 256:(a + 1) * 256], in0=bufY[:, tc, a * 256:(a + 1) * 256],
                         in1=zt[:], op=ALU.mult)
            if ws:
                self.smp_feat(u, a, wt, wn, "gbS", AF.Sigmoid)
        if u % 2 == 1:
            sc = av(self.OFF_M, [128, T], BF16)
            for tc in range(4):
                ts_ = slice(tc * 128, (tc + 1) * 128)
                r = self.ssq[:, 8 + tc:9 + tc]
                self.dve("tensor_scalar", ["ssq"], ["ssq"], out=r, in0=self.ssq[:, tc:tc + 1], scalar1=1.0 / 1024, scalar2=EPS, op0=ALU.mult, op1=ALU.add)
                self.act(["ssq"], ["ssq"], out=r, in_=r, func=AF.Sqrt)
                self.dve("reciprocal", ["ssq"], ["ssq"], out=r, in_=r)
                for (src, srcn, uh) in ((self.yzgA, "yzgA", u - 1), (xdt, "ssd_xdt", u)):
                    self.dve("tensor_scalar", [srcn, "ssq"], ["ssd_M"], out=sc[:], in0=src[:, tc, :], scalar1=r, scalar2=None, op0=ALU.mult)
                    b = self.bank()
                    pt = self.pb[b][:].bitcast(BF16)
                    for j in range(4):
                        self.tr(["ssd_M", "ident"], [f"pb{b}"], out=pt[:, j * 128:(j + 1) * 128], in_=sc[:, j * 128:(j + 1) * 128], identity=self.ident[:])
                    for j in range(4):
                        ch = uh * 4 + j
                        self.dve("scalar_tensor_tensor", [f"pb{b}", "wcols", "mergedT"], ["mergedT"], out=self.mergedT[:, ch, ts_],
                                 in0=pt[:, j * 128:(j + 1) * 128], scalar=self.wcols[:, 4, ch:ch + 1], in1=self.mergedT[:, ch, ts_],
                                 op0=ALU.mult, op1=ALU.add)

    def mix_out(self, ti, ws):
        chunks = [0, 1, 2, 3] + ([4] if ws else [])
        for bo in range(8):
            wt, wn = self.wget(f"wo_{bo}")
            for tc in chunks:
                rows = 128 if tc < 4 else NS
                b = self.bank()
                for dc in range(NKC):
                    self.mm([wn, "mergedT"], [f"pb{b}"], out=self.pb[b][:rows, 0:256], lhsT=self.mergedT[:, dc, tc * 128:tc * 128 + rows],
                            rhs=wt[:, dc, :], start=(dc == 0), stop=(dc == NKC - 1))
                xs = self.xres[:rows, tc, bo * 256:(bo + 1) * 256]
                self.dve("tensor_tensor", [f"pb{b}", f"xres{tc}"], [f"xres{tc}"], out=xs, in0=xs, in1=self.pb[b][:rows, 0:256], op=ALU.add)

    def store_pstates(self):
        self.dma("st_pC", ["Cst"], [], out=self.o_pC.rearrange("h k v -> (h k) v").rearrange("(hk p) v -> p hk v", p=128),
                 in_=self.Cst[:].rearrange("p a b c -> p (a b) c"))
        self.dma("st_pn", ["nst"], [], out=self.o_pn.rearrange("(hk p) -> p hk", p=128), in_=self.nst[:], nc_ok=True)
        self.dma("st_pm", ["mcol"], [], out=self.o_pm.rearrange("(o h) -> o h", o=1), in_=self.mcol[0:1, :])
        for j in range(3):
            self.dma(f"st_pc{j}", ["convhist"], [], out=self.o_pconv[j].rearrange("(c p) -> p c", p=128), in_=self.convhist[:, :, j], nc_ok=True)
        stage = self.av(self.OFF_YT, [128, 4, 128], F32)
        for u in range(4):
            b = self.bank()
            for q in range(4):
                self.tr(["Sst", "identf"], [f"pb{b}"], out=self.pb[b][:, q * 128:(q + 1) * 128], in_=self.Sst[:, u, q * 128:(q + 1) * 128],
                        identity=self.identf[:])
            self.act([f"pb{b}"], ["ssd_yt"], out=stage.rearrange("p a b -> p (a b)"), in_=self.pb[b][:, :], func=AF.Copy)
            self.dma("st_pS", ["ssd_yt"], [], out=self.o_pS[u * 512:(u + 1) * 512, :].rearrange("(q p) n -> p q n", p=128), in_=stage)

    SOFF = 15360

    def alloc_smp(self):
        sb = self.sb
        self.smS = sb("smS", [NS, 64], F32)
        self.xsT = sb("xsT", [128, 20, NS], F32)
        self.dtT = sb("dtT", [128, 2, NKC, NS], F32)
        self.histT = sb("histT", [128, 20, NS * 3], F32)
        self.BCs = sb("BCs", [NS, 2, 256], BF16)
        self.ssqS = sb("ssqS", [NS, 8], F32)
        self.ones16b = sb("ones16b", [NS, 128], BF16)

    def smp_views(self):
        av, o = self.av, self.SOFF
        v = {}
        v["qS"] = av(o, [NS, DK], F32)
        v["kS"] = av(o + 1024, [NS, DK], F32)
        v["vS"] = av(o + 2048, [NS, DV], F32)
        v["gS"] = av(o + 4096, [NS, DV], F32)
        v["kwbf"] = av(o + 6144, [NS, DK], BF16)
        v["kmask"] = av(o + 6656, [NS, DK], BF16)
        v["vSbf"] = av(o + 7168, [NS, DV], BF16)
        v["qTs"] = av(o + 8192, [128, 2, NS], F32)
        v["qmask"] = av(o + 8320, [128, 2, NS * NS], F32)
        v["decbc"] = av(o + 10368, [128, NS], F32)
        v["nS"] = av(o + 10432, [NS, DK], F32)
        v["hS"] = av(o + 11456, [NS, DV], F32)
        v["yaS"] = av(o + 13504, [NS, DV], BF16)
        v["cS"] = av(40192, [NS, 32], F32)
        v["tmpS"] = av(39168, [NS, 256], F32)
        return v

    def smp_init(self):
        self._op("pool", "memset", [], ["ones16b"], ap=self.ones16b[:], constant=1.0)
        stg = self.av(24576, [NS * 3, XBC], F32)
        self.dma("ld_sst", [], ["s_stg"], out=stg, in_=self.st_conv[:, :])
        for c0 in range(0, 20, 8):
            c1 = min(20, c0 + 8)
            b = self.bank()
            for ch in range(c0, c1):
                self.tr(["s_stg", "identf"], [f"pb{b}"], out=self.pb[b][:, (ch - c0) * 48:(ch - c0 + 1) * 48], in_=stg[:, ch * 128:(ch + 1) * 128],
                        identity=self.identf[:NS * 3, :NS * 3])
            self.dve("tensor_copy", [f"pb{b}"], ["histT"], out=self.histT[:, c0:c1, :].rearrange("p a b -> p (a b)"), in_=self.pb[b][:, 0:(c1 - c0) * 48])
        src = self.st_conv.rearrange("(b j) c -> b j c", j=3)
        self.dma("st_sc0", [], [], out=self.o_sconv[:, 0:2, :], in_=src[:, 1:3, :])
        self.dve("memset", [], ["ssqS"], ap=self.ssqS[:], constant=0.0)
        self.barrier()

    def smp_proj_tok(self, wt, wn, name, c0=0, scale=None, func=None, mul=False):
        v = self.smp_views()
        P7 = ["pb7g0", "pb7g1", "pb7u0", "pb7u1"]
        t0 = T
        for dc in range(NKC):
            self.mm([wn, "xnT"], P7, out=self.pb[7][:NS, 0:256], lhsT=self.xnT[:, dc, t0:t0 + NS], rhs=wt[:, dc, 0:256],
                    start=(dc == 0), stop=(dc == NKC - 1))
        dst = v[name][:, c0:c0 + 256]
        kw = {}
        if scale is not None:
            kw["scale"] = scale
        if not mul:
            self.act(P7, ["s_" + name], out=dst, in_=self.pb[7][:NS, 0:256], func=func or AF.Copy, **kw)
        else:
            self.act(P7, ["s_tmpS"], out=v["tmpS"][:], in_=self.pb[7][:NS, 0:256], func=func or AF.Copy, **kw)
            self.dve("tensor_tensor", ["s_tmpS", "s_" + name], ["s_" + name], out=dst, in0=dst, in1=v["tmpS"][:], op=ALU.mult)

    def smp_small(self, wt, wn):
        P7 = ["pb7g0", "pb7g1", "pb7u0", "pb7u1"]
        S = self.smS
        for dc in range(NKC):
            self.mm([wn, "xnT"], P7, out=self.pb[7][:NS, 0:40], lhsT=self.xnT[:, dc, T:TW], rhs=wt[:, dc, 0:40], start=(dc == 0), stop=(dc == NKC - 1))
        R, W = ["smS", "smallb"], ["smS"]
        self.dve("tensor_copy", P7, W, out=S[:, 0:40], in_=self.pb[7][:NS, 0:40])
        self.dma("ld_sm", [], ["smS"], out=S[:, 48:52], in_=self.st_m[:, :])
        self.dve("tensor_tensor", R, W, out=S[:, 40:44], in0=S[:, 0:4], in1=self.smallb[:NS, 0, 0:4], op=ALU.add)
        z = S[:, 56:60]
        self.dve("tensor_tensor", R, W, out=z, in0=S[:, 4:8], in1=self.smallb[:NS, 1, 0:4], op=ALU.add)
        self._softplus(z, S[:, 44:48], S[:, 60:64], True, "smS", NS)
        self.dve("tensor_tensor", R, W, out=S[:, 56:60], in0=S[:, 44:48], in1=S[:, 48:52], op=ALU.add)
        self.dve("tensor_tensor", R, W, out=S[:, 52:56], in0=S[:, 56:60], in1=S[:, 40:44], op=ALU.max)
        self.dma("st_sm", ["smS"], [], out=self.o_sm[:, :], in_=S[:, 52:56])
        self.dve("tensor_tensor", R, W, out=S[:, 60:64], in0=S[:, 56:60], in1=S[:, 52:56], op=ALU.subtract)
        self.act(R, W, out=S[:, 60:64], in_=S[:, 60:64], func=AF.Exp)
        self.dve("tensor_tensor", R, W, out=S[:, 56:60], in0=S[:, 40:44], in1=S[:, 52:56], op=ALU.subtract)
        self.act(R, W, out=S[:, 56:60], in_=S[:, 56:60], func=AF.Exp)
        dtx = self.av(self.SOFF, [NS, 2, SH, HP], F32)
        tmp = self.av(self.SOFF + 16384 - 512, [NS, 96], F32)
        self.dve("tensor_tensor", R, ["s_tmp96"], out=tmp[:, 0:32], in0=S[:, 8:40], in1=self.smallb[:NS, 2, :], op=ALU.add)
        self._softplus(tmp[:, 0:32], tmp[:, 32:64], tmp[:, 64:96], False, "s_tmp96", NS)
        self.dve("tensor_tensor", ["s_tmp96", "Arow"], ["s_tmp96"], out=tmp[:, 64:96], in0=tmp[:, 32:64], in1=self.Arow[:NS, :], op=ALU.mult)
        self.act(["s_tmp96"], ["s_tmp96"], out=tmp[:, 64:96], in_=tmp[:, 64:96], func=AF.Exp)
        for i in range(2):
            self.dve("tensor_copy", ["s_tmp96"], ["s_dtx"], out=dtx[:, i, :, :], in_=tmp[:, 32 + 32 * i:64 + 32 * i].unsqueeze(2).to_broadcast([NS, SH, HP]))
            b = self.bank()
            flat = dtx[:, i, :, :].rearrange("p a b -> p (a b)")
            for ch in range(NKC):
                self.tr(["s_dtx", "identf"], [f"pb{b}"], out=self.pb[b][:, ch * NS:(ch + 1) * NS], in_=flat[:, ch * 128:(ch + 1) * 128],
                        identity=self.identf[:NS, :NS])
            self.dve("tensor_copy", [f"pb{b}"], ["dtT"], out=self.dtT[:, i, :, :].rearrange("p a b -> p (a b)"), in_=self.pb[b][:, 0:NKC * NS])

    def _softplus(self, z, out, tmp, neg, nm, rows):
        R, Wt = [nm], [nm]
        self.dve("scalar_tensor_tensor", R, Wt, out=tmp, in0=z, scalar=-1.0, in1=z, op0=ALU.mult, op1=ALU.max)
        self.act(R, Wt, out=tmp, in_=tmp, func=AF.Exp, scale=-1.0)
        self.act(R + ["onesf"], Wt, out=tmp, in_=tmp, func=AF.Ln, bias=self.onesf[:rows, 0:1], scale=1.0)
        if neg:
            self.dve("scalar_tensor_tensor", R, Wt, out=out, in0=z, scalar=0.0, in1=tmp, op0=ALU.min, op1=ALU.subtract)
        else:
            self.dve("scalar_tensor_tensor", R, Wt, out=out, in0=z, scalar=0.0, in1=tmp, op0=ALU.max, op1=ALU.add)

    def smp_mlstm(self, h):
        v = self.smp_views()
        S = self.smS
        P7 = ["pb7g0", "pb7g1", "pb7u0", "pb7u1"]
        qS, kS, vS, gS, nS, hS, cS = v["qS"], v["kS"], v["vS"], v["gS"], v["nS"], v["hS"], v["cS"]
        wend, dec, mnew = S[:, 56 + h:57 + h], S[:, 60 + h:61 + h], S[:, 52 + h:53 + h]
        self.dma("ld_sn", [], ["s_nS", "s_dtx"], out=nS[:], in_=self.st_n[:, h * DK:(h + 1) * DK])
        self.dve("tensor_scalar", ["s_nS", "smS"], ["s_nS"], out=nS[:], in0=nS[:], scalar1=dec, scalar2=None, op0=ALU.mult)
        self.dve("scalar_tensor_tensor", ["s_kS", "smS", "s_nS"], ["s_nS"], out=nS[:], in0=kS[:], scalar=wend, in1=nS[:], op0=ALU.mult, op1=ALU.add)
        self.dma("st_sn", ["s_nS"], [], out=self.o_sn[:, h * DK:(h + 1) * DK], in_=nS[:])
        self.dve("memset", [], ["s_cS"], ap=cS[:, 0:1], constant=0.0)
        self.dve("scalar_tensor_tensor", ["s_qS", "s_nS", "s_cS"], ["s_tmpS", "s_cS"], out=v["tmpS"][:], in0=qS[:], scalar=1.0, in1=nS[:],
                 op0=ALU.mult, op1=ALU.mult, accum_out=cS[:, 0:1])
        self.dve("tensor_scalar", ["s_kS", "smS"], ["s_kwbf"], out=v["kwbf"][:], in0=kS[:], scalar1=wend, scalar2=None, op0=ALU.mult)
        self.dve("tensor_copy", ["s_vS"], ["s_vSbf"], out=v["vSbf"][:], in_=vS[:])
        self.dve("tensor_scalar", ["identf", "smS"], ["s_cS"], out=cS[:, 16:32], in0=self.identf[:NS, :NS], scalar1=dec, scalar2=None, op0=ALU.mult)
        self.mm(["onesf", "s_cS"], P7, out=self.pb[7][:, 0:NS], lhsT=self.onesf[:NS, :], rhs=cS[:, 16:32], start=True, stop=True)
        self.dve("tensor_copy", P7, ["s_decbc"], out=v["decbc"][:], in_=self.pb[7][:, 0:NS])
        for kc in range(2):
            self.tr(["s_qS", "identf"], P7, out=self.pb[7][:, 32 + kc * NS:32 + (kc + 1) * NS], in_=qS[:, kc * 128:(kc + 1) * 128], identity=self.identf[:NS, :NS])
        self.dve("tensor_copy", P7, ["s_qTs"], out=v["qTs"][:].rearrange("p a b -> p (a b)"), in_=self.pb[7][:, 32:32 + 2 * NS])
        self.dve("memset", [], ["s_qmask"], ap=v["qmask"][:].rearrange("p a b -> p (a b)"), constant=0.0)
        self.dve("tensor_copy", ["s_qTs", "s_qmask"], ["s_qmask"], out=v["qmask"][:, :, 0:NS * NS:NS + 1], in_=v["qTs"][:])
        kmask2 = self.av(self.SOFF + 14528, [NS, DK], BF16)
        KB = (5, 6)

        def kmm(j, kc):
            km, kmn = (v["kmask"], "s_kmask0") if j % 2 == 0 else (kmask2, "s_kmask1")
            if kc == 0:
                self.dve("tensor_scalar", ["s_kwbf", "identf"], [kmn], out=km[:], in0=v["kwbf"][:], scalar1=self.identf[:NS, j:j + 1],
                         scalar2=None, op0=ALU.mult)
            self.mm([kmn, "s_vSbf"], [f"pb{KB[kc]}"], out=self.pb[KB[kc]][:, :], lhsT=km[:, kc * 128:(kc + 1) * 128], rhs=v["vSbf"][:],
                    start=True, stop=True)

        def _ldC(jj):
            Cb_ = self.sarea[:, (jj % 2) * 1024:(jj % 2 + 1) * 1024].rearrange("p (a b) -> p a b", b=DV)
            self.dma(f"ld_C{jj % 2}", [], [f"s_C{jj % 2}"], out=Cb_, in_=self.st_C[jj, h].rearrange("(kc p) v -> p kc v", p=128))
        self.bank_mod = 5
        self.bank_rr %= 5
        _ldC(0)
        kmm(0, 0)
        kmm(0, 1)
        for j in range(NS):
            Cb = self.sarea[:, (j % 2) * 1024:(j % 2 + 1) * 1024].rearrange("p (a b) -> p a b", b=DV)
            cn = f"s_C{j % 2}"
            if j + 1 < NS:
                _ldC(j + 1)
            for kc in range(2):
                b = KB[kc]
                self.dve("scalar_tensor_tensor", [f"pb{b}", cn, "s_decbc"], [cn], out=Cb[:, kc, :], in0=Cb[:, kc, :], scalar=v["decbc"][:, j:j + 1],
                         in1=self.pb[b][:, :], op0=ALU.mult, op1=ALU.add)
                if j + 1 < NS:
                    kmm(j + 1, kc)
            self.dma(f"st_C{j % 2}", [cn], [], out=self.o_sC[j, h].rearrange("(kc p) v -> p kc v", p=128), in_=Cb)
            for kc in range(2):
                self.mm(["s_qmask", cn], P7, out=self.pb[7][:NS, :], lhsT=v["qmask"][:, kc, j * NS:(j + 1) * NS], rhs=Cb[:, kc, :],
                        start=(j == 0 and kc == 0), stop=(j == NS - 1 and kc == 1))
            yield
        self.bank_mod = 7
        C = ["s_cS", "smS"]
        self.act(C, ["s_cS"], out=cS[:, 1:2], in_=mnew, func=AF.Exp, scale=-1.0)
        self.dve("scalar_tensor_tensor", C, ["s_cS"], out=cS[:, 2:3], in0=cS[:, 0:1], scalar=-1.0, in1=cS[:, 0:1], op0=ALU.mult, op1=ALU.max)
        self.dve("tensor_tensor", C, ["s_cS"], out=cS[:, 2:3], in0=cS[:, 2:3], in1=cS[:, 1:2], op=ALU.max)
        self.dve("reciprocal", C, ["s_cS"], out=cS[:, 3:4], in_=cS[:, 2:3])
        self.act(P7, ["s_hS"], out=hS[:], in_=self.pb[7][:NS, :], func=AF.Copy)
        self.dve("memset", [], ["s_cS"], ap=cS[:, 4:5], constant=0.0)
        self.act(["s_hS"] + C, ["s_yaS", "s_cS"], out=v["yaS"][:], in_=hS[:], func=AF.Square, scale=cS[:, 3:4], accum_out=cS[:, 4:5])
        self.dve("tensor_scalar", C, ["s_cS"], out=cS[:, 5:6], in0=cS[:, 4:5], scalar1=1.0 / DV, scalar2=EPS, op0=ALU.mult, op1=ALU.add)
        self.act(C, ["s_cS"], out=cS[:, 5:6], in_=cS[:, 5:6], func=AF.Sqrt)
        self.dve("reciprocal", C, ["s_cS"], out=cS[:, 5:6], in_=cS[:, 5:6])
        self.dve("tensor_tensor", C, ["s_cS"], out=cS[:, 6:7], in0=cS[:, 5:6], in1=cS[:, 3:4], op=ALU.mult)
        self.dve("scalar_tensor_tensor", ["s_hS", "s_gS"] + C, ["s_yaS"], out=v["yaS"][:], in0=hS[:], scalar=cS[:, 6:7], in1=gS[:],
                 op0=ALU.mult, op1=ALU.mult)
        b = self.bank()
        pt = self.pb[b][:].bitcast(BF16)
        for j in range(4):
            self.tr(["s_yaS", "ident"], [f"pb{b}"], out=pt[:, j * NS:(j + 1) * NS], in_=v["yaS"][:, j * 128:(j + 1) * 128], identity=self.ident[:NS, :NS])
        for j in range(4):
            self.dve("tensor_scalar", [f"pb{b}", "wcols"], ["mergedT"], out=self.mergedT[:, h * 4 + j, T:TW], in0=pt[:, j * NS:(j + 1) * NS],
                     scalar1=self.wcols[:, 3, h * 4 + j:h * 4 + j + 1], scalar2=None, op0=ALU.mult)

    def smp_conv_chunk(self, wt, wn, oc, chunk, raw_ap, col):
        P7 = ["pb7g0", "pb7g1", "pb7u0", "pb7u1"]
        for dc in range(NKC):
            self.mm([wn, "xnT"], P7, out=self.pb[7][:, col:col + NS], lhsT=wt[:, dc, oc * 128:(oc + 1) * 128], rhs=self.xnT[:, dc, T:TW],
                    start=(dc == 0), stop=(dc == NKC - 1))
        cur = self.pb[7][:, col:col + NS]
        self.dve("tensor_copy", P7, ["s_raw"], out=raw_ap, in_=cur)
        hist = self.histT[:, chunk, :].rearrange("p (t j) -> p t j", j=3)
        w = self.convwc
        acc = self.av(self.OFF_MISC + 3264, [128, NS], F32)
        self.dve("tensor_scalar", ["histT", "convwc"], ["s_acc"], out=acc[:], in0=hist[:, :, 0], scalar1=w[:, chunk, 0:1], scalar2=None, op0=ALU.mult)
        for j in (1, 2):
            self.dve("scalar_tensor_tensor", ["histT", "convwc", "s_acc"], ["s_acc"], out=acc[:], in0=hist[:, :, j], scalar=w[:, chunk, j:j + 1],
                     in1=acc[:], op0=ALU.mult, op1=ALU.add)
        self.dve("scalar_tensor_tensor", ["s_raw", "convwc", "s_acc"], ["s_acc"], out=acc[:], in0=raw_ap, scalar=w[:, chunk, 3:4], in1=acc[:],
                 op0=ALU.mult, op1=ALU.add)
        self.act(["s_acc", "convwc"], ["xsT"], out=self.xsT[:, chunk, :], in_=acc[:], func=AF.Silu, bias=w[:, chunk, 4:5], scale=1.0)

    def smp_raw_out(self, raw, nchunk, dcol0):
        b = self.bank()
        for j in range(nchunk):
            self.tr(["s_raw", "identf"], [f"pb{b}"], out=self.pb[b][:NS, j * 128:(j + 1) * 128], in_=raw[:, j, :], identity=self.identf[:])
        stg = self.av(self.OFF_MISC + 3328, [NS, 512], F32)
        self.act([f"pb{b}"], ["s_rstg"], out=stg[:, 0:nchunk * 128], in_=self.pb[b][:NS, 0:nchunk * 128], func=AF.Copy)
        self.dma("st_rs", ["s_rstg"], [], out=self.o_sconv[:, 2, dcol0:dcol0 + nchunk * 128], in_=stg[:, 0:nchunk * 128])

    def smp_bc(self, g, wt, wn):
        raw = self.av(self.OFF_MISC + 5376, [128, 4, NS], F32)
        for oc in range(2):
            self.smp_conv_chunk(wt, wn, oc, 16 + g + 2 * oc, raw[:, oc, :], 64 + 16 * oc)
        b = self.bank()
        for oc in range(2):
            self.tr(["s_raw", "identf"], [f"pb{b}"], out=self.pb[b][:NS, oc * 128:(oc + 1) * 128], in_=raw[:, oc, :], identity=self.identf[:])
        stg = self.av(self.OFF_MISC + 3328, [NS, 512], F32)
        self.act([f"pb{b}"], ["s_rstg"], out=stg[:, 0:256], in_=self.pb[b][:NS, 0:256], func=AF.Copy)
        for oc in range(2):
            d0 = 2048 + 256 * oc + 128 * g
            self.dma("st_rs", ["s_rstg"], [], out=self.o_sconv[:, 2, d0:d0 + 128], in_=stg[:, oc * 128:(oc + 1) * 128])
        b = self.bank()
        for oc in range(2):
            self.tr(["xsT", "identf"], [f"pb{b}"], out=self.pb[b][:NS, oc * 128:(oc + 1) * 128], in_=self.xsT[:, 16 + g + 2 * oc, :], identity=self.identf[:])
        self.dve("tensor_copy", [f"pb{b}"], ["BCs"], out=self.BCs[:, g, :], in_=self.pb[b][:NS, 0:256])

    def smp_x(self, u, a, wt, wn):
        raw = self.av(self.OFF_MISC + 5376, [128, 4, NS], F32)
        for oc in range(2):
            j = 2 * a + oc
            self.smp_conv_chunk(wt, wn, oc, 4 * u + j, raw[:, j, :], 64 + 16 * oc)
        if a == 1:
            self.smp_raw_out(raw, 4, u * 512)

    def smp_ssd(self, u):
        g = u // 2
        SA = self.sarea
        t2 = SA[:, 1024:1536].rearrange("p (a b) -> p a b", b=SN)
        ysT = SA[:, 1536:1792].rearrange("p (a b) -> p a b", b=NS)
        xdtT = self.av(self.OFF_MISC + 5632, [128, 4, NS], F32)
        bcm = self.av(self.OFF_MISC + 5888, [NS, 256], BF16)
        cs = slice(4 * u, 4 * u + 4)
        self.dve("tensor_tensor", ["xsT", "dtT"], ["s_xdtT"], out=xdtT[:], in0=self.xsT[:, cs, :], in1=self.dtT[:, 0, cs, :], op=ALU.mult)
        self.dve("memset", [], ["s_ysT"], ap=ysT[:, cs, :], constant=0.0)
        for j in range(NS):
            Sb = SA[:, (j % 2) * 512:(j % 2 + 1) * 512].rearrange("p (a b) -> p a b", b=SN)
            sn_ = f"s_S{j % 2}"
            self.dma(f"ld_S{j % 2}", [], [sn_], out=Sb, in_=self.st_S[j, u * 512:(u + 1) * 512, :].rearrange("(rc p) n -> p rc n", p=128))
            self.dve("tensor_scalar", ["BCs", "identf"], ["s_bcm"], out=bcm[:], in0=self.BCs[:, g, :], scalar1=self.identf[:NS, j:j + 1], scalar2=None,
                     op0=ALU.mult)
            b = self.bank()
            self.mm(["ones16b", "s_bcm"], [f"pb{b}"], out=self.pb[b][:, 0:256], lhsT=self.ones16b[:], rhs=bcm[:], start=True, stop=True)
            self.dve("tensor_tensor", [f"pb{b}", "s_xdtT"], ["s_t2"], out=t2, in0=self.pb[b][:, 0:SN].unsqueeze(1).to_broadcast([128, 4, SN]),
                     in1=xdtT[:, :, j:j + 1].to_broadcast([128, 4, SN]), op=ALU.mult)
            for rc in range(4):
                self.dve("scalar_tensor_tensor", [sn_, "dtT", "s_t2"], [sn_], out=Sb[:, rc, :], in0=Sb[:, rc, :], scalar=self.dtT[:, 1, 4 * u + rc, j:j + 1],
                         in1=t2[:, rc, :], op0=ALU.mult, op1=ALU.add)
            self.dma(f"st_S{j % 2}", [sn_], [], out=self.o_sS[j, u * 512:(u + 1) * 512, :].rearrange("(rc p) n -> p rc n", p=128), in_=Sb)
            for rc in range(4):
                self.dve("scalar_tensor_tensor", [sn_, f"pb{b}", "s_ysT"], ["s_t2", "s_ysT"], out=t2[:, rc, :], in0=Sb[:, rc, :], scalar=1.0,
                         in1=self.pb[b][:, SN:2 * SN], op0=ALU.mult, op1=ALU.mult, accum_out=ysT[:, 4 * u + rc, j:j + 1])
            yield
        for rc in range(4):
            ch = 4 * u + rc
            self.dve("scalar_tensor_tensor", ["xsT", "wcols", "s_ysT"], ["s_ysT"], out=ysT[:, ch, :], in0=self.xsT[:, ch, :], scalar=self.wcols[:, 6, ch:ch + 1],
                     in1=ysT[:, ch, :], op0=ALU.mult, op1=ALU.add)

    def smp_feat(self, u, a, wt, wn, name, func):
        P7 = ["pb7g0", "pb7g1", "pb7u0", "pb7u1"]
        g = u // 2
        SA = self.sarea
        ysT = SA[:, 1536:1792].rearrange("p (a b) -> p a b", b=NS)
        gt = self.av(self.OFF_MISC + 6400, [128, NS], F32)
        sq = self.av(self.OFF_MISC + 6464, [128, NS], F32)
        for oc in range(2):
            ch = 4 * u + 2 * a + oc
            col = 64 + 16 * oc
            for dc in range(NKC):
                self.mm([wn, "xnT"], P7, out=self.pb[7][:, col:col + NS], lhsT=wt[:, dc, oc * 128:(oc + 1) * 128], rhs=self.xnT[:, dc, T:TW],
                        start=(dc == 0), stop=(dc == NKC - 1))
            self.act(P7, ["s_gt"], out=gt[:], in_=self.pb[7][:, col:col + NS], func=func)
            self.dve("tensor_tensor", ["s_ysT", "s_gt"], ["s_ysT"], out=ysT[:, ch, :], in0=ysT[:, ch, :], in1=gt[:], op=ALU.mult)
            if name == "zS":
                self.dve("tensor_tensor", ["s_ysT"], ["s_sq"], out=sq[:], in0=ysT[:, ch, :], in1=ysT[:, ch, :], op=ALU.mult)
                b = self.bank()
                self.mm(["s_sq", "onesf"], [f"pb{b}"], out=self.pb[b][:NS, 0:2], lhsT=sq[:], rhs=self.onesf[:, 0:2], start=True, stop=True)
                self.dve("tensor_tensor", [f"pb{b}", "ssqS"], ["ssqS"], out=self.ssqS[:, g:g + 1], in0=self.ssqS[:, g:g + 1], in1=self.pb[b][:NS, 0:1], op=ALU.add)
        if name == "gbS" and a == 1 and u % 2 == 1:
            r = self.ssqS[:, 4 + g:5 + g]
            self.dve("tensor_scalar", ["ssqS"], ["ssqS"], out=r, in0=self.ssqS[:, g:g + 1], scalar1=1.0 / 1024, scalar2=EPS, op0=ALU.mult, op1=ALU.add)
            self.act(["ssqS"], ["ssqS"], out=r, in_=r, func=AF.Sqrt)
            self.dve("reciprocal", ["ssqS"], ["ssqS"], out=r, in_=r)
            dg = self.av(self.OFF_MISC + 6528, [NS, NS], F32)
            self.dve("tensor_scalar", ["identf", "ssqS"], ["s_dg"], out=dg[:], in0=self.identf[:NS, :NS], scalar1=r, scalar2=None, op0=ALU.mult)
            b = self.bank()
            self.mm(["onesf", "s_dg"], [f"pb{b}"], out=self.pb[b][:, 0:NS], lhsT=self.onesf[:NS, :], rhs=dg[:], start=True, stop=True)
            for ch in range(8 * g, 8 * g + 8):
                self.dve("tensor_tensor", ["s_ysT", f"pb{b}"], ["s_gt"], out=gt[:], in0=ysT[:, ch, :], in1=self.pb[b][:, 0:NS], op=ALU.mult)
                self.dve("scalar_tensor_tensor", ["s_gt", "wcols", "mergedT"], ["mergedT"], out=self.mergedT[:, ch, T:TW], in0=gt[:],
                         scalar=self.wcols[:, 4, ch:ch + 1], in1=self.mergedT[:, ch, T:TW], op0=ALU.mult, op1=ALU.add)


class Builder(MixMixin, Builder):
    pass


def _tile_w(w, cols=256):
    Kd, N = w.shape
    nb = N // cols
    return np.ascontiguousarray(w.reshape(Kd // 128, 128, nb, cols).transpose(2, 1, 0, 3))


def _tile_wd(w):
    out = np.zeros((24, 128, 8, 512), np.float32)
    for hf in range(2):
        f0 = 0 if hf == 0 else 24
        nfc = 24 if hf == 0 else 20
        for db in range(4):
            for kb in range(3):
                nk = min(8, nfc - kb * 8)
                fc0 = f0 + kb * 8
                blk = w[fc0 * 128:(fc0 + nk) * 128, db * 512:(db + 1) * 512].reshape(nk, 128, 512).transpose(1, 0, 2)
                out[hf * 12 + db * 3 + kb, :, :nk, :] = blk
    return out


def _tile_win(w):
    out = np.zeros((len(WIN_PLAN), 128, 16, 256), np.float32)
    for i, (tag, cols) in enumerate(WIN_PLAN):
        ok = cols >= 0
        blk = np.zeros((2048, 256), np.float32)
        blk[:, ok] = w[:, cols[ok]]
        out[i] = blk.reshape(16, 128, 256).transpose(1, 0, 2)
    return out


def _prep_common(inp):
    m = {}
    vec = np.zeros((8, D), np.float32)
    vec[0] = inp["ffn1_norm"][0]
    vec[1] = inp["mix_norm"][0]
    vec[2] = inp["ffn2_norm"][0]
    vec[3] = inp["ml_head_norm"][0]
    vec[4] = inp["ssm_norm"][0]
    vec[5] = inp["final_norm"]
    vec[6] = np.repeat(inp["ssm_D"][0], HP)
    m["vecs"] = vec
    sm = np.zeros((8, 32), np.float32)
    sm[0, 0:4] = inp["ml_i_bias"][0]
    sm[1, 0:4] = inp["ml_f_bias"][0]
    sm[2] = inp["ssm_dt_bias"][0]
    sm[3] = inp["ssm_A_log"][0]
    sm[4] = inp["ssm_D"][0]
    m["small"] = sm
    m["convw"] = np.concatenate([inp["ssm_conv_w"][0], inp["ssm_conv_b"]], 0).astype(np.float32)
    for i, pre in ((1, "ffn1"), (2, "ffn2")):
        m[f"wg{i}"] = _tile_w(inp[pre + "_w_gate"][0])
        m[f"wu{i}"] = _tile_w(inp[pre + "_w_up"][0])
        m[f"wd{i}"] = _tile_wd(inp[pre + "_w_down"][0])
    m["wo"] = _tile_w(inp["w_out"][0])
    m["win"] = _tile_win(inp["w_in"][0])
    return m


FULL = [False, False, True, True]


def kernel(**inp):
    inp = {k: np.asarray(v) for k, v in inp.items()}
    NT = len(FULL)
    npre = sum(1 for f in FULL if not f)
    nfull = NT - npre
    B = Builder({"NT": NT, "full": FULL})
    B.build()
    common = _prep_common(inp)
    xpr = inp["x_prompt"]
    xsm = inp["x_sample"][:, 0, :]
    in_maps = []
    for c in range(8):
        m = dict(common)
        b, half = c // 2, c % 2
        sl = slice(c * NS, (c + 1) * NS)
        own = xpr[b, half * nfull * T:(half + 1) * nfull * T]
        pre = xpr[b, 0:npre * T]
        m["xp"] = np.ascontiguousarray(np.concatenate([pre, own], 0))
        sm = common["small"].copy()
        sm[5, 0] = 1.0 if half == 1 else 0.0
        sm[5, 1] = 0.0 if half == 1 else NEG
        m["small"] = sm
        m["xs"] = np.ascontiguousarray(xsm[sl])
        m["st_conv"] = np.ascontiguousarray(inp["state_conv"][0, sl].reshape(NS * 3, XBC))
        m["st_C"] = np.ascontiguousarray(inp["state_mlstm_C"][0, sl])
        m["st_n"] = np.ascontiguousarray(inp["state_mlstm_n"][0, sl].reshape(NS, NH * DK))
        m["st_m"] = np.ascontiguousarray(inp["state_mlstm_m"][0, sl])
        m["st_S"] = np.ascontiguousarray(inp["state_ssm"][0, sl].reshape(NS, SH * HP, SN))
        in_maps.append({k: v for k, v in m.items() if k in B.dram})
    res = run_bass_kernel_spmd(B.nc, in_maps, core_ids=list(range(8))).results
    y_prompt = np.stack([np.concatenate([res[2 * b]["yp"], res[2 * b + 1]["yp"]], 0) for b in range(4)], 0)
    y_sample = np.concatenate([res[c]["ys"] for c in range(8)], 0)[:, None, :]
    last = [2 * b + 1 for b in range(4)]
    p_conv = np.stack([res[c]["o_pconv"] for c in last], 0)[None]
    p_C = np.stack([res[c]["o_pC"] for c in last], 0)[None]
    p_n = np.stack([res[c]["o_pn"].reshape(NH, DK) for c in last], 0)[None]
    p_m = np.stack([res[c]["o_pm"] for c in last], 0)[None]
    p_ssm = np.stack([res[c]["o_pS"].reshape(SH, HP, SN) for c in last], 0)[None]
    s_conv = np.concatenate([res[c]["o_sconv"] for c in range(8)], 0)[None]
    s_C = np.concatenate([res[c]["o_sC"] for c in range(8)], 0)[None]
    s_n = np.concatenate([res[c]["o_sn"].reshape(NS, NH, DK) for c in range(8)], 0)[None]
    s_m = np.concatenate([res[c]["o_sm"] for c in range(8)], 0)[None]
    s_ssm = np.concatenate([res[c]["o_sS"].reshape(NS, SH, HP, SN) for c in range(8)], 0)[None]
    outs = (y_prompt, y_sample, p_conv, p_C, p_n, p_m, p_ssm, s_conv, s_C, s_n, s_m, s_ssm)
    return tuple(np.ascontiguousarray(o, dtype=np.float32) for o in outs)
```
